# Optimizing a Trainium2 kernel written in Bass

```python
import math
import jax, jax.numpy as jnp
from jax import lax
import numpy as np

D_MODEL = 1024
BATCH = 16
SEQ = 2048
DEPTH = 1

GRID_W = 64
NA_HEADS = 8
NA_HEAD_DIM = 64
NA_WIN_ROWS_MAX = 8
NA_WIN_COLS = 16
DA_HEADS = 4
DA_HEAD_DIM = 64
NA_WIDTH = NA_HEADS * NA_HEAD_DIM
DA_WIDTH = DA_HEADS * 2 * DA_HEAD_DIM
MIX_WIDTH = NA_WIDTH + DA_WIDTH
IN_COLS = 3 * NA_WIDTH + 3 * DA_WIDTH
D_FF = -(-8 * D_MODEL // (3 * 256)) * 256
Q_BLOCK = 128
N_MOD = 6
RMS_EPS = 1e-6

kernel_name = "hybrid_na_diffattn_adaln_block"


def rms_norm(x, g):
    x32 = x.astype(jnp.float32)
    y = x32 * lax.rsqrt(jnp.mean(x32 * x32, axis=-1, keepdims=True) + RMS_EPS)
    return (y * g.astype(jnp.float32)).astype(x.dtype)


def neighbourhood_attention(q, k, v, rpb):
    B, S, H, d = q.shape
    rows = S // GRID_W
    wh = min(NA_WIN_ROWS_MAX, rows)
    ww = NA_WIN_COLS

    def to_grid(t):
        return t.reshape(B, rows, GRID_W, H, d).transpose(0, 3, 1, 2, 4)

    qg = to_grid(q * (d ** -0.5))
    kg = to_grid(k)
    vg = to_grid(v)
    col = jnp.arange(GRID_W)
    col_start = jnp.clip(col - ww // 2, 0, GRID_W - ww)
    in_win = (col[None, :] >= col_start[:, None]) & (col[None, :] < col_start[:, None] + ww)
    dc_idx = jnp.clip(col[None, :] - col[:, None] + ww - 1, 0, 2 * ww - 2)

    def row_step(args):
        r, q_r = args
        row_start = jnp.clip(r - wh // 2, 0, rows - wh)
        k_w = lax.dynamic_slice_in_dim(kg, row_start, wh, axis=2)
        v_w = lax.dynamic_slice_in_dim(vg, row_start, wh, axis=2)
        dr_idx = row_start + jnp.arange(wh) - r + NA_WIN_ROWS_MAX - 1
        bias = rpb[:, dr_idx[None, :, None], dc_idx[:, None, :]].astype(jnp.float32)
        s = jnp.einsum('bhqd,bhikd->bhqik', q_r, k_w).astype(jnp.float32) + bias[None]
        s = jnp.where(in_win[:, None, :], s, -jnp.inf)
        p = jax.nn.softmax(s.reshape(B, H, GRID_W, wh * GRID_W), axis=-1).reshape(s.shape)
        return jnp.einsum('bhqik,bhikd->bhqd', p.astype(v_w.dtype), v_w)

    out = lax.map(row_step, (jnp.arange(rows), qg.transpose(2, 0, 1, 3, 4)))
    return out.transpose(1, 0, 3, 2, 4).reshape(B, S, H * d)


def differential_attention(q, k, v, lam, slopes):
    B, S, H, _, d = q.shape
    nb = S // Q_BLOCK
    qh = (q * (d ** -0.5)).transpose(0, 2, 3, 1, 4)
    kh = k.transpose(0, 2, 3, 1, 4)
    vh = v.transpose(0, 2, 1, 3)
    q_blocks = qh.reshape(B, H, 2, nb, Q_BLOCK, d).transpose(3, 0, 1, 2, 4, 5)
    kpos = jnp.arange(S)

    def block_step(args):
        blk, qb = args
        qpos = blk * Q_BLOCK + jnp.arange(Q_BLOCK)
        dist = jnp.abs(qpos[:, None] - kpos[None, :]).astype(jnp.float32)
        s = jnp.einsum('bhmqd,bhmkd->bhmqk', qb, kh).astype(jnp.float32)
        s = s - slopes[:, None, None, None] * dist
        p = jax.nn.softmax(s, axis=-1)
        a = p[:, :, 0] - lam * p[:, :, 1]
        return jnp.einsum('bhqk,bhke->bhqe', a.astype(vh.dtype), vh)

    out = lax.map(block_step, (jnp.arange(nb), q_blocks))
    return out.transpose(1, 0, 3, 2, 4).reshape(B, S, H, 2 * d)


def setup_inputs(seed: int = 0) -> dict:
    key = jax.random.key(seed)
    ks = jax.random.split(key, 20)
    f32 = jnp.float32
    L, D = DEPTH, D_MODEL

    def nrm(k, shape, scale):
        return jax.random.normal(k, shape, f32) * scale

    return {
        "x": nrm(ks[0], (BATCH, SEQ, D), 1.0),
        "c": nrm(ks[1], (BATCH, D), 1.0),
        "w_ada": nrm(ks[2], (L, D, N_MOD * D), 0.5 * D ** -0.5),
        "b_ada": nrm(ks[3], (L, N_MOD * D), 0.02),
        "g_mix": 1.0 + nrm(ks[4], (L, D), 0.02),
        "w_in": nrm(ks[5], (L, D, IN_COLS), D ** -0.5),
        "rpb": nrm(ks[6], (L, NA_HEADS, 2 * NA_WIN_ROWS_MAX - 1, 2 * NA_WIN_COLS - 1), 0.2),
        "lambda_q1": nrm(ks[7], (L, DA_HEAD_DIM), 0.1),
        "lambda_k1": nrm(ks[8], (L, DA_HEAD_DIM), 0.1),
        "lambda_q2": nrm(ks[9], (L, DA_HEAD_DIM), 0.1),
        "lambda_k2": nrm(ks[10], (L, DA_HEAD_DIM), 0.1),
        "subln_w": 1.0 + nrm(ks[11], (L, 2 * DA_HEAD_DIM), 0.02),
        "w_out": nrm(ks[12], (L, MIX_WIDTH, D), MIX_WIDTH ** -0.5),
        "g_ffn": 1.0 + nrm(ks[13], (L, D), 0.02),
        "w_gate_up": nrm(ks[14], (L, D, 2 * D_FF), D ** -0.5),
        "w_down": nrm(ks[15], (L, D_FF, D), D_FF ** -0.5),
        "g_final": 1.0 + nrm(ks[16], (D,), 0.02),
    }


def reference(x, c, w_ada, b_ada, g_mix, w_in, rpb, lambda_q1, lambda_k1, lambda_q2,
              lambda_k2, subln_w, w_out, g_ffn, w_gate_up, w_down, g_final):
    B, S, D = x.shape
    f32 = jnp.float32
    slopes = 2.0 ** (-8.0 * jnp.arange(1, DA_HEADS + 1, dtype=f32) / DA_HEADS)
    split_pts = [NA_WIDTH, 2 * NA_WIDTH, 3 * NA_WIDTH,
                 3 * NA_WIDTH + DA_WIDTH, 3 * NA_WIDTH + 2 * DA_WIDTH]
    for layer in range(DEPTH):
        mod = jnp.einsum('bd,de->be', jax.nn.silu(c), w_ada[layer]) + b_ada[layer]
        sh_m, sc_m, gt_m, sh_f, sc_f, gt_f = [m[:, None, :] for m in jnp.split(mod, N_MOD, axis=-1)]

        h = rms_norm(x, g_mix[layer]) * (1.0 + sc_m) + sh_m
        proj = jnp.einsum('bsd,de->bse', h, w_in[layer])
        qa, ka, va, qb, kb, vb = jnp.split(proj, split_pts, axis=-1)

        o_a = neighbourhood_attention(
            qa.reshape(B, S, NA_HEADS, NA_HEAD_DIM),
            ka.reshape(B, S, NA_HEADS, NA_HEAD_DIM),
            va.reshape(B, S, NA_HEADS, NA_HEAD_DIM),
            rpb[layer])

        lambda_init = 0.8 - 0.6 * math.exp(-0.3 * layer)
        lam = (jnp.exp(jnp.sum(lambda_q1[layer].astype(f32) * lambda_k1[layer].astype(f32)))
               - jnp.exp(jnp.sum(lambda_q2[layer].astype(f32) * lambda_k2[layer].astype(f32)))
               + lambda_init)
        o_b = differential_attention(
            qb.reshape(B, S, DA_HEADS, 2, DA_HEAD_DIM),
            kb.reshape(B, S, DA_HEADS, 2, DA_HEAD_DIM),
            vb.reshape(B, S, DA_HEADS, 2 * DA_HEAD_DIM),
            lam, slopes)
        o_b = (rms_norm(o_b, subln_w[layer]) * (1.0 - lambda_init)).reshape(B, S, DA_WIDTH)

        mix = jnp.einsum('bse,ed->bsd', jnp.concatenate([o_a, o_b], axis=-1), w_out[layer])
        x = x + gt_m * mix

        h = rms_norm(x, g_ffn[layer]) * (1.0 + sc_f) + sh_f
        gate, up = jnp.split(jnp.einsum('bsd,df->bsf', h, w_gate_up[layer]), 2, axis=-1)
        ffn = jnp.einsum('bsf,fd->bsd', jax.nn.silu(gate) * up, w_down[layer])
        x = x + gt_f * ffn
    return rms_norm(x, g_final)
```

```python
import contextlib
import math
import numpy as np
import concourse.bass as bass
import concourse.mybir as mybir
from concourse.bass_utils import run_bass_kernel_spmd

F32 = mybir.dt.float32
BF16 = mybir.dt.bfloat16
AF = mybir.ActivationFunctionType
ALU = mybir.AluOpType

D = 1024
S = 2048
NT = 16
NB = 2
DFF = 2816
NFC = 22
EPS = 1e-6
LAMBDA_INIT = 0.2
NEG = -30000.0
SAME_ENGINE_SYNC = True


class Prog:
    ENGINES = ("pe", "act", "dve", "pool", "sp")

    def __init__(self, nc, semstack):
        self.nc = nc
        self.semstack = semstack
        self.ops = []
        self.last_writer = {}
        self.readers = {}
        self.bar_deps = set()
        self.last_on_eng = {}
        self.dma_since_bar = []
        self.flushed = 0
        self.sems = {}
        self.counts = {}
        self.known = {e: {} for e in self.ENGINES}
        self.final_ops = []
        self.excl = set()

    def op(self, eng, fn, reads=(), writes=(), dma=None, final=False):
        oid = len(self.ops)
        deps = set(self.bar_deps)
        for k in reads:
            w = self.last_writer.get(k)
            if w is not None:
                deps.add(w)
            if k in self.excl:
                for r in self.readers.get(k, ()):
                    if self.ops[r]["eng"] != eng:
                        deps.add(r)
        for k in writes:
            w = self.last_writer.get(k)
            if w is not None:
                deps.add(w)
            for r in self.readers.get(k, ()):
                deps.add(r)
        for k in reads:
            self.readers.setdefault(k, []).append(oid)
        for k in writes:
            self.last_writer[k] = oid
            self.readers[k] = []
        self.ops.append(dict(eng=eng, fn=fn, deps=deps, dma=dma, needed=final, ev=None))
        self.last_on_eng[eng] = oid
        if dma is not None:
            self.dma_since_bar.append(oid)
        if final:
            self.final_ops.append(oid)
        return oid

    def _skip(self, do, ename):
        return do["dma"] is None and do["eng"] == ename and (ename == "pe" or not SAME_ENGINE_SYNC)

    def _sem(self, sn):
        if sn not in self.sems:
            self.sems[sn] = self.semstack.enter_context(self.nc.semaphore("sm%d" % len(self.sems)))
            self.counts[sn] = 0
        return self.sems[sn]

    def barrier(self, last=False):
        self.bar_deps = set(self.last_on_eng.values()) | set(self.dma_since_bar)
        self.dma_since_bar = []
        self._flush(last)

    def _flush(self, last):
        nc = self.nc
        ops = self.ops
        lo, hi = self.flushed, len(ops)
        self.flushed = hi
        for i in range(lo, hi):
            o = ops[i]
            for d in o["deps"]:
                if d >= lo and not self._skip(ops[d], o["eng"]):
                    ops[d]["needed"] = True
        for d in self.bar_deps:
            if d >= lo:
                ops[d]["needed"] = True
        for e in self.ENGINES:
            self._sem(("eng", e))
        for i in range(lo, hi):
            o = ops[i]
            if not o["needed"]:
                continue
            sn = ("dma", o["dma"]) if o["dma"] is not None else ("eng", o["eng"])
            self._sem(sn)
            self.counts[sn] += 16 if o["dma"] is not None else 1
            o["ev"] = (sn, self.counts[sn])
        by_eng = {e: [] for e in self.ENGINES}
        for i in range(lo, hi):
            by_eng[ops[i]["eng"]].append(i)

        def run_engine(ename, eh):
            known = self.known[ename]
            for i in by_eng[ename]:
                o = ops[i]
                need = {}
                for d in o["deps"]:
                    do = ops[d]
                    if do["ev"] is None or self._skip(do, ename):
                        continue
                    sn, v = do["ev"]
                    if need.get(sn, 0) < v:
                        need[sn] = v
                for sn, v in need.items():
                    if known.get(sn, 0) >= v:
                        continue
                    eh.wait_ge(self.sems[sn], v)
                    known[sn] = v
                ins = o["fn"](eh)
                if o["ev"] is not None:
                    ins.then_inc(self.sems[o["ev"][0]], 16 if o["dma"] is not None else 1)
                o["fn"] = None
            if last and ename == "sp":
                for d in self.final_ops:
                    sn, v = ops[d]["ev"]
                    if known.get(sn, 0) < v:
                        eh.wait_ge(self.sems[sn], v)
                        known[sn] = v

        with nc.Block() as block:
            @block.tensor
            def _(e):
                run_engine("pe", e)

            @block.scalar
            def _(e):
                run_engine("act", e)

            @block.vector
            def _(e):
                run_engine("dve", e)

            @block.gpsimd
            def _(e):
                run_engine("pool", e)

            @block.sync
            def _(e):
                run_engine("sp", e)


def _na_tables():
    rows, W, wh, ww = 32, 64, 8, 16
    pats = []
    pat_key = {}
    J = []
    pid = {}
    kk = np.arange(128)
    kr, ck = kk // 64, kk % 64
    for t in range(16):
        r_q = 2 * t + kr
        cq = ck
        rs = np.clip(r_q - wh // 2, 0, rows - wh)
        cs = np.clip(cq - ww // 2, 0, W - ww)
        jlo = int(rs.min()) // 2
        jhi = int(rs.max() + wh - 1) // 2
        js = list(range(jlo, jhi + 1))
        J.append(js)
        for j in js:
            krow = 2 * j + kr
            valid_r = (krow[:, None] >= rs[None, :]) & (krow[:, None] < rs[None, :] + wh)
            dr = krow[:, None] - r_q[None, :] + 7
            valid_c = (ck[:, None] >= cs[None, :]) & (ck[:, None] < cs[None, :] + ww)
            dc = np.clip(ck[:, None] - cq[None, :] + ww - 1, 0, 2 * ww - 2)
            valid = valid_r & valid_c
            drc = np.where(valid, dr, 0)
            key = (drc.tobytes(), dc.tobytes(), valid.tobytes())
            if key not in pat_key:
                pat_key[key] = len(pats)
                pats.append((drc.astype(np.int64), dc.astype(np.int64), valid))
            pid[(t, j)] = pat_key[key]
    return J, pid, pats


_NA_J, _NA_PID, _NA_PATS = _na_tables()


def _na_blocks():
    blk_pat = []
    off = {}
    ref = [_NA_PID[(2, j)] for j in _NA_J[2]]
    for t in range(2, 14):
        assert [_NA_PID[(t, j)] for j in _NA_J[t]] == ref
    blk_pat += ref
    for t in range(2, 14):
        off[t] = 0
    for t in (0, 1, 14, 15):
        off[t] = len(blk_pat)
        blk_pat += [_NA_PID[(t, j)] for j in _NA_J[t]]
    return blk_pat, off


_NA_BLKPAT, _NA_OFF = _na_blocks()
NPAT = len(_NA_BLKPAT)


def _host_consts():
    slopes = [2.0 ** (-8.0 * (i + 1) / 4) for i in range(4)]
    tok = np.arange(S)
    qaug = np.zeros((4, 4, S), np.float32)
    kaug = np.zeros((4, 2, 4, S), np.float32)
    bdiag = np.zeros((128, 4, 128), np.float32)
    for h, sl in enumerate(slopes):
        qaug[h, 0] = sl * (tok % 256)
        qaug[h, 1] = sl * (tok - tok % 256)
        qaug[h, 2] = 1.0
        qaug[h, 3] = 1.0
        kaug[h, 0, 0] = -1.0
        kaug[h, 0, 1] = -1.0
        kaug[h, 0, 2] = sl * (tok % 128)
        kaug[h, 0, 3] = sl * (tok - tok % 128)
        kaug[h, 1] = -kaug[h, 0]
        i = np.arange(128)
        bdiag[:, h, :] = -sl * np.abs(i[:, None] - i[None, :])
    mask = np.zeros((128, NPAT, 128), np.float32)
    for p, pat in enumerate(_NA_BLKPAT):
        mask[:, p, :] = np.where(_NA_PATS[pat][2], 0.0, NEG)
    return qaug, kaug, bdiag, mask


def build_nc(stop=None):
    nc = bass.Bass("TRN2", target_bir_lowering=False)

    def din(name, shape):
        return nc.dram_tensor(name, list(shape), F32, kind="ExternalInput").ap()

    x_d = din("x", [NB, S, D])
    cT_d = din("cT", [128, 16])
    wada_d = din("w_ada", [D, 6 * D])
    badaT_d = din("b_adaT", [128, 48])
    gmixT_d = din("g_mixT", [128, 8])
    gffnT_d = din("g_ffnT", [128, 8])
    win_d = din("w_in", [D, 3 * D])
    wout_d = din("w_out", [D, D])
    wgu_d = din("w_gate_up", [D, 2 * DFF])
    wdn_d = din("w_down", [DFF, D])
    gfin_d = din("g_final", [D])
    subw_d = din("subln_w", [128])
    lam_d = din("lams", [4, 64])
    nabg_d = din("nab_g", [8, 128, NPAT * 128])
    mask_d = din("na_mask", [128, NPAT * 128])
    qaug_d = din("qaug", [4, 4, S])
    kaug_d = din("kaug", [4, 2, 4, S])
    bdiag_d = din("bdiag", [128, 4 * 128])
    out_d = nc.dram_tensor("out", [NB, S, D], F32, kind="ExternalOutput").ap()

    semstack = contextlib.ExitStack()
    P = Prog(nc, semstack)
    _uid = [0]
    for _k in ["prjA", "prjB", "ptrA", "ptrB", "pmod"]:
        P.excl.add(_k)
    for _i in range(4):
        for _j in range(4):
            P.excl.update([("snA", _i), ("accA", _i), ("sps", _i), ("accb", _i, _j), ("pc", _i, _j), ("pg", _i), ("pu", _i),
                           ("pd", _i), ("ptr", "A", _i), ("ptr", "D", _i)])

    class Scope:
        def __init__(self):
            self.st = contextlib.ExitStack()

        def __enter__(self):
            self.st.__enter__()
            return self

        def __exit__(self, *a):
            return self.st.__exit__(*a)

        def sb(self, name, shape, dt):
            _uid[0] += 1
            return self.st.enter_context(nc.sbuf_tensor("%s_%d" % (name, _uid[0]), list(shape), dt))

        def ps(self, name, shape, dt):
            _uid[0] += 1
            return self.st.enter_context(nc.psum_tensor("%s_%d" % (name, _uid[0]), list(shape), dt))

    def mm(out, lhsT, rhs, start, stop, reads, writes, skip=False):
        if skip:
            P.op("pe", lambda e: e.matmul(out, lhsT=lhsT, rhs=rhs, start=start, stop=stop,
                                          skip_group_check=True), reads, writes)
        else:
            P.op("pe", lambda e: e.matmul(out, lhsT=lhsT, rhs=rhs, start=start, stop=stop), reads, writes)

    def tr(out, in_, ident, reads, writes):
        P.op("pe", lambda e: e.transpose(out=out, in_=in_, identity=ident), reads, writes)

    def act(out, in_, func, reads, writes, scale=None, bias=None, accum=None):
        kw = {}
        if scale is not None:
            kw["scale"] = scale
        if bias is not None:
            kw["bias"] = bias
        if accum is not None:
            kw["accum_out"] = accum
        P.op("act", lambda e: e.activation(out=out, in_=in_, func=func, **kw), reads, writes)

    def ts(eng, out, in0, s1, s2, op0, op1, reads, writes):
        if op1 is None:
            P.op(eng, lambda e: e.tensor_scalar(out=out, in0=in0, scalar1=s1, scalar2=None, op0=op0), reads, writes)
        else:
            P.op(eng, lambda e: e.tensor_scalar(out=out, in0=in0, scalar1=s1, scalar2=s2, op0=op0, op1=op1), reads, writes)

    def tt(eng, out, in0, in1, op, reads, writes):
        P.op(eng, lambda e: e.tensor_tensor(out=out, in0=in0, in1=in1, op=op), reads, writes)

    def stt(eng, out, in0, scalar, in1, op0, op1, reads, writes):
        P.op(eng, lambda e: e.scalar_tensor_tensor(out=out, in0=in0, scalar=scalar, in1=in1, op0=op0, op1=op1), reads, writes)

    def cp(eng, out, in_, reads, writes):
        if eng == "act":
            P.op(eng, lambda e: e.activation(out=out, in_=in_, func=AF.Copy), reads, writes)
        else:
            P.op(eng, lambda e: e.tensor_copy(out=out, in_=in_), reads, writes)

    def recip(out, in_, reads, writes):
        P.op("dve", lambda e: e.reciprocal(out=out, in_=in_), reads, writes)

    def memset(eng, ap, val, writes):
        P.op(eng, lambda e: e.memset(ap, val), (), writes)

    def dma(eng, out, in_, reads, writes, key, final=False):
        return P.op(eng, lambda e: e.dma_start(out=out, in_=in_), reads, writes, dma=key, final=final)

    with semstack, Scope() as G:
        identf = G.sb("identf", [128, 128], F32)
        ident = G.sb("ident", [128, 128], BF16)
        onesf = G.sb("onesf", [128, 128], F32)
        modT = G.sb("modT", [128, 48, 2], F32)
        aM = G.sb("aM", [128, 8, 2], F32)
        aF = G.sb("aF", [128, 8, 2], F32)
        gfbc = G.sb("gfbc", [128, D], F32)
        subw = G.sb("subw", [128, 128], F32)
        neglam = G.sb("neglam", [128, 1], F32)
        bdg = G.sb("bdg", [128, 4, 128], BF16)
        oT = G.sb("oT", [128, 8, S], BF16)

        def build_bc(L, colfn, rkeys, tag):
            bc = L.sb("bc" + tag, [128, D], F32)
            dg = [L.sb("dg%s%d" % (tag, i), [128, 128], F32) for i in range(2)]
            pb = [L.ps("pb%s%d" % (tag, i), [128, 512], F32) for i in range(2)]
            for half in range(2):
                for kk in range(4):
                    k = half * 4 + kk
                    sl = k % 2
                    ts("dve", dg[sl][:], identf[:], colfn(k), None, ALU.mult, None, ["identf"] + rkeys, [("dg", tag, sl)])
                    mm(pb[half][:, kk * 128:(kk + 1) * 128], onesf[:], dg[sl][:], True, True,
                       ["onesf", ("dg", tag, sl)], [("pb", tag, half)])
                cp("act", bc[:, half * 512:(half + 1) * 512], pb[half][:], [("pb", tag, half)], [("bc", tag, half)])
            return bc, [("bc", tag, 0), ("bc", tag, 1)]

        with Scope() as L:
            cT = L.sb("cT", [128, 16], F32)
            cS = L.sb("cS", [128, 16], BF16)
            badaT = L.sb("badaT", [128, 48], F32)
            gmT = L.sb("gmT", [128, 8], F32)
            gfT = L.sb("gfT", [128, 8], F32)
            lamv = L.sb("lamv", [128, 4, 64], F32)
            lamp = L.sb("lamp", [128, 2, 64], F32)
            lams = L.sb("lams_s", [128, 2], F32)
            wad = [L.sb("wad%d" % i, [128, 8, 1024], BF16) for i in range(2)]
            pmod = L.ps("pmod", [128, 512], F32)

            memset("pool", identf[:], 0.0, ["identf"])
            P.op("pool", lambda e: e.affine_select(out=identf[:], in_=identf[:], pattern=[[-1, 128]],
                                                   compare_op=ALU.not_equal, fill=1.0, base=0,
                                                   channel_multiplier=1), ["identf"], ["identf"])
            cp("dve", ident[:], identf[:], ["identf"], ["ident"])
            memset("pool", onesf[:], 1.0, ["onesf"])
            dma("sp", cT[:], cT_d, (), ["cT"], "s_cT")
            dma("sp", badaT[:], badaT_d, (), ["badaT"], "s_bada")
            dma("sp", gmT[:], gmixT_d, (), ["gmT"], "s_gm")
            dma("sp", gfT[:], gffnT_d, (), ["gfT"], "s_gf")
            dma("sp", gfbc[:], gfin_d.partition_broadcast(128), (), ["gfbc"], "s_gfbc")
            dma("sp", subw[:], subw_d.partition_broadcast(128), (), ["subw"], "s_subw")
            for i in range(4):
                dma("sp", lamv[:, i, :], lam_d[i].partition_broadcast(128), (), [("lamv", i)], "s_lam%d" % i)
            dma("pool", bdg[:].rearrange("p h q -> p (h q)"), bdiag_d, (), ["bdg"], "s_bdg")
            ts("dve", subw[:], subw[:], 1.0 - LAMBDA_INIT, None, ALU.mult, None, ["subw"], ["subw"])
            tt("dve", lamp[:, 0, :], lamv[:, 0, :], lamv[:, 1, :], ALU.mult, [("lamv", 0), ("lamv", 1)], [("lamp", 0)])
            tt("dve", lamp[:, 1, :], lamv[:, 2, :], lamv[:, 3, :], ALU.mult, [("lamv", 2), ("lamv", 3)], [("lamp", 1)])
            P.op("dve", lambda e: e.reduce_sum(out=lams[:, 0:1], in_=lamp[:, 0, :], axis=mybir.AxisListType.X),
                 [("lamp", 0)], [("lams", 0)])
            P.op("dve", lambda e: e.reduce_sum(out=lams[:, 1:2], in_=lamp[:, 1, :], axis=mybir.AxisListType.X),
                 [("lamp", 1)], [("lams", 1)])
            act(lams[:], lams[:], AF.Exp, [("lams", 0), ("lams", 1)], ["lamse"])
            tt("dve", neglam[:], lams[:, 1:2], lams[:, 0:1], ALU.subtract, ["lamse"], ["neglam"])
            ts("dve", neglam[:], neglam[:], -LAMBDA_INIT, None, ALU.add, None, ["neglam"], ["neglam"])
            act(cS[:], cT[:], AF.Silu, ["cT"], ["cS"])
            wv = wada_d.rearrange("(k p) n -> p k n", p=128)
            for pc in range(6):
                sl = pc % 2
                dma("pool", wad[sl][:], wv[:, :, pc * 1024:(pc + 1) * 1024], (), [("wad", sl)], "s_wad%d" % sl)
                for jj in range(8):
                    j = pc * 8 + jj
                    for k in range(8):
                        mm(pmod[:, 2 * j:2 * j + 2], wad[sl][:, k, jj * 128:(jj + 1) * 128],
                           cS[:, 2 * k:2 * k + 2], k == 0, k == 7, [("wad", sl), "cS"], ["pmod"])
            pm3 = pmod[:, 0:96].rearrange("p (j b) -> p j b", b=2)
            for b in range(2):
                tt("dve", modT[:, :, b], pm3[:, :, b], badaT[:], ALU.add, ["pmod", "badaT"], [("modT", b)])
            for b in range(2):
                stt("dve", aM[:, :, b], modT[:, 8:16, b], 1.0, gmT[:], ALU.add, ALU.mult, [("modT", b), "gmT"], [("aM", b)])
                stt("dve", aF[:, :, b], modT[:, 32:40, b], 1.0, gfT[:], ALU.add, ALU.mult, [("modT", b), "gfT"], [("aF", b)])
            P.barrier()
            if stop == 0:
                return nc

        for b in range(NB):
            with Scope() as M:
                hT = M.sb("hT", [128, 8, S], BF16)

                def norm_phase(L, src_tile_fn, dstT, a_sc, sh_sc, tiles, tok0, tagp):
                    ss = L.sb("ss" + tagp, [128, 16], F32)
                    rs = L.sb("rs" + tagp, [128, 16], F32)
                    junk = L.sb("junk" + tagp, [128, D], BF16)
                    xn = [L.sb("xn%s%d" % (tagp, i), [128, D], BF16) for i in range(2)]
                    xf = [L.sb("xf%s%d" % (tagp, i), [128, D], F32) for i in range(2)]
                    ptr = [L.ps("ptr%s%d" % (tagp, i), [128, 8, 128], BF16) for i in range(2)]
                    memset("dve", ss[:], 0.0, [("ss", tagp)])
                    import os
                    DBG = int(os.environ.get("DBGA", "9"))
                    for ti, t in enumerate(tiles):
                        src, rk = src_tile_fn(t)
                        sl = ti % 2
                        act(junk[:], src, AF.Square, rk + [("ss", tagp)], [("junk", tagp), ("ssd", tagp, ti)],
                            accum=ss[:, ti:ti + 1])
                        ts("dve", rs[:, ti:ti + 1], ss[:, ti:ti + 1], 1.0 / D, EPS, ALU.mult, ALU.add,
                           [("ssd", tagp, ti)], [("rs", tagp, ti)])
                        act(rs[:, ti:ti + 1], rs[:, ti:ti + 1], AF.Sqrt, [("rs", tagp, ti)], [("rs", tagp, ti)])
                        recip(rs[:, ti:ti + 1], rs[:, ti:ti + 1], [("rs", tagp, ti)], [("rs", tagp, ti)])
                        stt("dve", xf[sl][:], src, rs[:, ti:ti + 1], a_sc[0][:], ALU.mult, ALU.mult,
                            rk + [("rs", tagp, ti)] + a_sc[1], [("xf", tagp, sl)])
                        tt("pool" if ti % 2 == 0 else "dve", xn[sl][:], xf[sl][:], sh_sc[0][:], ALU.add, [("xf", tagp, sl)] + sh_sc[1], [("xn", tagp, sl)])
                        for k in range(8):
                            tr(ptr[sl][:, k, :], xn[sl][:, k * 128:(k + 1) * 128], ident[:],
                               [("xn", tagp, sl), "ident"], [("ptr", tagp, sl)])
                        dst = dstT[:, :, tok0 + ti * 128: tok0 + (ti + 1) * 128]
                        wk = [(tagp, "T", k, tok0 // 128 + ti) for k in range(8)]
                        cp("act" if ti % 2 == 0 else "dve", dst, ptr[sl][:], [("ptr", tagp, sl)], wk)

                with Scope() as L:
                    xt = [L.sb("xtA%d" % i, [128, D], F32) for i in range(2)]

                    def srcA(t):
                        sl = t % 2
                        dma("sp", xt[sl][:], x_d[b, t * 128:(t + 1) * 128, :], (), [("xtA", sl)], "s_xtA%d" % sl)
                        return xt[sl][:], [("xtA", sl)]

                    abc = build_bc(L, lambda k: aM[:, k, b:b + 1], [("aM", b)], "A")
                    sbc = build_bc(L, lambda k: modT[:, 0 + k, b:b + 1], [("modT", b)], "As")
                    norm_phase(L, srcA, hT, abc, sbc, list(range(NT)), 0, "A")
                    P.barrier()
                    if stop == 1:
                        return nc
                hkeys = lambda k, t0, n: [("A", "T", k, t) for t in range(t0, t0 + n)]

                with Scope() as L:
                    nab = L.sb("nab", [128, 8, NPAT, 128], BF16)
                    with Scope() as L2:
                        msk = L2.sb("msk", [128, NPAT * 128], F32)
                        stg = [L2.sb("stg%d" % i, [128, NPAT * 128], F32) for i in range(2)]
                        dma("sp", msk[:], mask_d, (), ["msk"], "s_msk")
                        for h in range(8):
                            sl = h % 2
                            dma("sp", stg[sl][:], nabg_d[h], (), [("stg", sl)], "s_stg%d" % sl)
                            tt("pool", nab[:, h, :, :].rearrange("p a q -> p (a q)"), stg[sl][:], msk[:], ALU.add,
                               [("stg", sl), "msk"], [("nab", h)])
                            act(nab[:, h, :, :].rearrange("p a q -> p (a q)"), nab[:, h, :, :].rearrange("p a q -> p (a q)"),
                                AF.Exp, [("nab", h)], [("nab", h)])
                        P.barrier()
                    wg = [L.sb("wgA%d" % i, [128, 8, 3, 128], BF16) for i in range(2)]
                    Qz = [L.sb("Qz%d" % i, [128, S], BF16) for i in range(2)]
                    KaT = L.sb("KaT", [128, S], BF16)
                    Va = L.sb("Va", [128, NT, 2, 65], BF16)
                    Ea = [L.sb("Ea%d" % i, [128, 640], BF16) for i in range(3)]
                    rc = L.sb("rcA", [128, 2], F32)
                    oa = [L.sb("oa%d" % i, [128, 128], BF16) for i in range(2)]
                    snA = [L.ps("snA%d" % i, [128, 1024], F32) for i in range(2)]
                    accA = [L.ps("accA%d" % i, [128, 512], F32) for i in range(2)]
                    prj = L.ps("prjA", [128, 512], F32)
                    ptr = L.ps("ptrA", [128, 8, 128], BF16)
                    memset("pool", Va[:, :, :, 64:65], 1.0, ["Va1"])
                    memset("pool", Qz[0][64:128, :], 0.0, [("Qzpad", 0)])
                    memset("pool", Qz[1][0:64, :], 0.0, [("Qzpad", 1)])
                    wiv = win_d.rearrange("(k p) n -> p k n", p=128)
                    ecnt = 0
                    for g in range(4):
                        sl = g % 2
                        for tsel in range(3):
                            c0 = tsel * 512 + g * 128
                            dma("pool", wg[sl][:, :, tsel, :], wiv[:, :, c0:c0 + 128], (), [("wgA", sl, tsel)],
                                "s_wgA%d_%d" % (sl, tsel))
                        for tsel, dst, scl in ((0, None, 0.125), (1, KaT, 1.0)):
                            for c in range(4):
                                for k in range(8):
                                    mm(prj[:], wg[sl][:, k, tsel, :], hT[:, k, c * 512:(c + 1) * 512], k == 0, k == 7,
                                       [("wgA", sl, tsel)] + hkeys(k, c * 4, 4), ["prjA"])
                                if tsel == 0:
                                    act(Qz[0][0:64, c * 512:(c + 1) * 512], prj[0:64, :], AF.Copy, ["prjA"], [("Qz", 0, c)], scale=scl)
                                    act(Qz[1][64:128, c * 512:(c + 1) * 512], prj[64:128, :], AF.Copy, ["prjA"], [("Qz", 1, c)], scale=scl)
                                else:
                                    cp("dve", dst[:, c * 512:(c + 1) * 512], prj[:], ["prjA"], [("KaT", c)])
                        for c in range(4):
                            for tq in range(4):
                                t = c * 4 + tq
                                for k in range(8):
                                    mm(prj[:, tq * 128:(tq + 1) * 128], hT[:, k, t * 128:(t + 1) * 128], wg[sl][:, k, 2, :],
                                       k == 0, k == 7, [("wgA", sl, 2)] + hkeys(k, t, 1), ["prjA"])
                            cp("dve", Va[:, c * 4:(c + 1) * 4, :, 0:64],
                               prj[:].rearrange("p (t h e) -> p t h e", t=4, h=2), ["prjA"], [("Va", c)])
                        for t in range(NT):
                            js = _NA_J[t]
                            nb_ = len(js)
                            asl = t % 2
                            for hh in range(2):
                                head = 2 * g + hh
                                ssl = ecnt % 2
                                esl = ecnt % 3
                                ecnt += 1
                                for i, j in enumerate(js):
                                    mm(snA[ssl][:, i * 128:(i + 1) * 128], KaT[:, j * 128:(j + 1) * 128],
                                       Qz[hh][:, t * 128:(t + 1) * 128], True, True,
                                       [("KaT", j // 4), ("Qz", hh, t // 4), ("Qzpad", hh)], [("snA", ssl)])
                                act(Ea[esl][:, 0:nb_ * 128], snA[ssl][:, 0:nb_ * 128], AF.Exp, [("snA", ssl)], [("Ea", esl)])
                                o0 = _NA_OFF[t]
                                tt("dve", Ea[esl][:, 0:nb_ * 128], Ea[esl][:, 0:nb_ * 128],
                                   nab[:, head, o0:o0 + nb_, :].rearrange("p a q -> p (a q)"), ALU.mult,
                                   [("Ea", esl), ("nab", head)], [("Ea", esl)])
                                for i, j in enumerate(js):
                                    mm(accA[asl][:, hh * 65:(hh + 1) * 65], Ea[esl][:, i * 128:(i + 1) * 128],
                                       Va[:, j, hh, :], i == 0, i == nb_ - 1,
                                       [("Ea", esl), ("Va", j // 4), "Va1"], [("accA", asl)])
                            a3 = accA[asl][:, 0:130].rearrange("p (h e) -> p h e", h=2)
                            recip(rc[:], a3[:, :, 64], [("accA", asl)], ["rcA"])
                            for hh in range(2):
                                ts("dve", oa[asl][:, hh * 64:(hh + 1) * 64], a3[:, hh, 0:64], rc[:, hh:hh + 1], None,
                                   ALU.mult, None, [("accA", asl), "rcA"], [("oa", asl, hh)])
                            tr(ptr[:, 0, :], oa[asl][:], ident[:], [("oa", asl, 0), ("oa", asl, 1), "ident"],
                               ["ptrA"])
                            cp("act", oT[:, g, t * 128:(t + 1) * 128], ptr[:, 0, :], ["ptrA"], [("oT", g, t)])
                    P.barrier()
                    if stop == 2:
                        return nc

                with Scope() as L:
                    wg = [L.sb("wgB%d" % i, [128, 8, 3, 128], BF16) for i in range(2)]
                    QT = [L.sb("QT%d" % m, [128, S], BF16) for m in range(2)]
                    KT = [[L.sb("KT%d_%d" % (m, v), [128, S], BF16) for v in range(2)] for m in range(2)]
                    Vb = L.sb("Vb", [128, NT, 129], BF16)
                    Eb = [L.sb("Eb%d" % i, [128, 512], BF16) for i in range(3)]
                    om = [L.sb("om%d" % m, [128, 4, 128], F32) for m in range(2)]
                    df = L.sb("df", [128, 4, 128], F32)
                    junk = L.sb("junkB", [128, 128], BF16)
                    ssb = L.sb("ssb", [128, 4], F32)
                    rsb = L.sb("rsb", [128, 4], F32)
                    rcb = L.sb("rcb", [128, 4], F32)
                    ob = [L.sb("ob%d" % i, [128, 128], BF16) for i in range(4)]
                    sps = [L.ps("sps%d" % i, [128, 512], F32) for i in range(2)]
                    accb = [[L.ps("accb%d_%d" % (s_, i), [128, 512], F32) for i in range(2)] for s_ in range(2)]
                    prj = L.ps("prjB", [128, 512], F32)
                    ptr = L.ps("ptrB", [128, 8, 128], BF16)
                    memset("pool", Vb[:, :, 128:129], 1.0, ["Vb1"])
                    for m_ in range(2):
                        memset("pool", QT[m_][64:128, :], 0.0, [("QTa", m_)])
                        for v_ in range(2):
                            memset("pool", KT[m_][v_][64:128, :], 0.0, [("KTa", m_, v_)])
                    wiv = win_d.rearrange("(k p) n -> p k n", p=128)
                    ecnt = 0
                    acnt = 0
                    tcnt = 0
                    for h in range(4):
                        sl = h % 2
                        for tsel in range(3):
                            c0 = 1536 + tsel * 512 + h * 128
                            dma("pool", wg[sl][:, :, tsel, :], wiv[:, :, c0:c0 + 128], (), [("wgB", sl, tsel)],
                                "s_wgB%d_%d" % (sl, tsel))
                        import os
                        DBGQ = os.environ.get("DBGQ", "").split(",")
                        for m in range(2):
                            if "noaug" in DBGQ:
                                continue
                            dma("pool", QT[m][64:68, :], qaug_d[h], (), [("QTa", m)], "s_qa%d" % m)
                            for v in range(2):
                                dma("pool", KT[m][v][64:68, :], kaug_d[h, v], (), [("KTa", m, v)], "s_ka%d_%d" % (m, v))
                        for c in range(4):
                            if "noq" in DBGQ:
                                continue
                            for k in range(8):
                                mm(prj[:], wg[sl][:, k, 0, :], hT[:, k, c * 512:(c + 1) * 512], k == 0, k == 7,
                                   [("wgB", sl, 0)] + hkeys(k, c * 4, 4), ["prjB"])
                            act(QT[0][0:64, c * 512:(c + 1) * 512], prj[0:64, :], AF.Copy, ["prjB"], [("QT", 0, c)], scale=0.125)
                            act(QT[1][0:64, c * 512:(c + 1) * 512], prj[64:128, :], AF.Copy, ["prjB"], [("QT", 1, c)], scale=0.125)
                        for c in range(4):
                            if "nok" in DBGQ:
                                continue
                            for k in range(8):
                                mm(prj[:], wg[sl][:, k, 1, :], hT[:, k, c * 512:(c + 1) * 512], k == 0, k == 7,
                                   [("wgB", sl, 1)] + hkeys(k, c * 4, 4), ["prjB"])
                            cp("dve", KT[0][0][0:64, c * 512:(c + 1) * 512], prj[0:64, :], ["prjB"], [("KT", 0, 0, c)])
                            cp("dve", KT[1][0][0:64, c * 512:(c + 1) * 512], prj[64:128, :], ["prjB"], [("KT", 1, 0, c)])
                            for m_ in range(2):
                                cp("pool", KT[m_][1][0:64, c * 512:(c + 1) * 512], KT[m_][0][0:64, c * 512:(c + 1) * 512],
                                   [("KT", m_, 0, c)], [("KT", m_, 1, c)])
                        for c in range(4):
                            if "nov" in DBGQ:
                                continue
                            for tq in range(4):
                                t = c * 4 + tq
                                for k in range(8):
                                    mm(prj[:, tq * 128:(tq + 1) * 128], hT[:, k, t * 128:(t + 1) * 128], wg[sl][:, k, 2, :],
                                       k == 0, k == 7, [("wgB", sl, 2)] + hkeys(k, t, 1), ["prjB"])
                            cp("dve", Vb[:, c * 4:(c + 1) * 4, 0:128], prj[:].rearrange("p (t e) -> p t e", t=4),
                               ["prjB"], [("Vb", c)])
                        import os
                        DBGB = int(os.environ.get("DBGB", "9"))
                        for qc in range(4):
                            if DBGB < 2:
                                continue
                            for m in range(2):
                                aset = accb[acnt % 2]
                                akey = acnt % 2
                                acnt += 1

                                def qk(kt, ssl):
                                    o = sps[ssl]
                                    wk = [("sps", ssl)]

                                    def rng(v, c0, c1, aug=True):
                                        kk = 128 if aug else 64
                                        rd = [("KT", m, v, kt // 4), ("QT", m, qc)]
                                        if aug:
                                            rd += [("KTa", m, v), ("QTa", m)]
                                        return (o[:, c0:c1], KT[m][v][0:kk, kt * 128:(kt + 1) * 128],
                                                QT[m][0:kk, qc * 512 + c0: qc * 512 + c1], rd)
                                    if kt < 4 * qc:
                                        o_, l_, r_, rd = rng(0, 0, 512)
                                        mm(o_, l_, r_, True, True, rd, wk)
                                    elif kt >= 4 * qc + 4:
                                        o_, l_, r_, rd = rng(1, 0, 512)
                                        mm(o_, l_, r_, True, True, rd, wk)
                                    else:
                                        jd = kt - 4 * qc
                                        if jd > 0:
                                            o_, l_, r_, rd = rng(1, 0, jd * 128)
                                            mm(o_, l_, r_, True, True, rd, wk)
                                        o_, l_, r_, rd = rng(0, jd * 128, (jd + 1) * 128, aug=False)
                                        mm(o_, l_, r_, True, False, rd, wk)
                                        mm(o_, ident[:], bdg[:, h, :], False, True, ["ident", "bdg"], wk)
                                        if jd < 3:
                                            o_, l_, r_, rd = rng(0, (jd + 1) * 128, 512)
                                            mm(o_, l_, r_, True, True, rd, wk)

                                qk(0, ecnt % 2)
                                for kt in range(16):
                                    ssl = ecnt % 2
                                    esl = ecnt % 3
                                    ecnt += 1
                                    if kt + 1 < 16:
                                        qk(kt + 1, ecnt % 2)
                                    act(Eb[esl][:], sps[ssl][:], AF.Exp, [("sps", ssl)], [("Eb", esl)])
                                    for j in range(4):
                                        if DBGB < 3:
                                            continue
                                        mm(aset[j // 2][:, (j % 2) * 129:(j % 2) * 129 + 129], Eb[esl][:, j * 128:(j + 1) * 128],
                                           Vb[:, kt, :], kt == 0 and j % 2 == 0, kt == 15,
                                           [("Eb", esl), ("Vb", kt // 4), "Vb1"], [("accb", akey, j // 2)], skip=True)
                                for j in range(4):
                                    if DBGB < 4:
                                        continue
                                    a_ = aset[j // 2]
                                    c0 = (j % 2) * 129
                                    recip(rcb[:, j:j + 1], a_[:, c0 + 128:c0 + 129], [("accb", akey, j // 2)], [("rcb", j)])
                                    ts("dve", om[m][:, j, :], a_[:, c0:c0 + 128], rcb[:, j:j + 1], None, ALU.mult, None,
                                       [("accb", akey, j // 2), ("rcb", j)], [("om", m, j)])
                            if DBGB < 5:
                                continue
                            stt("dve", df[:], om[1][:], neglam[:, 0:1], om[0][:], ALU.mult, ALU.add,
                                [("om", 0, j) for j in range(4)] + [("om", 1, j) for j in range(4)] + ["neglam"], ["df"])
                            memset("dve", ssb[:], 0.0, ["ssb"])
                            for j in range(4):
                                act(junk[:], df[:, j, :], AF.Square, ["df", "ssb"], ["junkB", ("ssbd", j)], accum=ssb[:, j:j + 1])
                            ts("dve", rsb[:], ssb[:], 1.0 / 128, EPS, ALU.mult, ALU.add, [("ssbd", j) for j in range(4)], ["rsb"])
                            act(rsb[:], rsb[:], AF.Sqrt, ["rsb"], ["rsb"])
                            recip(rsb[:], rsb[:], ["rsb"], ["rsb"])
                            for j in range(4):
                                stt("dve", ob[j][:], df[:, j, :], rsb[:, j:j + 1], subw[:], ALU.mult, ALU.mult,
                                    ["df", "rsb", "subw"], [("ob", j)])
                                tr(ptr[:, j, :], ob[j][:], ident[:], [("ob", j), "ident"], ["ptrB"])
                            cp("act", oT[:, 4 + h, qc * 512:(qc + 1) * 512], ptr[:, 0:4, :].rearrange("p a q -> p (a q)"),
                               ["ptrB"], [("oT", 4 + h, qc * 4 + j) for j in range(4)])
                    P.barrier()
                    if stop == 3:
                        return nc

            with Scope() as R:
                x1 = R.sb("x1", [128, NT, D], F32)
                with Scope() as L:
                    wo = L.sb("wo", [128, 8, D], BF16)
                    wst = [L.sb("wst%d" % i, [128, D], F32) for i in range(2)]
                    xt = [L.sb("xtC%d" % i, [128, D], F32) for i in range(2)]
                    pc = [[L.ps("pc%d_%d" % (i, hf), [128, 512], F32) for hf in range(2)] for i in range(2)]
                    gbc, gkeys = build_bc(L, lambda k: modT[:, 16 + k, b:b + 1], [("modT", b)], "C")
                    for k in range(8):
                        sl = k % 2
                        dma("sp", wst[sl][:], wout_d[k * 128:(k + 1) * 128, :], (), [("wst", sl)], "s_wst%d" % sl)
                        tt("pool" if k % 3 == 2 else "dve", wo[:, k, :], wst[sl][:], gbc[:], ALU.mult,
                           [("wst", sl)] + gkeys, [("wo", k)])
                    for t in range(NT):
                        sl = t % 2
                        dma("sp", xt[sl][:], x_d[b, t * 128:(t + 1) * 128, :], (), [("xtC", sl)], "s_xtC%d" % sl)
                        for hf in range(2):
                            for k in range(8):
                                mm(pc[sl][hf][:], oT[:, k, t * 128:(t + 1) * 128], wo[:, k, hf * 512:(hf + 1) * 512],
                                   k == 0, k == 7, [("oT", k, t), ("wo", k)], [("pc", sl, hf)])
                            tt("dve", x1[:, t, hf * 512:(hf + 1) * 512], pc[sl][hf][:], xt[sl][:, hf * 512:(hf + 1) * 512],
                               ALU.add, [("pc", sl, hf), ("xtC", sl)], [("x1", t, hf)])
                    P.barrier()
                    if stop == 4:
                        return nc

                wguv = wgu_d.rearrange("(k p) n -> p k n", p=128)
                for half in range(2):
                    with Scope() as L:
                        actT = L.sb("actT", [128, NFC, 1024], BF16)
                        with Scope() as L2:
                            h2T = L2.sb("h2T", [128, 8, 1024], BF16)

                            def srcD(t):
                                return x1[:, t, :], [("x1", t, 0), ("x1", t, 1)]

                            with Scope() as L3:
                                abc = build_bc(L3, lambda k: aF[:, k, b:b + 1], [("aF", b)], "D")
                                sbc = build_bc(L3, lambda k: modT[:, 24 + k, b:b + 1], [("modT", b)], "Ds")
                                norm_phase(L3, srcD, h2T, abc, sbc, list(range(half * 8, half * 8 + 8)), 0, "D")
                                P.barrier()
                            h2keys = lambda k, t0, n: [("D", "T", k, t) for t in range(t0, t0 + n)]
                            with Scope() as L3:
                                wgu = [L3.sb("wgu%d" % i, [128, 8, 2, 128], BF16) for i in range(3)]
                                gs = [L3.sb("gs%d" % i, [128, 512], F32) for i in range(2)]
                                pg = [L3.ps("pg%d" % i, [128, 512], F32) for i in range(2)]
                                pu = [L3.ps("pu%d" % i, [128, 512], F32) for i in range(2)]
                                cnt = 0
                                for fc in range(NFC):
                                    sl = fc % 3
                                    dma("pool", wgu[sl][:, :, 0, :], wguv[:, :, fc * 128:(fc + 1) * 128], (), [("wgu", sl, 0)],
                                        "s_wgu%d_0" % sl)
                                    dma("pool", wgu[sl][:, :, 1, :], wguv[:, :, DFF + fc * 128:DFF + (fc + 1) * 128], (),
                                        [("wgu", sl, 1)], "s_wgu%d_1" % sl)
                                    for c in range(2):
                                        ps_ = cnt % 2
                                        cnt += 1
                                        for k in range(8):
                                            mm(pg[ps_][:], wgu[sl][:, k, 0, :], h2T[:, k, c * 512:(c + 1) * 512], k == 0, k == 7,
                                               [("wgu", sl, 0)] + h2keys(k, c * 4, 4), [("pg", ps_)])
                                        for k in range(8):
                                            mm(pu[ps_][:], wgu[sl][:, k, 1, :], h2T[:, k, c * 512:(c + 1) * 512], k == 0, k == 7,
                                               [("wgu", sl, 1)] + h2keys(k, c * 4, 4), [("pu", ps_)])
                                        act(gs[ps_][:], pg[ps_][:], AF.Silu, [("pg", ps_)], [("gs", ps_)])
                                        tt("dve", actT[:, fc, c * 512:(c + 1) * 512], pu[ps_][:], gs[ps_][:], ALU.mult,
                                           [("pu", ps_), ("gs", ps_)], [("actT", fc, c)])
                                P.barrier()
                        with Scope() as L2:
                            wd = L2.sb("wd", [128, NFC, 512], BF16)
                            wds = [L2.sb("wds%d" % i, [128, 512], F32) for i in range(2)]
                            pd = [L2.ps("pd%d" % i, [128, 512], F32) for i in range(4)]
                            ot = [L2.sb("ot%d" % i, [128, D], F32) for i in range(2)]
                            junk = L2.sb("junkF", [128, D], BF16)
                            ssf = L2.sb("ssf", [128, 8], F32)
                            rsf = L2.sb("rsf", [128, 8], F32)
                            cnt = 0
                            gbc, gkeys = build_bc(L2, lambda k: modT[:, 40 + k, b:b + 1], [("modT", b)], "F")
                            for dh in range(2):
                                for fc in range(NFC):
                                    sl = (dh * NFC + fc) % 2
                                    dma("sp", wds[sl][:], wdn_d[fc * 128:(fc + 1) * 128, dh * 512:(dh + 1) * 512], (),
                                        [("wds", sl)], "s_wds%d" % sl)
                                    tt("pool" if fc % 3 == 2 else "dve", wd[:, fc, :], wds[sl][:], gbc[:, dh * 512:(dh + 1) * 512], ALU.mult,
                                       [("wds", sl)] + gkeys, [("wd", fc)])
                                for ti in range(8):
                                    t = half * 8 + ti
                                    ps_ = cnt % 4
                                    cnt += 1
                                    for fc in range(NFC):
                                        mm(pd[ps_][:], actT[:, fc, ti * 128:(ti + 1) * 128], wd[:, fc, :], fc == 0, fc == NFC - 1,
                                           [("actT", fc, ti // 4), ("wd", fc)], [("pd", ps_)])
                                    tt("dve", x1[:, t, dh * 512:(dh + 1) * 512], pd[ps_][:], x1[:, t, dh * 512:(dh + 1) * 512],
                                       ALU.add, [("pd", ps_), ("x1", t, dh)], [("x1", t, dh)])
                            memset("dve", ssf[:], 0.0, ["ssf"])
                            for ti in range(8):
                                t = half * 8 + ti
                                sl = ti % 2
                                act(junk[:], x1[:, t, :], AF.Square, [("x1", t, 0), ("x1", t, 1), "ssf"],
                                    ["junkF", ("ssfd", ti)], accum=ssf[:, ti:ti + 1])
                                ts("dve", rsf[:, ti:ti + 1], ssf[:, ti:ti + 1], 1.0 / D, EPS, ALU.mult, ALU.add,
                                   [("ssfd", ti)], [("rsf", ti)])
                                act(rsf[:, ti:ti + 1], rsf[:, ti:ti + 1], AF.Sqrt, [("rsf", ti)], [("rsf", ti)])
                                recip(rsf[:, ti:ti + 1], rsf[:, ti:ti + 1], [("rsf", ti)], [("rsf", ti)])
                                stt("dve", ot[sl][:], x1[:, t, :], rsf[:, ti:ti + 1], gfbc[:], ALU.mult, ALU.mult,
                                    [("x1", t, 0), ("x1", t, 1), ("rsf", ti), "gfbc"], [("ot", sl)])
                                dma("sp", out_d[b, t * 128:(t + 1) * 128, :], ot[sl][:], [("ot", sl)], (),
                                    "s_out%d" % sl, final=True)
                            P.barrier(last=(b == NB - 1 and half == 1))
    return nc


_NC_CACHE = {}


def kernel(x, c, w_ada, b_ada, g_mix, w_in, rpb, lambda_q1, lambda_k1, lambda_q2, lambda_k2,
           subln_w, w_out, g_ffn, w_gate_up, w_down, g_final):
    f32 = np.float32
    x = np.asarray(x, f32)
    c = np.asarray(c, f32)
    if "nc" not in _NC_CACHE:
        _NC_CACHE["nc"] = build_nc()
    nc = _NC_CACHE["nc"]
    qaug, kaug, bdiag, mask = _host_consts()
    rpb0 = np.asarray(rpb, f32)[0]
    nabg = np.zeros((8, 128, NPAT, 128), f32)
    for p, pat in enumerate(_NA_BLKPAT):
        dr, dc, valid = _NA_PATS[pat]
        g_ = rpb0[:, dr, dc]
        nabg[:, :, p, :] = np.where(valid[None], g_, f32(0))
    nabg = nabg.reshape(8, 128, NPAT * 128)

    def colT(v, n):
        return np.ascontiguousarray(np.asarray(v, f32).reshape(n, 128).T)

    shared = {
        "w_ada": np.ascontiguousarray(np.asarray(w_ada, f32)[0]),
        "b_adaT": colT(np.asarray(b_ada, f32)[0], 48),
        "g_mixT": colT(np.asarray(g_mix, f32)[0], 8),
        "g_ffnT": colT(np.asarray(g_ffn, f32)[0], 8),
        "w_in": np.ascontiguousarray(np.asarray(w_in, f32)[0]),
        "w_out": np.ascontiguousarray(np.asarray(w_out, f32)[0]),
        "w_gate_up": np.ascontiguousarray(np.asarray(w_gate_up, f32)[0]),
        "w_down": np.ascontiguousarray(np.asarray(w_down, f32)[0]),
        "g_final": np.ascontiguousarray(np.asarray(g_final, f32)),
        "subln_w": np.ascontiguousarray(np.asarray(subln_w, f32)[0]),
        "lams": np.ascontiguousarray(np.stack([np.asarray(v, f32)[0] for v in
                                               (lambda_q1, lambda_k1, lambda_q2, lambda_k2)])),
        "nab_g": nabg,
        "na_mask": np.ascontiguousarray(mask.reshape(128, NPAT * 128)),
        "qaug": qaug, "kaug": kaug,
        "bdiag": np.ascontiguousarray(bdiag.reshape(128, 512)),
    }
    in_maps = []
    for i in range(8):
        cc = c[2 * i:2 * i + 2]
        cT = np.ascontiguousarray(cc.reshape(2, 8, 128).transpose(2, 1, 0).reshape(128, 16))
        m = dict(shared)
        m["x"] = np.ascontiguousarray(x[2 * i:2 * i + 2])
        m["cT"] = cT
        in_maps.append(m)
    res = run_bass_kernel_spmd(nc, in_maps, core_ids=list(range(8)))
    return np.concatenate([np.asarray(r["out"], f32) for r in res.results], axis=0)
```

```python
import contextlib
import math
import numpy as np
import concourse.bass as bass
import concourse.mybir as mybir
from concourse.bass_utils import run_bass_kernel_spmd

F32 = mybir.dt.float32
BF16 = mybir.dt.bfloat16
AF = mybir.ActivationFunctionType
ALU = mybir.AluOpType

D = 1024
S = 2048
NT = 16
NB = 2
DFF = 2816
NFC = 22
EPS = 1e-6
LAMBDA_INIT = 0.2
NEG = -30000.0
SAME_ENGINE_SYNC = True


class Prog:
    ENGINES = ("pe", "act", "dve", "pool", "sp")

    def __init__(self, nc, semstack):
        self.nc = nc
        self.semstack = semstack
        self.ops = []
        self.last_writer = {}
        self.readers = {}
        self.bar_deps = set()
        self.last_on_eng = {}
        self.dma_since_bar = []
        self.flushed = 0
        self.sems = {}
        self.counts = {}
        self.known = {e: {} for e in self.ENGINES}
        self.final_ops = []
        self.excl = set()

    def op(self, eng, fn, reads=(), writes=(), dma=None, final=False):
        oid = len(self.ops)
        deps = set(self.bar_deps)
        for k in reads:
            w = self.last_writer.get(k)
            if w is not None:
                deps.add(w)
            if k in self.excl:
                for r in self.readers.get(k, ()):
                    if self.ops[r]["eng"] != eng:
                        deps.add(r)
        for k in writes:
            w = self.last_writer.get(k)
            if w is not None:
                deps.add(w)
            for r in self.readers.get(k, ()):
                deps.add(r)
        for k in reads:
            self.readers.setdefault(k, []).append(oid)
        for k in writes:
            self.last_writer[k] = oid
            self.readers[k] = []
        self.ops.append(dict(eng=eng, fn=fn, deps=deps, dma=dma, needed=final, ev=None))
        self.last_on_eng[eng] = oid
        if dma is not None:
            self.dma_since_bar.append(oid)
        if final:
            self.final_ops.append(oid)
        return oid

    def _skip(self, do, ename):
        return do["dma"] is None and do["eng"] == ename and (ename == "pe" or not SAME_ENGINE_SYNC)

    def _sem(self, sn):
        if sn not in self.sems:
            self.sems[sn] = self.semstack.enter_context(self.nc.semaphore("sm%d" % len(self.sems)))
            self.counts[sn] = 0
        return self.sems[sn]

    def barrier(self, last=False):
        self.bar_deps = set(self.last_on_eng.values()) | set(self.dma_since_bar)
        self.dma_since_bar = []
        self._flush(last)

    def _flush(self, last):
        nc = self.nc
        ops = self.ops
        lo, hi = self.flushed, len(ops)
        self.flushed = hi
        for i in range(lo, hi):
            o = ops[i]
            for d in o["deps"]:
                if d >= lo and not self._skip(ops[d], o["eng"]):
                    ops[d]["needed"] = True
        for d in self.bar_deps:
            if d >= lo:
                ops[d]["needed"] = True
        for e in self.ENGINES:
            self._sem(("eng", e))
        for i in range(lo, hi):
            o = ops[i]
            if not o["needed"]:
                continue
            sn = ("dma", o["dma"]) if o["dma"] is not None else ("eng", o["eng"])
            self._sem(sn)
            self.counts[sn] += 16 if o["dma"] is not None else 1
            o["ev"] = (sn, self.counts[sn])
        by_eng = {e: [] for e in self.ENGINES}
        for i in range(lo, hi):
            by_eng[ops[i]["eng"]].append(i)

        def run_engine(ename, eh):
            known = self.known[ename]
            for i in by_eng[ename]:
                o = ops[i]
                need = {}
                for d in o["deps"]:
                    do = ops[d]
                    if do["ev"] is None or self._skip(do, ename):
                        continue
                    sn, v = do["ev"]
                    if need.get(sn, 0) < v:
                        need[sn] = v
                for sn, v in need.items():
                    if known.get(sn, 0) >= v:
                        continue
                    eh.wait_ge(self.sems[sn], v)
                    known[sn] = v
                ins = o["fn"](eh)
                if o["ev"] is not None:
                    ins.then_inc(self.sems[o["ev"][0]], 16 if o["dma"] is not None else 1)
                o["fn"] = None
            if last and ename == "sp":
                for d in self.final_ops:
                    sn, v = ops[d]["ev"]
                    if known.get(sn, 0) < v:
                        eh.wait_ge(self.sems[sn], v)
                        known[sn] = v

        with nc.Block() as block:
            @block.tensor
            def _(e):
                run_engine("pe", e)

            @block.scalar
            def _(e):
                run_engine("act", e)

            @block.vector
            def _(e):
                run_engine("dve", e)

            @block.gpsimd
            def _(e):
                run_engine("pool", e)

            @block.sync
            def _(e):
                run_engine("sp", e)


def _na_tables():
    rows, W, wh, ww = 32, 64, 8, 16
    pats = []
    pat_key = {}
    J = []
    pid = {}
    kk = np.arange(128)
    kr, ck = kk // 64, kk % 64
    for t in range(16):
        r_q = 2 * t + kr
        cq = ck
        rs = np.clip(r_q - wh // 2, 0, rows - wh)
        cs = np.clip(cq - ww // 2, 0, W - ww)
        jlo = int(rs.min()) // 2
        jhi = int(rs.max() + wh - 1) // 2
        js = list(range(jlo, jhi + 1))
        J.append(js)
        for j in js:
            krow = 2 * j + kr
            valid_r = (krow[:, None] >= rs[None, :]) & (krow[:, None] < rs[None, :] + wh)
            dr = krow[:, None] - r_q[None, :] + 7
            valid_c = (ck[:, None] >= cs[None, :]) & (ck[:, None] < cs[None, :] + ww)
            dc = np.clip(ck[:, None] - cq[None, :] + ww - 1, 0, 2 * ww - 2)
            valid = valid_r & valid_c
            drc = np.where(valid, dr, 0)
            key = (drc.tobytes(), dc.tobytes(), valid.tobytes())
            if key not in pat_key:
                pat_key[key] = len(pats)
                pats.append((drc.astype(np.int64), dc.astype(np.int64), valid))
            pid[(t, j)] = pat_key[key]
    return J, pid, pats


_NA_J, _NA_PID, _NA_PATS = _na_tables()


def _na_blocks():
    blk_pat = []
    off = {}
    ref = [_NA_PID[(2, j)] for j in _NA_J[2]]
    for t in range(2, 14):
        assert [_NA_PID[(t, j)] for j in _NA_J[t]] == ref
    blk_pat += ref
    for t in range(2, 14):
        off[t] = 0
    for t in (0, 1, 14, 15):
        off[t] = len(blk_pat)
        blk_pat += [_NA_PID[(t, j)] for j in _NA_J[t]]
    return blk_pat, off


_NA_BLKPAT, _NA_OFF = _na_blocks()
NPAT = len(_NA_BLKPAT)


def _host_consts():
    slopes = [2.0 ** (-8.0 * (i + 1) / 4) for i in range(4)]
    tok = np.arange(S)
    qaug = np.zeros((4, 4, S), np.float32)
    kaug = np.zeros((4, 2, 4, S), np.float32)
    bdiag = np.zeros((128, 4, 128), np.float32)
    for h, sl in enumerate(slopes):
        qaug[h, 0] = sl * (tok % 256)
        qaug[h, 1] = sl * (tok - tok % 256)
        qaug[h, 2] = 1.0
        qaug[h, 3] = 1.0
        kaug[h, 0, 0] = -1.0
        kaug[h, 0, 1] = -1.0
        kaug[h, 0, 2] = sl * (tok % 128)
        kaug[h, 0, 3] = sl * (tok - tok % 128)
        kaug[h, 1] = -kaug[h, 0]
        i = np.arange(128)
        bdiag[:, h, :] = -sl * np.abs(i[:, None] - i[None, :])
    mask = np.zeros((128, NPAT, 128), np.float32)
    for p, pat in enumerate(_NA_BLKPAT):
        mask[:, p, :] = np.where(_NA_PATS[pat][2], 0.0, NEG)
    return qaug, kaug, bdiag, mask


def build_nc(stop=None):
    nc = bass.Bass("TRN2", target_bir_lowering=False)

    def din(name, shape):
        return nc.dram_tensor(name, list(shape), F32, kind="ExternalInput").ap()

    x_d = din("x", [NB, S, D])
    cT_d = din("cT", [128, 16])
    wada_d = din("w_ada", [D, 6 * D])
    badaT_d = din("b_adaT", [128, 48])
    gmixT_d = din("g_mixT", [128, 8])
    gffnT_d = din("g_ffnT", [128, 8])
    win_d = din("w_in", [D, 3 * D])
    wout_d = din("w_out", [D, D])
    wgu_d = din("w_gate_up", [D, 2 * DFF])
    wdn_d = din("w_down", [DFF, D])
    gfin_d = din("g_final", [D])
    subw_d = din("subln_w", [128])
    lam_d = din("lams", [4, 64])
    nabg_d = din("nab_g", [8, 128, NPAT * 128])
    mask_d = din("na_mask", [128, NPAT * 128])
    qaug_d = din("qaug", [4, 4, S])
    kaug_d = din("kaug", [4, 2, 4, S])
    bdiag_d = din("bdiag", [128, 4 * 128])
    out_d = nc.dram_tensor("out", [NB, S, D], F32, kind="ExternalOutput").ap()

    semstack = contextlib.ExitStack()
    P = Prog(nc, semstack)
    _uid = [0]
    for _k in ["prjA", "prjB", "ptrA", "ptrB", "pmod"]:
        P.excl.add(_k)
    for _i in range(4):
        for _j in range(4):
            P.excl.update([("snA", _i), ("accA", _i), ("sps", _i), ("accb", _i, _j), ("pc", _i, _j), ("pg", _i), ("pu", _i),
                           ("pd", _i), ("ptr", "A", _i), ("ptr", "D", _i)])

    class Scope:
        def __init__(self):
            self.st = contextlib.ExitStack()

        def __enter__(self):
            self.st.__enter__()
            return self

        def __exit__(self, *a):
            return self.st.__exit__(*a)

        def sb(self, name, shape, dt):
            _uid[0] += 1
            return self.st.enter_context(nc.sbuf_tensor("%s_%d" % (name, _uid[0]), list(shape), dt))

        def ps(self, name, shape, dt):
            _uid[0] += 1
            return self.st.enter_context(nc.psum_tensor("%s_%d" % (name, _uid[0]), list(shape), dt))

    def mm(out, lhsT, rhs, start, stop, reads, writes, skip=False):
        if skip:
            P.op("pe", lambda e: e.matmul(out, lhsT=lhsT, rhs=rhs, start=start, stop=stop,
                                          skip_group_check=True), reads, writes)
        else:
            P.op("pe", lambda e: e.matmul(out, lhsT=lhsT, rhs=rhs, start=start, stop=stop), reads, writes)

    def tr(out, in_, ident, reads, writes):
        P.op("pe", lambda e: e.transpose(out=out, in_=in_, identity=ident), reads, writes)

    def act(out, in_, func, reads, writes, scale=None, bias=None, accum=None):
        kw = {}
        if scale is not None:
            kw["scale"] = scale
        if bias is not None:
            kw["bias"] = bias
        if accum is not None:
            kw["accum_out"] = accum
        P.op("act", lambda e: e.activation(out=out, in_=in_, func=func, **kw), reads, writes)

    def ts(eng, out, in0, s1, s2, op0, op1, reads, writes):
        if op1 is None:
            P.op(eng, lambda e: e.tensor_scalar(out=out, in0=in0, scalar1=s1, scalar2=None, op0=op0), reads, writes)
        else:
            P.op(eng, lambda e: e.tensor_scalar(out=out, in0=in0, scalar1=s1, scalar2=s2, op0=op0, op1=op1), reads, writes)

    def tt(eng, out, in0, in1, op, reads, writes):
        P.op(eng, lambda e: e.tensor_tensor(out=out, in0=in0, in1=in1, op=op), reads, writes)

    def stt(eng, out, in0, scalar, in1, op0, op1, reads, writes):
        P.op(eng, lambda e: e.scalar_tensor_tensor(out=out, in0=in0, scalar=scalar, in1=in1, op0=op0, op1=op1), reads, writes)

    def cp(eng, out, in_, reads, writes):
        if eng == "act":
            P.op(eng, lambda e: e.activation(out=out, in_=in_, func=AF.Copy), reads, writes)
        else:
            P.op(eng, lambda e: e.tensor_copy(out=out, in_=in_), reads, writes)

    def recip(out, in_, reads, writes):
        P.op("dve", lambda e: e.reciprocal(out=out, in_=in_), reads, writes)

    def memset(eng, ap, val, writes):
        P.op(eng, lambda e: e.memset(ap, val), (), writes)

    def dma(eng, out, in_, reads, writes, key, final=False):
        return P.op(eng, lambda e: e.dma_start(out=out, in_=in_), reads, writes, dma=key, final=final)

    with semstack, Scope() as G:
        identf = G.sb("identf", [128, 128], F32)
        ident = G.sb("ident", [128, 128], BF16)
        onesf = G.sb("onesf", [128, 128], F32)
        modT = G.sb("modT", [128, 48, 2], F32)
        aM = G.sb("aM", [128, 8, 2], F32)
        aF = G.sb("aF", [128, 8, 2], F32)
        gfbc = G.sb("gfbc", [128, D], F32)
        subw = G.sb("subw", [128, 128], F32)
        neglam = G.sb("neglam", [128, 1], F32)
        bdg = G.sb("bdg", [128, 4, 128], BF16)
        oT = G.sb("oT", [128, 8, S], BF16)

        def build_bc(L, colfn, rkeys, tag):
            bc = L.sb("bc" + tag, [128, D], F32)
            dg = [L.sb("dg%s%d" % (tag, i), [128, 128], F32) for i in range(2)]
            pb = [L.ps("pb%s%d" % (tag, i), [128, 512], F32) for i in range(2)]
            for half in range(2):
                for kk in range(4):
                    k = half * 4 + kk
                    sl = k % 2
                    ts("dve", dg[sl][:], identf[:], colfn(k), None, ALU.mult, None, ["identf"] + rkeys, [("dg", tag, sl)])
                    mm(pb[half][:, kk * 128:(kk + 1) * 128], onesf[:], dg[sl][:], True, True,
                       ["onesf", ("dg", tag, sl)], [("pb", tag, half)])
                cp("act", bc[:, half * 512:(half + 1) * 512], pb[half][:], [("pb", tag, half)], [("bc", tag, half)])
            return bc, [("bc", tag, 0), ("bc", tag, 1)]

        with Scope() as L:
            cT = L.sb("cT", [128, 16], F32)
            cS = L.sb("cS", [128, 16], BF16)
            badaT = L.sb("badaT", [128, 48], F32)
            gmT = L.sb("gmT", [128, 8], F32)
            gfT = L.sb("gfT", [128, 8], F32)
            lamv = L.sb("lamv", [128, 4, 64], F32)
            lamp = L.sb("lamp", [128, 2, 64], F32)
            lams = L.sb("lams_s", [128, 2], F32)
            wad = [L.sb("wad%d" % i, [128, 8, 1024], BF16) for i in range(2)]
            pmod = L.ps("pmod", [128, 512], F32)

            memset("pool", identf[:], 0.0, ["identf"])
            P.op("pool", lambda e: e.affine_select(out=identf[:], in_=identf[:], pattern=[[-1, 128]],
                                                   compare_op=ALU.not_equal, fill=1.0, base=0,
                                                   channel_multiplier=1), ["identf"], ["identf"])
            cp("dve", ident[:], identf[:], ["identf"], ["ident"])
            memset("pool", onesf[:], 1.0, ["onesf"])
            dma("sp", cT[:], cT_d, (), ["cT"], "s_cT")
            dma("sp", badaT[:], badaT_d, (), ["badaT"], "s_bada")
            dma("sp", gmT[:], gmixT_d, (), ["gmT"], "s_gm")
            dma("sp", gfT[:], gffnT_d, (), ["gfT"], "s_gf")
            dma("sp", gfbc[:], gfin_d.partition_broadcast(128), (), ["gfbc"], "s_gfbc")
            dma("sp", subw[:], subw_d.partition_broadcast(128), (), ["subw"], "s_subw")
            for i in range(4):
                dma("sp", lamv[:, i, :], lam_d[i].partition_broadcast(128), (), [("lamv", i)], "s_lam%d" % i)
            dma("pool", bdg[:].rearrange("p h q -> p (h q)"), bdiag_d, (), ["bdg"], "s_bdg")
            ts("dve", subw[:], subw[:], 1.0 - LAMBDA_INIT, None, ALU.mult, None, ["subw"], ["subw"])
            tt("dve", lamp[:, 0, :], lamv[:, 0, :], lamv[:, 1, :], ALU.mult, [("lamv", 0), ("lamv", 1)], [("lamp", 0)])
            tt("dve", lamp[:, 1, :], lamv[:, 2, :], lamv[:, 3, :], ALU.mult, [("lamv", 2), ("lamv", 3)], [("lamp", 1)])
            P.op("dve", lambda e: e.reduce_sum(out=lams[:, 0:1], in_=lamp[:, 0, :], axis=mybir.AxisListType.X),
                 [("lamp", 0)], [("lams", 0)])
            P.op("dve", lambda e: e.reduce_sum(out=lams[:, 1:2], in_=lamp[:, 1, :], axis=mybir.AxisListType.X),
                 [("lamp", 1)], [("lams", 1)])
            act(lams[:], lams[:], AF.Exp, [("lams", 0), ("lams", 1)], ["lamse"])
            tt("dve", neglam[:], lams[:, 1:2], lams[:, 0:1], ALU.subtract, ["lamse"], ["neglam"])
            ts("dve", neglam[:], neglam[:], -LAMBDA_INIT, None, ALU.add, None, ["neglam"], ["neglam"])
            act(cS[:], cT[:], AF.Silu, ["cT"], ["cS"])
            wv = wada_d.rearrange("(k p) n -> p k n", p=128)
            for pc in range(6):
                sl = pc % 2
                dma("pool", wad[sl][:], wv[:, :, pc * 1024:(pc + 1) * 1024], (), [("wad", sl)], "s_wad%d" % sl)
                for jj in range(8):
                    j = pc * 8 + jj
                    for k in range(8):
                        mm(pmod[:, 2 * j:2 * j + 2], wad[sl][:, k, jj * 128:(jj + 1) * 128],
                           cS[:, 2 * k:2 * k + 2], k == 0, k == 7, [("wad", sl), "cS"], ["pmod"])
            pm3 = pmod[:, 0:96].rearrange("p (j b) -> p j b", b=2)
            for b in range(2):
                tt("dve", modT[:, :, b], pm3[:, :, b], badaT[:], ALU.add, ["pmod", "badaT"], [("modT", b)])
            for b in range(2):
                stt("dve", aM[:, :, b], modT[:, 8:16, b], 1.0, gmT[:], ALU.add, ALU.mult, [("modT", b), "gmT"], [("aM", b)])
                stt("dve", aF[:, :, b], modT[:, 32:40, b], 1.0, gfT[:], ALU.add, ALU.mult, [("modT", b), "gfT"], [("aF", b)])
            P.barrier()
            if stop == 0:
                return nc

        for b in range(NB):
            with Scope() as M:
                hT = M.sb("hT", [128, 8, S], BF16)

                def norm_phase(L, src_tile_fn, dstT, a_sc, sh_sc, tiles, tok0, tagp):
                    ss = L.sb("ss" + tagp, [128, 16], F32)
                    rs = L.sb("rs" + tagp, [128, 16], F32)
                    junk = L.sb("junk" + tagp, [128, D], BF16)
                    xn = [L.sb("xn%s%d" % (tagp, i), [128, D], BF16) for i in range(2)]
                    xf = [L.sb("xf%s%d" % (tagp, i), [128, D], F32) for i in range(2)]
                    ptr = [L.ps("ptr%s%d" % (tagp, i), [128, 8, 128], BF16) for i in range(2)]
                    memset("dve", ss[:], 0.0, [("ss", tagp)])
                    import os
                    DBG = int(os.environ.get("DBGA", "9"))
                    for ti, t in enumerate(tiles):
                        src, rk = src_tile_fn(t)
                        sl = ti % 2
                        act(junk[:], src, AF.Square, rk + [("ss", tagp)], [("junk", tagp), ("ssd", tagp, ti)],
                            accum=ss[:, ti:ti + 1])
                        ts("dve", rs[:, ti:ti + 1], ss[:, ti:ti + 1], 1.0 / D, EPS, ALU.mult, ALU.add,
                           [("ssd", tagp, ti)], [("rs", tagp, ti)])
                        act(rs[:, ti:ti + 1], rs[:, ti:ti + 1], AF.Sqrt, [("rs", tagp, ti)], [("rs", tagp, ti)])
                        recip(rs[:, ti:ti + 1], rs[:, ti:ti + 1], [("rs", tagp, ti)], [("rs", tagp, ti)])
                        stt("dve", xf[sl][:], src, rs[:, ti:ti + 1], a_sc[0][:], ALU.mult, ALU.mult,
                            rk + [("rs", tagp, ti)] + a_sc[1], [("xf", tagp, sl)])
                        tt("pool" if ti % 2 == 0 else "dve", xn[sl][:], xf[sl][:], sh_sc[0][:], ALU.add, [("xf", tagp, sl)] + sh_sc[1], [("xn", tagp, sl)])
                        for k in range(8):
                            tr(ptr[sl][:, k, :], xn[sl][:, k * 128:(k + 1) * 128], ident[:],
                               [("xn", tagp, sl), "ident"], [("ptr", tagp, sl)])
                        dst = dstT[:, :, tok0 + ti * 128: tok0 + (ti + 1) * 128]
                        wk = [(tagp, "T", k, tok0 // 128 + ti) for k in range(8)]
                        cp("act" if ti % 2 == 0 else "dve", dst, ptr[sl][:], [("ptr", tagp, sl)], wk)

                with Scope() as L:
                    xt = [L.sb("xtA%d" % i, [128, D], F32) for i in range(2)]

                    def srcA(t):
                        sl = t % 2
                        dma("sp", xt[sl][:], x_d[b, t * 128:(t + 1) * 128, :], (), [("xtA", sl)], "s_xtA%d" % sl)
                        return xt[sl][:], [("xtA", sl)]

                    abc = build_bc(L, lambda k: aM[:, k, b:b + 1], [("aM", b)], "A")
                    sbc = build_bc(L, lambda k: modT[:, 0 + k, b:b + 1], [("modT", b)], "As")
                    norm_phase(L, srcA, hT, abc, sbc, list(range(NT)), 0, "A")
                    P.barrier()
                    if stop == 1:
                        return nc
                hkeys = lambda k, t0, n: [("A", "T", k, t) for t in range(t0, t0 + n)]

                with Scope() as L:
                    nab = L.sb("nab", [128, 8, NPAT, 128], BF16)
                    with Scope() as L2:
                        msk = L2.sb("msk", [128, NPAT * 128], F32)
                        stg = [L2.sb("stg%d" % i, [128, NPAT * 128], F32) for i in range(2)]
                        dma("sp", msk[:], mask_d, (), ["msk"], "s_msk")
                        for h in range(8):
                            sl = h % 2
                            dma("sp", stg[sl][:], nabg_d[h], (), [("stg", sl)], "s_stg%d" % sl)
                            tt("pool", nab[:, h, :, :].rearrange("p a q -> p (a q)"), stg[sl][:], msk[:], ALU.add,
                               [("stg", sl), "msk"], [("nab", h)])
                            act(nab[:, h, :, :].rearrange("p a q -> p (a q)"), nab[:, h, :, :].rearrange("p a q -> p (a q)"),
                                AF.Exp, [("nab", h)], [("nab", h)])
                        P.barrier()
                    wg = [L.sb("wgA%d" % i, [128, 8, 3, 128], BF16) for i in range(2)]
                    Qz = [L.sb("Qz%d" % i, [128, S], BF16) for i in range(2)]
                    KaT = L.sb("KaT", [128, S], BF16)
                    Va = L.sb("Va", [128, NT, 2, 65], BF16)
                    Ea = [L.sb("Ea%d" % i, [128, 640], BF16) for i in range(3)]
                    rc = L.sb("rcA", [128, 2], F32)
                    oa = [L.sb("oa%d" % i, [128, 128], BF16) for i in range(2)]
                    snA = [L.ps("snA%d" % i, [128, 1024], F32) for i in range(2)]
                    accA = [L.ps("accA%d" % i, [128, 512], F32) for i in range(2)]
                    prj = L.ps("prjA", [128, 512], F32)
                    ptr = L.ps("ptrA", [128, 8, 128], BF16)
                    memset("pool", Va[:, :, :, 64:65], 1.0, ["Va1"])
                    memset("pool", Qz[0][64:128, :], 0.0, [("Qzpad", 0)])
                    memset("pool", Qz[1][0:64, :], 0.0, [("Qzpad", 1)])
                    wiv = win_d.rearrange("(k p) n -> p k n", p=128)
                    ecnt = 0
                    for g in range(4):
                        sl = g % 2
                        for tsel in range(3):
                            c0 = tsel * 512 + g * 128
                            dma("pool", wg[sl][:, :, tsel, :], wiv[:, :, c0:c0 + 128], (), [("wgA", sl, tsel)],
                                "s_wgA%d_%d" % (sl, tsel))
                        for tsel, dst, scl in ((0, None, 0.125), (1, KaT, 1.0)):
                            for c in range(4):
                                for k in range(8):
                                    mm(prj[:], wg[sl][:, k, tsel, :], hT[:, k, c * 512:(c + 1) * 512], k == 0, k == 7,
                                       [("wgA", sl, tsel)] + hkeys(k, c * 4, 4), ["prjA"])
                                if tsel == 0:
                                    act(Qz[0][0:64, c * 512:(c + 1) * 512], prj[0:64, :], AF.Copy, ["prjA"], [("Qz", 0, c)], scale=scl)
                                    act(Qz[1][64:128, c * 512:(c + 1) * 512], prj[64:128, :], AF.Copy, ["prjA"], [("Qz", 1, c)], scale=scl)
                                else:
                                    cp("dve", dst[:, c * 512:(c + 1) * 512], prj[:], ["prjA"], [("KaT", c)])
                        for c in range(4):
                            for tq in range(4):
                                t = c * 4 + tq
                                for k in range(8):
                                    mm(prj[:, tq * 128:(tq + 1) * 128], hT[:, k, t * 128:(t + 1) * 128], wg[sl][:, k, 2, :],
                                       k == 0, k == 7, [("wgA", sl, 2)] + hkeys(k, t, 1), ["prjA"])
                            cp("dve", Va[:, c * 4:(c + 1) * 4, :, 0:64],
                               prj[:].rearrange("p (t h e) -> p t h e", t=4, h=2), ["prjA"], [("Va", c)])
                        units = [(t, hh) for t in range(NT) for hh in range(2)]
                        u0 = ecnt

                        def qk_unit(u):
                            t, hh = units[u]
                            ssl = (u0 + u) % 2
                            for i, j in enumerate(_NA_J[t]):
                                mm(snA[ssl][:, i * 128:(i + 1) * 128], KaT[:, j * 128:(j + 1) * 128],
                                   Qz[hh][:, t * 128:(t + 1) * 128], True, True,
                                   [("KaT", j // 4), ("Qz", hh, t // 4), ("Qzpad", hh)], [("snA", ssl)])

                        qk_unit(0)
                        for u, (t, hh) in enumerate(units):
                            js = _NA_J[t]
                            nb_ = len(js)
                            asl = t % 2
                            head = 2 * g + hh
                            ssl = (u0 + u) % 2
                            esl = (u0 + u) % 3
                            if u + 1 < len(units):
                                qk_unit(u + 1)
                            act(Ea[esl][:, 0:nb_ * 128], snA[ssl][:, 0:nb_ * 128], AF.Exp, [("snA", ssl)], [("Ea", esl)])
                            o0 = _NA_OFF[t]
                            tt("dve", Ea[esl][:, 0:nb_ * 128], Ea[esl][:, 0:nb_ * 128],
                               nab[:, head, o0:o0 + nb_, :].rearrange("p a q -> p (a q)"), ALU.mult,
                               [("Ea", esl), ("nab", head)], [("Ea", esl)])
                            for i, j in enumerate(js):
                                mm(accA[asl][:, hh * 65:(hh + 1) * 65], Ea[esl][:, i * 128:(i + 1) * 128],
                                   Va[:, j, hh, :], i == 0, i == nb_ - 1,
                                   [("Ea", esl), ("Va", j // 4), "Va1"], [("accA", asl)])
                            if hh == 0:
                                continue
                            a3 = accA[asl][:, 0:130].rearrange("p (h e) -> p h e", h=2)
                            recip(rc[:], a3[:, :, 64], [("accA", asl)], ["rcA"])
                            for h2 in range(2):
                                ts("dve", oa[asl][:, h2 * 64:(h2 + 1) * 64], a3[:, h2, 0:64], rc[:, h2:h2 + 1], None,
                                   ALU.mult, None, [("accA", asl), "rcA"], [("oa", asl, h2)])
                            tr(ptr[:, 0, :], oa[asl][:], ident[:], [("oa", asl, 0), ("oa", asl, 1), "ident"],
                               ["ptrA"])
                            cp("act", oT[:, g, t * 128:(t + 1) * 128], ptr[:, 0, :], ["ptrA"], [("oT", g, t)])
                        ecnt += len(units)
                    P.barrier()
                    if stop == 2:
                        return nc

                with Scope() as L:
                    wg = [L.sb("wgB%d" % i, [128, 8, 3, 128], BF16) for i in range(2)]
                    QT = [L.sb("QT%d" % m, [128, S], BF16) for m in range(2)]
                    KT = [[L.sb("KT%d_%d" % (m, v), [128, S], BF16) for v in range(2)] for m in range(2)]
                    Vb = L.sb("Vb", [128, NT, 129], BF16)
                    Eb = [L.sb("Eb%d" % i, [128, 512], BF16) for i in range(3)]
                    om = [L.sb("om%d" % m, [128, 4, 128], F32) for m in range(2)]
                    df = L.sb("df", [128, 4, 128], F32)
                    junk = L.sb("junkB", [128, 128], BF16)
                    ssb = L.sb("ssb", [128, 4], F32)
                    rsb = L.sb("rsb", [128, 4], F32)
                    rcb = L.sb("rcb", [128, 4], F32)
                    ob = [L.sb("ob%d" % i, [128, 128], BF16) for i in range(4)]
                    sps = [L.ps("sps%d" % i, [128, 512], F32) for i in range(2)]
                    accb = [[L.ps("accb%d_%d" % (s_, i), [128, 512], F32) for i in range(2)] for s_ in range(2)]
                    prj = L.ps("prjB", [128, 512], F32)
                    ptr = L.ps("ptrB", [128, 8, 128], BF16)
                    memset("pool", Vb[:, :, 128:129], 1.0, ["Vb1"])
                    for m_ in range(2):
                        memset("pool", QT[m_][64:128, :], 0.0, [("QTa", m_)])
                        for v_ in range(2):
                            memset("pool", KT[m_][v_][64:128, :], 0.0, [("KTa", m_, v_)])
                    wiv = win_d.rearrange("(k p) n -> p k n", p=128)
                    ecnt = 0
                    acnt = 0
                    tcnt = 0
                    for h in range(4):
                        sl = h % 2
                        for tsel in range(3):
                            c0 = 1536 + tsel * 512 + h * 128
                            dma("pool", wg[sl][:, :, tsel, :], wiv[:, :, c0:c0 + 128], (), [("wgB", sl, tsel)],
                                "s_wgB%d_%d" % (sl, tsel))
                        import os
                        DBGQ = os.environ.get("DBGQ", "").split(",")
                        for m in range(2):
                            if "noaug" in DBGQ:
                                continue
                            dma("pool", QT[m][64:68, :], qaug_d[h], (), [("QTa", m)], "s_qa%d" % m)
                            for v in range(2):
                                dma("pool", KT[m][v][64:68, :], kaug_d[h, v], (), [("KTa", m, v)], "s_ka%d_%d" % (m, v))
                        for c in range(4):
                            if "noq" in DBGQ:
                                continue
                            for k in range(8):
                                mm(prj[:], wg[sl][:, k, 0, :], hT[:, k, c * 512:(c + 1) * 512], k == 0, k == 7,
                                   [("wgB", sl, 0)] + hkeys(k, c * 4, 4), ["prjB"])
                            act(QT[0][0:64, c * 512:(c + 1) * 512], prj[0:64, :], AF.Copy, ["prjB"], [("QT", 0, c)], scale=0.125)
                            act(QT[1][0:64, c * 512:(c + 1) * 512], prj[64:128, :], AF.Copy, ["prjB"], [("QT", 1, c)], scale=0.125)
                        for c in range(4):
                            if "nok" in DBGQ:
                                continue
                            for k in range(8):
                                mm(prj[:], wg[sl][:, k, 1, :], hT[:, k, c * 512:(c + 1) * 512], k == 0, k == 7,
                                   [("wgB", sl, 1)] + hkeys(k, c * 4, 4), ["prjB"])
                            cp("dve", KT[0][0][0:64, c * 512:(c + 1) * 512], prj[0:64, :], ["prjB"], [("KT", 0, 0, c)])
                            cp("dve", KT[1][0][0:64, c * 512:(c + 1) * 512], prj[64:128, :], ["prjB"], [("KT", 1, 0, c)])
                            for m_ in range(2):
                                cp("pool", KT[m_][1][0:64, c * 512:(c + 1) * 512], KT[m_][0][0:64, c * 512:(c + 1) * 512],
                                   [("KT", m_, 0, c)], [("KT", m_, 1, c)])
                        for c in range(4):
                            if "nov" in DBGQ:
                                continue
                            for tq in range(4):
                                t = c * 4 + tq
                                for k in range(8):
                                    mm(prj[:, tq * 128:(tq + 1) * 128], hT[:, k, t * 128:(t + 1) * 128], wg[sl][:, k, 2, :],
                                       k == 0, k == 7, [("wgB", sl, 2)] + hkeys(k, t, 1), ["prjB"])
                            cp("dve", Vb[:, c * 4:(c + 1) * 4, 0:128], prj[:].rearrange("p (t e) -> p t e", t=4),
                               ["prjB"], [("Vb", c)])
                        import os
                        DBGB = int(os.environ.get("DBGB", "9"))
                        for qc in range(4):
                            if DBGB < 2:
                                continue
                            for m in range(2):
                                aset = accb[acnt % 2]
                                akey = acnt % 2
                                acnt += 1

                                def qk(kt, ssl):
                                    o = sps[ssl]
                                    wk = [("sps", ssl)]

                                    def rng(v, c0, c1, aug=True):
                                        kk = 128 if aug else 64
                                        rd = [("KT", m, v, kt // 4), ("QT", m, qc)]
                                        if aug:
                                            rd += [("KTa", m, v), ("QTa", m)]
                                        return (o[:, c0:c1], KT[m][v][0:kk, kt * 128:(kt + 1) * 128],
                                                QT[m][0:kk, qc * 512 + c0: qc * 512 + c1], rd)
                                    if kt < 4 * qc:
                                        o_, l_, r_, rd = rng(0, 0, 512)
                                        mm(o_, l_, r_, True, True, rd, wk)
                                    elif kt >= 4 * qc + 4:
                                        o_, l_, r_, rd = rng(1, 0, 512)
                                        mm(o_, l_, r_, True, True, rd, wk)
                                    else:
                                        jd = kt - 4 * qc
                                        if jd > 0:
                                            o_, l_, r_, rd = rng(1, 0, jd * 128)
                                            mm(o_, l_, r_, True, True, rd, wk)
                                        o_, l_, r_, rd = rng(0, jd * 128, (jd + 1) * 128, aug=False)
                                        mm(o_, l_, r_, True, False, rd, wk)
                                        mm(o_, ident[:], bdg[:, h, :], False, True, ["ident", "bdg"], wk)
                                        if jd < 3:
                                            o_, l_, r_, rd = rng(0, (jd + 1) * 128, 512)
                                            mm(o_, l_, r_, True, True, rd, wk)

                                qk(0, ecnt % 2)
                                for kt in range(16):
                                    ssl = ecnt % 2
                                    esl = ecnt % 3
                                    ecnt += 1
                                    if kt + 1 < 16:
                                        qk(kt + 1, ecnt % 2)
                                    act(Eb[esl][:], sps[ssl][:], AF.Exp, [("sps", ssl)], [("Eb", esl)])
                                    for j in range(4):
                                        if DBGB < 3:
                                            continue
                                        mm(aset[j // 2][:, (j % 2) * 129:(j % 2) * 129 + 129], Eb[esl][:, j * 128:(j + 1) * 128],
                                           Vb[:, kt, :], kt == 0 and j % 2 == 0, kt == 15,
                                           [("Eb", esl), ("Vb", kt // 4), "Vb1"], [("accb", akey, j // 2)], skip=True)
                                for j in range(4):
                                    if DBGB < 4:
                                        continue
                                    a_ = aset[j // 2]
                                    c0 = (j % 2) * 129
                                    recip(rcb[:, j:j + 1], a_[:, c0 + 128:c0 + 129], [("accb", akey, j // 2)], [("rcb", j)])
                                    ts("dve", om[m][:, j, :], a_[:, c0:c0 + 128], rcb[:, j:j + 1], None, ALU.mult, None,
                                       [("accb", akey, j // 2), ("rcb", j)], [("om", m, j)])
                            if DBGB < 5:
                                continue
                            stt("dve", df[:], om[1][:], neglam[:, 0:1], om[0][:], ALU.mult, ALU.add,
                                [("om", 0, j) for j in range(4)] + [("om", 1, j) for j in range(4)] + ["neglam"], ["df"])
                            memset("dve", ssb[:], 0.0, ["ssb"])
                            for j in range(4):
                                act(junk[:], df[:, j, :], AF.Square, ["df", "ssb"], ["junkB", ("ssbd", j)], accum=ssb[:, j:j + 1])
                            ts("dve", rsb[:], ssb[:], 1.0 / 128, EPS, ALU.mult, ALU.add, [("ssbd", j) for j in range(4)], ["rsb"])
                            act(rsb[:], rsb[:], AF.Sqrt, ["rsb"], ["rsb"])
                            recip(rsb[:], rsb[:], ["rsb"], ["rsb"])
                            for j in range(4):
                                stt("dve", ob[j][:], df[:, j, :], rsb[:, j:j + 1], subw[:], ALU.mult, ALU.mult,
                                    ["df", "rsb", "subw"], [("ob", j)])
                                tr(ptr[:, j, :], ob[j][:], ident[:], [("ob", j), "ident"], ["ptrB"])
                            cp("act", oT[:, 4 + h, qc * 512:(qc + 1) * 512], ptr[:, 0:4, :].rearrange("p a q -> p (a q)"),
                               ["ptrB"], [("oT", 4 + h, qc * 4 + j) for j in range(4)])
                    P.barrier()
                    if stop == 3:
                        return nc

            with Scope() as R:
                x1 = R.sb("x1", [128, NT, D], F32)
                with Scope() as L:
                    wo = L.sb("wo", [128, 8, D], BF16)
                    wst = [L.sb("wst%d" % i, [128, D], F32) for i in range(2)]
                    xt = [L.sb("xtC%d" % i, [128, D], F32) for i in range(2)]
                    pc = [[L.ps("pc%d_%d" % (i, hf), [128, 512], F32) for hf in range(2)] for i in range(2)]
                    gbc, gkeys = build_bc(L, lambda k: modT[:, 16 + k, b:b + 1], [("modT", b)], "C")
                    for k in range(8):
                        sl = k % 2
                        dma("sp", wst[sl][:], wout_d[k * 128:(k + 1) * 128, :], (), [("wst", sl)], "s_wst%d" % sl)
                        tt("pool" if k % 3 == 2 else "dve", wo[:, k, :], wst[sl][:], gbc[:], ALU.mult,
                           [("wst", sl)] + gkeys, [("wo", k)])
                    for t in range(NT):
                        sl = t % 2
                        dma("sp", xt[sl][:], x_d[b, t * 128:(t + 1) * 128, :], (), [("xtC", sl)], "s_xtC%d" % sl)
                        for hf in range(2):
                            for k in range(8):
                                mm(pc[sl][hf][:], oT[:, k, t * 128:(t + 1) * 128], wo[:, k, hf * 512:(hf + 1) * 512],
                                   k == 0, k == 7, [("oT", k, t), ("wo", k)], [("pc", sl, hf)])
                            tt("dve", x1[:, t, hf * 512:(hf + 1) * 512], pc[sl][hf][:], xt[sl][:, hf * 512:(hf + 1) * 512],
                               ALU.add, [("pc", sl, hf), ("xtC", sl)], [("x1", t, hf)])
                    P.barrier()
                    if stop == 4:
                        return nc

                wguv = wgu_d.rearrange("(k p) n -> p k n", p=128)
                for half in range(2):
                    with Scope() as L:
                        actT = L.sb("actT", [128, NFC, 1024], BF16)
                        with Scope() as L2:
                            h2T = L2.sb("h2T", [128, 8, 1024], BF16)

                            def srcD(t):
                                return x1[:, t, :], [("x1", t, 0), ("x1", t, 1)]

                            with Scope() as L3:
                                abc = build_bc(L3, lambda k: aF[:, k, b:b + 1], [("aF", b)], "D")
                                sbc = build_bc(L3, lambda k: modT[:, 24 + k, b:b + 1], [("modT", b)], "Ds")
                                norm_phase(L3, srcD, h2T, abc, sbc, list(range(half * 8, half * 8 + 8)), 0, "D")
                                P.barrier()
                            h2keys = lambda k, t0, n: [("D", "T", k, t) for t in range(t0, t0 + n)]
                            with Scope() as L3:
                                wgu = [L3.sb("wgu%d" % i, [128, 8, 2, 128], BF16) for i in range(3)]
                                gs = [L3.sb("gs%d" % i, [128, 512], F32) for i in range(2)]
                                pg = [L3.ps("pg%d" % i, [128, 512], F32) for i in range(2)]
                                pu = [L3.ps("pu%d" % i, [128, 512], F32) for i in range(2)]
                                cnt = 0
                                for fc in range(NFC):
                                    sl = fc % 3
                                    dma("pool", wgu[sl][:, :, 0, :], wguv[:, :, fc * 128:(fc + 1) * 128], (), [("wgu", sl, 0)],
                                        "s_wgu%d_0" % sl)
                                    dma("pool", wgu[sl][:, :, 1, :], wguv[:, :, DFF + fc * 128:DFF + (fc + 1) * 128], (),
                                        [("wgu", sl, 1)], "s_wgu%d_1" % sl)
                                    for c in range(2):
                                        ps_ = cnt % 2
                                        cnt += 1
                                        for k in range(8):
                                            mm(pg[ps_][:], wgu[sl][:, k, 0, :], h2T[:, k, c * 512:(c + 1) * 512], k == 0, k == 7,
                                               [("wgu", sl, 0)] + h2keys(k, c * 4, 4), [("pg", ps_)])
                                        for k in range(8):
                                            mm(pu[ps_][:], wgu[sl][:, k, 1, :], h2T[:, k, c * 512:(c + 1) * 512], k == 0, k == 7,
                                               [("wgu", sl, 1)] + h2keys(k, c * 4, 4), [("pu", ps_)])
                                        act(gs[ps_][:], pg[ps_][:], AF.Silu, [("pg", ps_)], [("gs", ps_)])
                                        tt("dve", actT[:, fc, c * 512:(c + 1) * 512], pu[ps_][:], gs[ps_][:], ALU.mult,
                                           [("pu", ps_), ("gs", ps_)], [("actT", fc, c)])
                                P.barrier()
                        with Scope() as L2:
                            wd = L2.sb("wd", [128, NFC, 512], BF16)
                            wds = [L2.sb("wds%d" % i, [128, 512], F32) for i in range(2)]
                            pd = [L2.ps("pd%d" % i, [128, 512], F32) for i in range(4)]
                            ot = [L2.sb("ot%d" % i, [128, D], F32) for i in range(2)]
                            junk = L2.sb("junkF", [128, D], BF16)
                            ssf = L2.sb("ssf", [128, 8], F32)
                            rsf = L2.sb("rsf", [128, 8], F32)
                            cnt = 0
                            gbc, gkeys = build_bc(L2, lambda k: modT[:, 40 + k, b:b + 1], [("modT", b)], "F")
                            for dh in range(2):
                                for fc in range(NFC):
                                    sl = (dh * NFC + fc) % 2
                                    dma("sp", wds[sl][:], wdn_d[fc * 128:(fc + 1) * 128, dh * 512:(dh + 1) * 512], (),
                                        [("wds", sl)], "s_wds%d" % sl)
                                    tt("pool" if fc % 3 == 2 else "dve", wd[:, fc, :], wds[sl][:], gbc[:, dh * 512:(dh + 1) * 512], ALU.mult,
                                       [("wds", sl)] + gkeys, [("wd", fc)])
                                for ti in range(8):
                                    t = half * 8 + ti
                                    ps_ = cnt % 4
                                    cnt += 1
                                    for fc in range(NFC):
                                        mm(pd[ps_][:], actT[:, fc, ti * 128:(ti + 1) * 128], wd[:, fc, :], fc == 0, fc == NFC - 1,
                                           [("actT", fc, ti // 4), ("wd", fc)], [("pd", ps_)])
                                    tt("dve", x1[:, t, dh * 512:(dh + 1) * 512], pd[ps_][:], x1[:, t, dh * 512:(dh + 1) * 512],
                                       ALU.add, [("pd", ps_), ("x1", t, dh)], [("x1", t, dh)])
                            memset("dve", ssf[:], 0.0, ["ssf"])
                            for ti in range(8):
                                t = half * 8 + ti
                                sl = ti % 2
                                act(junk[:], x1[:, t, :], AF.Square, [("x1", t, 0), ("x1", t, 1), "ssf"],
                                    ["junkF", ("ssfd", ti)], accum=ssf[:, ti:ti + 1])
                                ts("dve", rsf[:, ti:ti + 1], ssf[:, ti:ti + 1], 1.0 / D, EPS, ALU.mult, ALU.add,
                                   [("ssfd", ti)], [("rsf", ti)])
                                act(rsf[:, ti:ti + 1], rsf[:, ti:ti + 1], AF.Sqrt, [("rsf", ti)], [("rsf", ti)])
                                recip(rsf[:, ti:ti + 1], rsf[:, ti:ti + 1], [("rsf", ti)], [("rsf", ti)])
                                stt("dve", ot[sl][:], x1[:, t, :], rsf[:, ti:ti + 1], gfbc[:], ALU.mult, ALU.mult,
                                    [("x1", t, 0), ("x1", t, 1), ("rsf", ti), "gfbc"], [("ot", sl)])
                                dma("sp", out_d[b, t * 128:(t + 1) * 128, :], ot[sl][:], [("ot", sl)], (),
                                    "s_out%d" % sl, final=True)
                            P.barrier(last=(b == NB - 1 and half == 1))
    return nc


_NC_CACHE = {}


def kernel(x, c, w_ada, b_ada, g_mix, w_in, rpb, lambda_q1, lambda_k1, lambda_q2, lambda_k2,
           subln_w, w_out, g_ffn, w_gate_up, w_down, g_final):
    f32 = np.float32
    x = np.asarray(x, f32)
    c = np.asarray(c, f32)
    if "nc" not in _NC_CACHE:
        _NC_CACHE["nc"] = build_nc()
    nc = _NC_CACHE["nc"]
    qaug, kaug, bdiag, mask = _host_consts()
    rpb0 = np.asarray(rpb, f32)[0]
    nabg = np.zeros((8, 128, NPAT, 128), f32)
    for p, pat in enumerate(_NA_BLKPAT):
        dr, dc, valid = _NA_PATS[pat]
        g_ = rpb0[:, dr, dc]
        nabg[:, :, p, :] = np.where(valid[None], g_, f32(0))
    nabg = nabg.reshape(8, 128, NPAT * 128)

    def colT(v, n):
        return np.ascontiguousarray(np.asarray(v, f32).reshape(n, 128).T)

    shared = {
        "w_ada": np.ascontiguousarray(np.asarray(w_ada, f32)[0]),
        "b_adaT": colT(np.asarray(b_ada, f32)[0], 48),
        "g_mixT": colT(np.asarray(g_mix, f32)[0], 8),
        "g_ffnT": colT(np.asarray(g_ffn, f32)[0], 8),
        "w_in": np.ascontiguousarray(np.asarray(w_in, f32)[0]),
        "w_out": np.ascontiguousarray(np.asarray(w_out, f32)[0]),
        "w_gate_up": np.ascontiguousarray(np.asarray(w_gate_up, f32)[0]),
        "w_down": np.ascontiguousarray(np.asarray(w_down, f32)[0]),
        "g_final": np.ascontiguousarray(np.asarray(g_final, f32)),
        "subln_w": np.ascontiguousarray(np.asarray(subln_w, f32)[0]),
        "lams": np.ascontiguousarray(np.stack([np.asarray(v, f32)[0] for v in
                                               (lambda_q1, lambda_k1, lambda_q2, lambda_k2)])),
        "nab_g": nabg,
        "na_mask": np.ascontiguousarray(mask.reshape(128, NPAT * 128)),
        "qaug": qaug, "kaug": kaug,
        "bdiag": np.ascontiguousarray(bdiag.reshape(128, 512)),
    }
    in_maps = []
    for i in range(8):
        cc = c[2 * i:2 * i + 2]
        cT = np.ascontiguousarray(cc.reshape(2, 8, 128).transpose(2, 1, 0).reshape(128, 16))
        m = dict(shared)
        m["x"] = np.ascontiguousarray(x[2 * i:2 * i + 2])
        m["cT"] = cT
        in_maps.append(m)
    res = run_bass_kernel_spmd(nc, in_maps, core_ids=list(range(8)))
    return np.concatenate([np.asarray(r["out"], f32) for r in res.results], axis=0)
```

```python
import contextlib
import math
import numpy as np
import concourse.bass as bass
import concourse.mybir as mybir
from concourse.bass_utils import run_bass_kernel_spmd

F32 = mybir.dt.float32
BF16 = mybir.dt.bfloat16
AF = mybir.ActivationFunctionType
ALU = mybir.AluOpType

D = 1024
S = 2048
NT = 16
NB = 2
DFF = 2816
NFC = 22
EPS = 1e-6
LAMBDA_INIT = 0.2
NEG = -30000.0
SAME_ENGINE_SYNC = True


class Prog:
    ENGINES = ("pe", "act", "dve", "pool", "sp")

    def __init__(self, nc, semstack):
        self.nc = nc
        self.semstack = semstack
        self.ops = []
        self.last_writer = {}
        self.readers = {}
        self.bar_deps = set()
        self.last_on_eng = {}
        self.dma_since_bar = []
        self.flushed = 0
        self.sems = {}
        self.counts = {}
        self.known = {e: {} for e in self.ENGINES}
        self.final_ops = []
        self.excl = set()

    def op(self, eng, fn, reads=(), writes=(), dma=None, final=False):
        oid = len(self.ops)
        deps = set(self.bar_deps)
        for k in reads:
            w = self.last_writer.get(k)
            if w is not None:
                deps.add(w)
            if k in self.excl:
                for r in self.readers.get(k, ()):
                    if self.ops[r]["eng"] != eng:
                        deps.add(r)
        for k in writes:
            w = self.last_writer.get(k)
            if w is not None:
                deps.add(w)
            for r in self.readers.get(k, ()):
                deps.add(r)
        for k in reads:
            self.readers.setdefault(k, []).append(oid)
        for k in writes:
            self.last_writer[k] = oid
            self.readers[k] = []
        self.ops.append(dict(eng=eng, fn=fn, deps=deps, dma=dma, needed=final, ev=None))
        self.last_on_eng[eng] = oid
        if dma is not None:
            self.dma_since_bar.append(oid)
        if final:
            self.final_ops.append(oid)
        return oid

    def _skip(self, do, ename):
        return do["dma"] is None and do["eng"] == ename and (ename == "pe" or not SAME_ENGINE_SYNC)

    def _sem(self, sn):
        if sn not in self.sems:
            self.sems[sn] = self.semstack.enter_context(self.nc.semaphore("sm%d" % len(self.sems)))
            self.counts[sn] = 0
        return self.sems[sn]

    def barrier(self, last=False):
        self.bar_deps = set(self.last_on_eng.values()) | set(self.dma_since_bar)
        self.dma_since_bar = []
        self._flush(last)

    def _flush(self, last):
        nc = self.nc
        ops = self.ops
        lo, hi = self.flushed, len(ops)
        self.flushed = hi
        for i in range(lo, hi):
            o = ops[i]
            for d in o["deps"]:
                if d >= lo and not self._skip(ops[d], o["eng"]):
                    ops[d]["needed"] = True
        for d in self.bar_deps:
            if d >= lo:
                ops[d]["needed"] = True
        for e in self.ENGINES:
            self._sem(("eng", e))
        for i in range(lo, hi):
            o = ops[i]
            if not o["needed"]:
                continue
            sn = ("dma", o["dma"]) if o["dma"] is not None else ("eng", o["eng"])
            self._sem(sn)
            self.counts[sn] += 16 if o["dma"] is not None else 1
            o["ev"] = (sn, self.counts[sn])
        by_eng = {e: [] for e in self.ENGINES}
        for i in range(lo, hi):
            by_eng[ops[i]["eng"]].append(i)

        def run_engine(ename, eh):
            known = self.known[ename]
            for i in by_eng[ename]:
                o = ops[i]
                need = {}
                for d in o["deps"]:
                    do = ops[d]
                    if do["ev"] is None or self._skip(do, ename):
                        continue
                    sn, v = do["ev"]
                    if need.get(sn, 0) < v:
                        need[sn] = v
                for sn, v in need.items():
                    if known.get(sn, 0) >= v:
                        continue
                    eh.wait_ge(self.sems[sn], v)
                    known[sn] = v
                ins = o["fn"](eh)
                if o["ev"] is not None:
                    ins.then_inc(self.sems[o["ev"][0]], 16 if o["dma"] is not None else 1)
                o["fn"] = None
            if last and ename == "sp":
                for d in self.final_ops:
                    sn, v = ops[d]["ev"]
                    if known.get(sn, 0) < v:
                        eh.wait_ge(self.sems[sn], v)
                        known[sn] = v

        with nc.Block() as block:
            @block.tensor
            def _(e):
                run_engine("pe", e)

            @block.scalar
            def _(e):
                run_engine("act", e)

            @block.vector
            def _(e):
                run_engine("dve", e)

            @block.gpsimd
            def _(e):
                run_engine("pool", e)

            @block.sync
            def _(e):
                run_engine("sp", e)


def _na_tables():
    rows, W, wh, ww = 32, 64, 8, 16
    pats = []
    pat_key = {}
    J = []
    pid = {}
    kk = np.arange(128)
    kr, ck = kk // 64, kk % 64
    for t in range(16):
        r_q = 2 * t + kr
        cq = ck
        rs = np.clip(r_q - wh // 2, 0, rows - wh)
        cs = np.clip(cq - ww // 2, 0, W - ww)
        jlo = int(rs.min()) // 2
        jhi = int(rs.max() + wh - 1) // 2
        js = list(range(jlo, jhi + 1))
        J.append(js)
        for j in js:
            krow = 2 * j + kr
            valid_r = (krow[:, None] >= rs[None, :]) & (krow[:, None] < rs[None, :] + wh)
            dr = krow[:, None] - r_q[None, :] + 7
            valid_c = (ck[:, None] >= cs[None, :]) & (ck[:, None] < cs[None, :] + ww)
            dc = np.clip(ck[:, None] - cq[None, :] + ww - 1, 0, 2 * ww - 2)
            valid = valid_r & valid_c
            drc = np.where(valid, dr, 0)
            key = (drc.tobytes(), dc.tobytes(), valid.tobytes())
            if key not in pat_key:
                pat_key[key] = len(pats)
                pats.append((drc.astype(np.int64), dc.astype(np.int64), valid))
            pid[(t, j)] = pat_key[key]
    return J, pid, pats


_NA_J, _NA_PID, _NA_PATS = _na_tables()


def _na_blocks():
    blk_pat = []
    off = {}
    ref = [_NA_PID[(2, j)] for j in _NA_J[2]]
    for t in range(2, 14):
        assert [_NA_PID[(t, j)] for j in _NA_J[t]] == ref
    blk_pat += ref
    for t in range(2, 14):
        off[t] = 0
    for t in (0, 1, 14, 15):
        off[t] = len(blk_pat)
        blk_pat += [_NA_PID[(t, j)] for j in _NA_J[t]]
    return blk_pat, off


_NA_BLKPAT, _NA_OFF = _na_blocks()
NPAT = len(_NA_BLKPAT)


def _host_consts():
    slopes = [2.0 ** (-8.0 * (i + 1) / 4) for i in range(4)]
    tok = np.arange(S)
    qaug = np.zeros((4, 4, S), np.float32)
    kaug = np.zeros((4, 2, 4, S), np.float32)
    bdiag = np.zeros((128, 4, 128), np.float32)
    for h, sl in enumerate(slopes):
        qaug[h, 0] = sl * (tok % 256)
        qaug[h, 1] = sl * (tok - tok % 256)
        qaug[h, 2] = 1.0
        qaug[h, 3] = 1.0
        kaug[h, 0, 0] = -1.0
        kaug[h, 0, 1] = -1.0
        kaug[h, 0, 2] = sl * (tok % 128)
        kaug[h, 0, 3] = sl * (tok - tok % 128)
        kaug[h, 1] = -kaug[h, 0]
        i = np.arange(128)
        bdiag[:, h, :] = -sl * np.abs(i[:, None] - i[None, :])
    mask = np.zeros((128, NPAT, 128), np.float32)
    for p, pat in enumerate(_NA_BLKPAT):
        mask[:, p, :] = np.where(_NA_PATS[pat][2], 0.0, NEG)
    return qaug, kaug, bdiag, mask


def build_nc(stop=None):
    nc = bass.Bass("TRN2", target_bir_lowering=False)

    def din(name, shape):
        return nc.dram_tensor(name, list(shape), F32, kind="ExternalInput").ap()

    x_d = din("x", [NB, S, D])
    cT_d = din("cT", [128, 16])
    wada_d = din("w_ada", [D, 6 * D])
    badaT_d = din("b_adaT", [128, 48])
    gmixT_d = din("g_mixT", [128, 8])
    gffnT_d = din("g_ffnT", [128, 8])
    win_d = din("w_in", [D, 3 * D])
    wout_d = din("w_out", [D, D])
    wgu_d = din("w_gate_up", [D, 2 * DFF])
    wdn_d = din("w_down", [DFF, D])
    gfin_d = din("g_final", [D])
    subw_d = din("subln_w", [128])
    lam_d = din("lams", [4, 64])
    nabg_d = din("nab_g", [8, 128, NPAT * 128])
    mask_d = din("na_mask", [128, NPAT * 128])
    qaug_d = din("qaug", [4, 4, S])
    kaug_d = din("kaug", [4, 2, 4, S])
    bdiag_d = din("bdiag", [128, 4 * 128])
    out_d = nc.dram_tensor("out", [NB, S, D], F32, kind="ExternalOutput").ap()

    semstack = contextlib.ExitStack()
    P = Prog(nc, semstack)
    _uid = [0]
    for _k in ["prjA", "prjB", "ptrA", "ptrB", "pmod"]:
        P.excl.add(_k)
    for _i in range(4):
        for _j in range(4):
            P.excl.update([("snA", _i), ("accA", _i), ("sps", _i), ("accb", _i, _j), ("pc", _i, _j), ("pg", _i), ("pu", _i),
                           ("pd", _i), ("ptr", "A", _i), ("ptr", "D", _i)])

    class Scope:
        def __init__(self):
            self.st = contextlib.ExitStack()

        def __enter__(self):
            self.st.__enter__()
            return self

        def __exit__(self, *a):
            return self.st.__exit__(*a)

        def sb(self, name, shape, dt):
            _uid[0] += 1
            return self.st.enter_context(nc.sbuf_tensor("%s_%d" % (name, _uid[0]), list(shape), dt))

        def ps(self, name, shape, dt):
            _uid[0] += 1
            return self.st.enter_context(nc.psum_tensor("%s_%d" % (name, _uid[0]), list(shape), dt))

    def mm(out, lhsT, rhs, start, stop, reads, writes, skip=False):
        if skip:
            P.op("pe", lambda e: e.matmul(out, lhsT=lhsT, rhs=rhs, start=start, stop=stop,
                                          skip_group_check=True), reads, writes)
        else:
            P.op("pe", lambda e: e.matmul(out, lhsT=lhsT, rhs=rhs, start=start, stop=stop), reads, writes)

    def tr(out, in_, ident, reads, writes):
        P.op("pe", lambda e: e.transpose(out=out, in_=in_, identity=ident), reads, writes)

    def act(out, in_, func, reads, writes, scale=None, bias=None, accum=None):
        kw = {}
        if scale is not None:
            kw["scale"] = scale
        if bias is not None:
            kw["bias"] = bias
        if accum is not None:
            kw["accum_out"] = accum
        P.op("act", lambda e: e.activation(out=out, in_=in_, func=func, **kw), reads, writes)

    def ts(eng, out, in0, s1, s2, op0, op1, reads, writes):
        if op1 is None:
            P.op(eng, lambda e: e.tensor_scalar(out=out, in0=in0, scalar1=s1, scalar2=None, op0=op0), reads, writes)
        else:
            P.op(eng, lambda e: e.tensor_scalar(out=out, in0=in0, scalar1=s1, scalar2=s2, op0=op0, op1=op1), reads, writes)

    def tt(eng, out, in0, in1, op, reads, writes):
        P.op(eng, lambda e: e.tensor_tensor(out=out, in0=in0, in1=in1, op=op), reads, writes)

    def stt(eng, out, in0, scalar, in1, op0, op1, reads, writes):
        P.op(eng, lambda e: e.scalar_tensor_tensor(out=out, in0=in0, scalar=scalar, in1=in1, op0=op0, op1=op1), reads, writes)

    def cp(eng, out, in_, reads, writes):
        if eng == "act":
            P.op(eng, lambda e: e.activation(out=out, in_=in_, func=AF.Copy), reads, writes)
        else:
            P.op(eng, lambda e: e.tensor_copy(out=out, in_=in_), reads, writes)

    def recip(out, in_, reads, writes):
        P.op("dve", lambda e: e.reciprocal(out=out, in_=in_), reads, writes)

    def memset(eng, ap, val, writes):
        P.op(eng, lambda e: e.memset(ap, val), (), writes)

    def dma(eng, out, in_, reads, writes, key, final=False):
        return P.op(eng, lambda e: e.dma_start(out=out, in_=in_), reads, writes, dma=key, final=final)

    with semstack, Scope() as G:
        identf = G.sb("identf", [128, 128], F32)
        ident = G.sb("ident", [128, 128], BF16)
        onesf = G.sb("onesf", [128, 128], F32)
        modT = G.sb("modT", [128, 48, 2], F32)
        aM = G.sb("aM", [128, 8, 2], F32)
        aF = G.sb("aF", [128, 8, 2], F32)
        gfbc = G.sb("gfbc", [128, D], F32)
        subw = G.sb("subw", [128, 128], F32)
        neglam = G.sb("neglam", [128, 1], F32)
        bdg = G.sb("bdg", [128, 4, 128], BF16)
        oT = G.sb("oT", [128, 8, S], BF16)

        def build_bc(L, colfn, rkeys, tag):
            bc = L.sb("bc" + tag, [128, D], F32)
            dg = [L.sb("dg%s%d" % (tag, i), [128, 128], F32) for i in range(2)]
            pb = [L.ps("pb%s%d" % (tag, i), [128, 512], F32) for i in range(2)]
            for half in range(2):
                for kk in range(4):
                    k = half * 4 + kk
                    sl = k % 2
                    ts("dve", dg[sl][:], identf[:], colfn(k), None, ALU.mult, None, ["identf"] + rkeys, [("dg", tag, sl)])
                    mm(pb[half][:, kk * 128:(kk + 1) * 128], onesf[:], dg[sl][:], True, True,
                       ["onesf", ("dg", tag, sl)], [("pb", tag, half)])
                cp("act", bc[:, half * 512:(half + 1) * 512], pb[half][:], [("pb", tag, half)], [("bc", tag, half)])
            return bc, [("bc", tag, 0), ("bc", tag, 1)]

        with Scope() as L:
            cT = L.sb("cT", [128, 16], F32)
            cS = L.sb("cS", [128, 16], BF16)
            badaT = L.sb("badaT", [128, 48], F32)
            gmT = L.sb("gmT", [128, 8], F32)
            gfT = L.sb("gfT", [128, 8], F32)
            lamv = L.sb("lamv", [128, 4, 64], F32)
            lamp = L.sb("lamp", [128, 2, 64], F32)
            lams = L.sb("lams_s", [128, 2], F32)
            wad = [L.sb("wad%d" % i, [128, 8, 1024], BF16) for i in range(2)]
            pmod = L.ps("pmod", [128, 512], F32)

            memset("pool", identf[:], 0.0, ["identf"])
            P.op("pool", lambda e: e.affine_select(out=identf[:], in_=identf[:], pattern=[[-1, 128]],
                                                   compare_op=ALU.not_equal, fill=1.0, base=0,
                                                   channel_multiplier=1), ["identf"], ["identf"])
            cp("dve", ident[:], identf[:], ["identf"], ["ident"])
            memset("pool", onesf[:], 1.0, ["onesf"])
            dma("sp", cT[:], cT_d, (), ["cT"], "s_cT")
            dma("sp", badaT[:], badaT_d, (), ["badaT"], "s_bada")
            dma("sp", gmT[:], gmixT_d, (), ["gmT"], "s_gm")
            dma("sp", gfT[:], gffnT_d, (), ["gfT"], "s_gf")
            dma("sp", gfbc[:], gfin_d.partition_broadcast(128), (), ["gfbc"], "s_gfbc")
            dma("sp", subw[:], subw_d.partition_broadcast(128), (), ["subw"], "s_subw")
            for i in range(4):
                dma("sp", lamv[:, i, :], lam_d[i].partition_broadcast(128), (), [("lamv", i)], "s_lam%d" % i)
            dma("pool", bdg[:].rearrange("p h q -> p (h q)"), bdiag_d, (), ["bdg"], "s_bdg")
            ts("dve", subw[:], subw[:], 1.0 - LAMBDA_INIT, None, ALU.mult, None, ["subw"], ["subw"])
            tt("dve", lamp[:, 0, :], lamv[:, 0, :], lamv[:, 1, :], ALU.mult, [("lamv", 0), ("lamv", 1)], [("lamp", 0)])
            tt("dve", lamp[:, 1, :], lamv[:, 2, :], lamv[:, 3, :], ALU.mult, [("lamv", 2), ("lamv", 3)], [("lamp", 1)])
            P.op("dve", lambda e: e.reduce_sum(out=lams[:, 0:1], in_=lamp[:, 0, :], axis=mybir.AxisListType.X),
                 [("lamp", 0)], [("lams", 0)])
            P.op("dve", lambda e: e.reduce_sum(out=lams[:, 1:2], in_=lamp[:, 1, :], axis=mybir.AxisListType.X),
                 [("lamp", 1)], [("lams", 1)])
            act(lams[:], lams[:], AF.Exp, [("lams", 0), ("lams", 1)], ["lamse"])
            tt("dve", neglam[:], lams[:, 1:2], lams[:, 0:1], ALU.subtract, ["lamse"], ["neglam"])
            ts("dve", neglam[:], neglam[:], -LAMBDA_INIT, None, ALU.add, None, ["neglam"], ["neglam"])
            act(cS[:], cT[:], AF.Silu, ["cT"], ["cS"])
            wv = wada_d.rearrange("(k p) n -> p k n", p=128)
            for pc in range(6):
                sl = pc % 2
                dma("pool", wad[sl][:], wv[:, :, pc * 1024:(pc + 1) * 1024], (), [("wad", sl)], "s_wad%d" % sl)
                for jj in range(8):
                    j = pc * 8 + jj
                    for k in range(8):
                        mm(pmod[:, 2 * j:2 * j + 2], wad[sl][:, k, jj * 128:(jj + 1) * 128],
                           cS[:, 2 * k:2 * k + 2], k == 0, k == 7, [("wad", sl), "cS"], ["pmod"])
            pm3 = pmod[:, 0:96].rearrange("p (j b) -> p j b", b=2)
            for b in range(2):
                tt("dve", modT[:, :, b], pm3[:, :, b], badaT[:], ALU.add, ["pmod", "badaT"], [("modT", b)])
            for b in range(2):
                stt("dve", aM[:, :, b], modT[:, 8:16, b], 1.0, gmT[:], ALU.add, ALU.mult, [("modT", b), "gmT"], [("aM", b)])
                stt("dve", aF[:, :, b], modT[:, 32:40, b], 1.0, gfT[:], ALU.add, ALU.mult, [("modT", b), "gfT"], [("aF", b)])
            P.barrier()
            if stop == 0:
                return nc

        for b in range(NB):
            with Scope() as M:
                hT = M.sb("hT", [128, 8, S], BF16)

                def norm_phase(L, src_tile_fn, dstT, a_sc, sh_sc, tiles, tok0, tagp):
                    ss = L.sb("ss" + tagp, [128, 16], F32)
                    rs = L.sb("rs" + tagp, [128, 16], F32)
                    junk = L.sb("junk" + tagp, [128, D], BF16)
                    xn = [L.sb("xn%s%d" % (tagp, i), [128, D], BF16) for i in range(2)]
                    xf = [L.sb("xf%s%d" % (tagp, i), [128, D], F32) for i in range(2)]
                    ptr = [L.ps("ptr%s%d" % (tagp, i), [128, 8, 128], BF16) for i in range(2)]
                    memset("dve", ss[:], 0.0, [("ss", tagp)])
                    import os
                    DBG = int(os.environ.get("DBGA", "9"))
                    for ti, t in enumerate(tiles):
                        src, rk = src_tile_fn(t)
                        sl = ti % 2
                        act(junk[:], src, AF.Square, rk + [("ss", tagp)], [("junk", tagp), ("ssd", tagp, ti)],
                            accum=ss[:, ti:ti + 1])
                        ts("dve", rs[:, ti:ti + 1], ss[:, ti:ti + 1], 1.0 / D, EPS, ALU.mult, ALU.add,
                           [("ssd", tagp, ti)], [("rs", tagp, ti)])
                        act(rs[:, ti:ti + 1], rs[:, ti:ti + 1], AF.Sqrt, [("rs", tagp, ti)], [("rs", tagp, ti)])
                        recip(rs[:, ti:ti + 1], rs[:, ti:ti + 1], [("rs", tagp, ti)], [("rs", tagp, ti)])
                        stt("dve", xf[sl][:], src, rs[:, ti:ti + 1], a_sc[0][:], ALU.mult, ALU.mult,
                            rk + [("rs", tagp, ti)] + a_sc[1], [("xf", tagp, sl)])
                        tt("pool" if ti % 2 == 0 else "dve", xn[sl][:], xf[sl][:], sh_sc[0][:], ALU.add, [("xf", tagp, sl)] + sh_sc[1], [("xn", tagp, sl)])
                        for k in range(8):
                            tr(ptr[sl][:, k, :], xn[sl][:, k * 128:(k + 1) * 128], ident[:],
                               [("xn", tagp, sl), "ident"], [("ptr", tagp, sl)])
                        dst = dstT[:, :, tok0 + ti * 128: tok0 + (ti + 1) * 128]
                        wk = [(tagp, "T", k, tok0 // 128 + ti) for k in range(8)]
                        cp("act" if ti % 2 == 0 else "dve", dst, ptr[sl][:], [("ptr", tagp, sl)], wk)

                with Scope() as L:
                    xt = [L.sb("xtA%d" % i, [128, D], F32) for i in range(2)]

                    def srcA(t):
                        sl = t % 2
                        dma("sp", xt[sl][:], x_d[b, t * 128:(t + 1) * 128, :], (), [("xtA", sl)], "s_xtA%d" % sl)
                        return xt[sl][:], [("xtA", sl)]

                    abc = build_bc(L, lambda k: aM[:, k, b:b + 1], [("aM", b)], "A")
                    sbc = build_bc(L, lambda k: modT[:, 0 + k, b:b + 1], [("modT", b)], "As")
                    norm_phase(L, srcA, hT, abc, sbc, list(range(NT)), 0, "A")
                    P.barrier()
                    if stop == 1:
                        return nc
                hkeys = lambda k, t0, n: [("A", "T", k, t) for t in range(t0, t0 + n)]

                with Scope() as L:
                    nab = L.sb("nab", [128, 8, NPAT, 128], BF16)
                    with Scope() as L2:
                        msk = L2.sb("msk", [128, NPAT * 128], F32)
                        stg = [L2.sb("stg%d" % i, [128, NPAT * 128], F32) for i in range(2)]
                        dma("sp", msk[:], mask_d, (), ["msk"], "s_msk")
                        for h in range(8):
                            sl = h % 2
                            dma("sp", stg[sl][:], nabg_d[h], (), [("stg", sl)], "s_stg%d" % sl)
                            tt("pool", nab[:, h, :, :].rearrange("p a q -> p (a q)"), stg[sl][:], msk[:], ALU.add,
                               [("stg", sl), "msk"], [("nab", h)])
                            act(nab[:, h, :, :].rearrange("p a q -> p (a q)"), nab[:, h, :, :].rearrange("p a q -> p (a q)"),
                                AF.Exp, [("nab", h)], [("nab", h)])
                        P.barrier()
                    wg = [L.sb("wgA%d" % i, [128, 8, 3, 128], BF16) for i in range(2)]
                    Qz = [L.sb("Qz%d" % i, [128, S], BF16) for i in range(2)]
                    KaT = L.sb("KaT", [128, S], BF16)
                    Va = L.sb("Va", [128, NT, 2, 65], BF16)
                    Ea = [L.sb("Ea%d" % i, [128, 640], BF16) for i in range(3)]
                    rc = L.sb("rcA", [128, 2], F32)
                    oa = [L.sb("oa%d" % i, [128, 128], BF16) for i in range(2)]
                    snA = [L.ps("snA%d" % i, [128, 1024], F32) for i in range(2)]
                    accA = [L.ps("accA%d" % i, [128, 512], F32) for i in range(2)]
                    prj = L.ps("prjA", [128, 512], F32)
                    ptr = L.ps("ptrA", [128, 8, 128], BF16)
                    memset("pool", Va[:, :, :, 64:65], 1.0, ["Va1"])
                    memset("pool", Qz[0][64:128, :], 0.0, [("Qzpad", 0)])
                    memset("pool", Qz[1][0:64, :], 0.0, [("Qzpad", 1)])
                    wiv = win_d.rearrange("(k p) n -> p k n", p=128)
                    ecnt = 0
                    for g in range(4):
                        sl = g % 2
                        for tsel in range(3):
                            c0 = tsel * 512 + g * 128
                            dma("pool", wg[sl][:, :, tsel, :], wiv[:, :, c0:c0 + 128], (), [("wgA", sl, tsel)],
                                "s_wgA%d_%d" % (sl, tsel))
                        for tsel, dst, scl in ((0, None, 0.125), (1, KaT, 1.0)):
                            for c in range(4):
                                for k in range(8):
                                    mm(prj[:], wg[sl][:, k, tsel, :], hT[:, k, c * 512:(c + 1) * 512], k == 0, k == 7,
                                       [("wgA", sl, tsel)] + hkeys(k, c * 4, 4), ["prjA"])
                                if tsel == 0:
                                    act(Qz[0][0:64, c * 512:(c + 1) * 512], prj[0:64, :], AF.Copy, ["prjA"], [("Qz", 0, c)], scale=scl)
                                    act(Qz[1][64:128, c * 512:(c + 1) * 512], prj[64:128, :], AF.Copy, ["prjA"], [("Qz", 1, c)], scale=scl)
                                else:
                                    cp("dve", dst[:, c * 512:(c + 1) * 512], prj[:], ["prjA"], [("KaT", c)])
                        for c in range(4):
                            for tq in range(4):
                                t = c * 4 + tq
                                for k in range(8):
                                    mm(prj[:, tq * 128:(tq + 1) * 128], hT[:, k, t * 128:(t + 1) * 128], wg[sl][:, k, 2, :],
                                       k == 0, k == 7, [("wgA", sl, 2)] + hkeys(k, t, 1), ["prjA"])
                            cp("dve", Va[:, c * 4:(c + 1) * 4, :, 0:64],
                               prj[:].rearrange("p (t h e) -> p t h e", t=4, h=2), ["prjA"], [("Va", c)])
                        units = [(t, hh) for t in range(NT) for hh in range(2)]
                        u0 = ecnt

                        def qk_unit(u):
                            t, hh = units[u]
                            ssl = (u0 + u) % 2
                            for i, j in enumerate(_NA_J[t]):
                                mm(snA[ssl][:, i * 128:(i + 1) * 128], KaT[:, j * 128:(j + 1) * 128],
                                   Qz[hh][:, t * 128:(t + 1) * 128], True, True,
                                   [("KaT", j // 4), ("Qz", hh, t // 4), ("Qzpad", hh)], [("snA", ssl)])

                        qk_unit(0)
                        pend = []
                        for u, (t, hh) in enumerate(units):
                            js = _NA_J[t]
                            nb_ = len(js)
                            asl = t % 2
                            head = 2 * g + hh
                            ssl = (u0 + u) % 2
                            esl = (u0 + u) % 3
                            if u + 1 < len(units):
                                qk_unit(u + 1)
                            act(Ea[esl][:, 0:nb_ * 128], snA[ssl][:, 0:nb_ * 128], AF.Exp, [("snA", ssl)], [("Ea", esl)])
                            o0 = _NA_OFF[t]
                            tt("dve", Ea[esl][:, 0:nb_ * 128], Ea[esl][:, 0:nb_ * 128],
                               nab[:, head, o0:o0 + nb_, :].rearrange("p a q -> p (a q)"), ALU.mult,
                               [("Ea", esl), ("nab", head)], [("Ea", esl)])
                            for i, j in enumerate(js):
                                mm(accA[asl][:, hh * 65:(hh + 1) * 65], Ea[esl][:, i * 128:(i + 1) * 128],
                                   Va[:, j, hh, :], i == 0, i == nb_ - 1,
                                   [("Ea", esl), ("Va", j // 4), "Va1"], [("accA", asl)])
                            while pend:
                                pend.pop(0)()
                            if hh == 0:
                                continue
                            a3 = accA[asl][:, 0:130].rearrange("p (h e) -> p h e", h=2)
                            recip(rc[:], a3[:, :, 64], [("accA", asl)], ["rcA"])
                            for h2 in range(2):
                                ts("dve", oa[asl][:, h2 * 64:(h2 + 1) * 64], a3[:, h2, 0:64], rc[:, h2:h2 + 1], None,
                                   ALU.mult, None, [("accA", asl), "rcA"], [("oa", asl, h2)])

                            def epi(t=t, asl=asl):
                                tr(ptr[:, 0, :], oa[asl][:], ident[:], [("oa", asl, 0), ("oa", asl, 1), "ident"],
                                   ["ptrA"])
                                cp("act", oT[:, g, t * 128:(t + 1) * 128], ptr[:, 0, :], ["ptrA"], [("oT", g, t)])
                            pend.append(epi)
                        while pend:
                            pend.pop(0)()
                        ecnt += len(units)
                    P.barrier()
                    if stop == 2:
                        return nc

                with Scope() as L:
                    wg = [L.sb("wgB%d" % i, [128, 8, 3, 128], BF16) for i in range(2)]
                    QT = [L.sb("QT%d" % m, [128, S], BF16) for m in range(2)]
                    KT = [[L.sb("KT%d_%d" % (m, v), [128, S], BF16) for v in range(2)] for m in range(2)]
                    Vb = L.sb("Vb", [128, NT, 129], BF16)
                    Eb = [L.sb("Eb%d" % i, [128, 512], BF16) for i in range(3)]
                    om = [L.sb("om%d" % m, [128, 4, 128], F32) for m in range(2)]
                    df = L.sb("df", [128, 4, 128], F32)
                    junk = L.sb("junkB", [128, 128], BF16)
                    ssb = L.sb("ssb", [128, 4], F32)
                    rsb = L.sb("rsb", [128, 4], F32)
                    rcb = L.sb("rcb", [128, 4], F32)
                    ob = [L.sb("ob%d" % i, [128, 128], BF16) for i in range(4)]
                    sps = [L.ps("sps%d" % i, [128, 512], F32) for i in range(2)]
                    accb = [[L.ps("accb%d_%d" % (s_, i), [128, 512], F32) for i in range(2)] for s_ in range(2)]
                    prj = L.ps("prjB", [128, 512], F32)
                    ptr = L.ps("ptrB", [128, 8, 128], BF16)
                    memset("pool", Vb[:, :, 128:129], 1.0, ["Vb1"])
                    for m_ in range(2):
                        memset("pool", QT[m_][64:128, :], 0.0, [("QTa", m_)])
                        for v_ in range(2):
                            memset("pool", KT[m_][v_][64:128, :], 0.0, [("KTa", m_, v_)])
                    wiv = win_d.rearrange("(k p) n -> p k n", p=128)
                    ecnt = 0
                    acnt = 0
                    tcnt = 0
                    pendB = []
                    for h in range(4):
                        sl = h % 2
                        for tsel in range(3):
                            c0 = 1536 + tsel * 512 + h * 128
                            dma("pool", wg[sl][:, :, tsel, :], wiv[:, :, c0:c0 + 128], (), [("wgB", sl, tsel)],
                                "s_wgB%d_%d" % (sl, tsel))
                        import os
                        DBGQ = os.environ.get("DBGQ", "").split(",")
                        for m in range(2):
                            if "noaug" in DBGQ:
                                continue
                            dma("pool", QT[m][64:68, :], qaug_d[h], (), [("QTa", m)], "s_qa%d" % m)
                            for v in range(2):
                                dma("pool", KT[m][v][64:68, :], kaug_d[h, v], (), [("KTa", m, v)], "s_ka%d_%d" % (m, v))
                        for c in range(4):
                            if "noq" in DBGQ:
                                continue
                            for k in range(8):
                                mm(prj[:], wg[sl][:, k, 0, :], hT[:, k, c * 512:(c + 1) * 512], k == 0, k == 7,
                                   [("wgB", sl, 0)] + hkeys(k, c * 4, 4), ["prjB"])
                            act(QT[0][0:64, c * 512:(c + 1) * 512], prj[0:64, :], AF.Copy, ["prjB"], [("QT", 0, c)], scale=0.125)
                            act(QT[1][0:64, c * 512:(c + 1) * 512], prj[64:128, :], AF.Copy, ["prjB"], [("QT", 1, c)], scale=0.125)
                        for c in range(4):
                            if "nok" in DBGQ:
                                continue
                            for k in range(8):
                                mm(prj[:], wg[sl][:, k, 1, :], hT[:, k, c * 512:(c + 1) * 512], k == 0, k == 7,
                                   [("wgB", sl, 1)] + hkeys(k, c * 4, 4), ["prjB"])
                            cp("dve", KT[0][0][0:64, c * 512:(c + 1) * 512], prj[0:64, :], ["prjB"], [("KT", 0, 0, c)])
                            cp("dve", KT[1][0][0:64, c * 512:(c + 1) * 512], prj[64:128, :], ["prjB"], [("KT", 1, 0, c)])
                            for m_ in range(2):
                                cp("pool", KT[m_][1][0:64, c * 512:(c + 1) * 512], KT[m_][0][0:64, c * 512:(c + 1) * 512],
                                   [("KT", m_, 0, c)], [("KT", m_, 1, c)])
                        for c in range(4):
                            if "nov" in DBGQ:
                                continue
                            for tq in range(4):
                                t = c * 4 + tq
                                for k in range(8):
                                    mm(prj[:, tq * 128:(tq + 1) * 128], hT[:, k, t * 128:(t + 1) * 128], wg[sl][:, k, 2, :],
                                       k == 0, k == 7, [("wgB", sl, 2)] + hkeys(k, t, 1), ["prjB"])
                            cp("dve", Vb[:, c * 4:(c + 1) * 4, 0:128], prj[:].rearrange("p (t e) -> p t e", t=4),
                               ["prjB"], [("Vb", c)])
                        import os
                        DBGB = int(os.environ.get("DBGB", "9"))
                        for qc in range(4):
                            if DBGB < 2:
                                continue
                            for m in range(2):
                                aset = accb[acnt % 2]
                                akey = acnt % 2
                                acnt += 1

                                def qk(kt, ssl):
                                    o = sps[ssl]
                                    wk = [("sps", ssl)]

                                    def rng(v, c0, c1, aug=True):
                                        kk = 128 if aug else 64
                                        rd = [("KT", m, v, kt // 4), ("QT", m, qc)]
                                        if aug:
                                            rd += [("KTa", m, v), ("QTa", m)]
                                        return (o[:, c0:c1], KT[m][v][0:kk, kt * 128:(kt + 1) * 128],
                                                QT[m][0:kk, qc * 512 + c0: qc * 512 + c1], rd)
                                    if kt < 4 * qc:
                                        o_, l_, r_, rd = rng(0, 0, 512)
                                        mm(o_, l_, r_, True, True, rd, wk)
                                    elif kt >= 4 * qc + 4:
                                        o_, l_, r_, rd = rng(1, 0, 512)
                                        mm(o_, l_, r_, True, True, rd, wk)
                                    else:
                                        jd = kt - 4 * qc
                                        if jd > 0:
                                            o_, l_, r_, rd = rng(1, 0, jd * 128)
                                            mm(o_, l_, r_, True, True, rd, wk)
                                        o_, l_, r_, rd = rng(0, jd * 128, (jd + 1) * 128, aug=False)
                                        mm(o_, l_, r_, True, False, rd, wk)
                                        mm(o_, ident[:], bdg[:, h, :], False, True, ["ident", "bdg"], wk)
                                        if jd < 3:
                                            o_, l_, r_, rd = rng(0, (jd + 1) * 128, 512)
                                            mm(o_, l_, r_, True, True, rd, wk)

                                qk(0, ecnt % 2)
                                for kt in range(16):
                                    if kt == 2:
                                        while pendB:
                                            pendB.pop(0)()
                                    ssl = ecnt % 2
                                    esl = ecnt % 3
                                    ecnt += 1
                                    if kt + 1 < 16:
                                        qk(kt + 1, ecnt % 2)
                                    act(Eb[esl][:], sps[ssl][:], AF.Exp, [("sps", ssl)], [("Eb", esl)])
                                    for j in range(4):
                                        if DBGB < 3:
                                            continue
                                        mm(aset[j // 2][:, (j % 2) * 129:(j % 2) * 129 + 129], Eb[esl][:, j * 128:(j + 1) * 128],
                                           Vb[:, kt, :], kt == 0 and j % 2 == 0, kt == 15,
                                           [("Eb", esl), ("Vb", kt // 4), "Vb1"], [("accb", akey, j // 2)], skip=True)
                                for j in range(4):
                                    if DBGB < 4:
                                        continue
                                    a_ = aset[j // 2]
                                    c0 = (j % 2) * 129
                                    recip(rcb[:, j:j + 1], a_[:, c0 + 128:c0 + 129], [("accb", akey, j // 2)], [("rcb", j)])
                                    ts("dve", om[m][:, j, :], a_[:, c0:c0 + 128], rcb[:, j:j + 1], None, ALU.mult, None,
                                       [("accb", akey, j // 2), ("rcb", j)], [("om", m, j)])
                            if DBGB < 5:
                                continue
                            stt("dve", df[:], om[1][:], neglam[:, 0:1], om[0][:], ALU.mult, ALU.add,
                                [("om", 0, j) for j in range(4)] + [("om", 1, j) for j in range(4)] + ["neglam"], ["df"])
                            memset("dve", ssb[:], 0.0, ["ssb"])
                            for j in range(4):
                                act(junk[:], df[:, j, :], AF.Square, ["df", "ssb"], ["junkB", ("ssbd", j)], accum=ssb[:, j:j + 1])
                            ts("dve", rsb[:], ssb[:], 1.0 / 128, EPS, ALU.mult, ALU.add, [("ssbd", j) for j in range(4)], ["rsb"])
                            act(rsb[:], rsb[:], AF.Sqrt, ["rsb"], ["rsb"])
                            recip(rsb[:], rsb[:], ["rsb"], ["rsb"])
                            for j in range(4):
                                stt("dve", ob[j][:], df[:, j, :], rsb[:, j:j + 1], subw[:], ALU.mult, ALU.mult,
                                    ["df", "rsb", "subw"], [("ob", j)])

                            def epiB(h=h, qc=qc):
                                for j in range(4):
                                    tr(ptr[:, j, :], ob[j][:], ident[:], [("ob", j), "ident"], ["ptrB"])
                                cp("act", oT[:, 4 + h, qc * 512:(qc + 1) * 512], ptr[:, 0:4, :].rearrange("p a q -> p (a q)"),
                                   ["ptrB"], [("oT", 4 + h, qc * 4 + j) for j in range(4)])
                            pendB.append(epiB)
                    while pendB:
                        pendB.pop(0)()
                    P.barrier()
                    if stop == 3:
                        return nc

            with Scope() as R:
                x1 = R.sb("x1", [128, NT, D], F32)
                with Scope() as L:
                    wo = L.sb("wo", [128, 8, D], BF16)
                    wst = [L.sb("wst%d" % i, [128, D], F32) for i in range(2)]
                    xt = [L.sb("xtC%d" % i, [128, D], F32) for i in range(2)]
                    pc = [[L.ps("pc%d_%d" % (i, hf), [128, 512], F32) for hf in range(2)] for i in range(2)]
                    gbc, gkeys = build_bc(L, lambda k: modT[:, 16 + k, b:b + 1], [("modT", b)], "C")
                    for k in range(8):
                        sl = k % 2
                        dma("sp", wst[sl][:], wout_d[k * 128:(k + 1) * 128, :], (), [("wst", sl)], "s_wst%d" % sl)
                        tt("pool" if k % 3 == 2 else "dve", wo[:, k, :], wst[sl][:], gbc[:], ALU.mult,
                           [("wst", sl)] + gkeys, [("wo", k)])
                    for t in range(NT):
                        sl = t % 2
                        dma("sp", xt[sl][:], x_d[b, t * 128:(t + 1) * 128, :], (), [("xtC", sl)], "s_xtC%d" % sl)
                        for hf in range(2):
                            for k in range(8):
                                mm(pc[sl][hf][:], oT[:, k, t * 128:(t + 1) * 128], wo[:, k, hf * 512:(hf + 1) * 512],
                                   k == 0, k == 7, [("oT", k, t), ("wo", k)], [("pc", sl, hf)])
                            tt("dve", x1[:, t, hf * 512:(hf + 1) * 512], pc[sl][hf][:], xt[sl][:, hf * 512:(hf + 1) * 512],
                               ALU.add, [("pc", sl, hf), ("xtC", sl)], [("x1", t, hf)])
                    P.barrier()
                    if stop == 4:
                        return nc

                wguv = wgu_d.rearrange("(k p) n -> p k n", p=128)
                for half in range(2):
                    with Scope() as L:
                        actT = L.sb("actT", [128, NFC, 1024], BF16)
                        with Scope() as L2:
                            h2T = L2.sb("h2T", [128, 8, 1024], BF16)

                            def srcD(t):
                                return x1[:, t, :], [("x1", t, 0), ("x1", t, 1)]

                            with Scope() as L3:
                                abc = build_bc(L3, lambda k: aF[:, k, b:b + 1], [("aF", b)], "D")
                                sbc = build_bc(L3, lambda k: modT[:, 24 + k, b:b + 1], [("modT", b)], "Ds")
                                norm_phase(L3, srcD, h2T, abc, sbc, list(range(half * 8, half * 8 + 8)), 0, "D")
                                P.barrier()
                            h2keys = lambda k, t0, n: [("D", "T", k, t) for t in range(t0, t0 + n)]
                            with Scope() as L3:
                                wgu = [L3.sb("wgu%d" % i, [128, 8, 2, 128], BF16) for i in range(3)]
                                gs = [L3.sb("gs%d" % i, [128, 512], F32) for i in range(2)]
                                pg = [L3.ps("pg%d" % i, [128, 512], F32) for i in range(2)]
                                pu = [L3.ps("pu%d" % i, [128, 512], F32) for i in range(2)]
                                cnt = 0
                                for fc in range(NFC):
                                    sl = fc % 3
                                    dma("pool", wgu[sl][:, :, 0, :], wguv[:, :, fc * 128:(fc + 1) * 128], (), [("wgu", sl, 0)],
                                        "s_wgu%d_0" % sl)
                                    dma("pool", wgu[sl][:, :, 1, :], wguv[:, :, DFF + fc * 128:DFF + (fc + 1) * 128], (),
                                        [("wgu", sl, 1)], "s_wgu%d_1" % sl)
                                    for c in range(2):
                                        ps_ = cnt % 2
                                        cnt += 1
                                        for k in range(8):
                                            mm(pg[ps_][:], wgu[sl][:, k, 0, :], h2T[:, k, c * 512:(c + 1) * 512], k == 0, k == 7,
                                               [("wgu", sl, 0)] + h2keys(k, c * 4, 4), [("pg", ps_)])
                                        for k in range(8):
                                            mm(pu[ps_][:], wgu[sl][:, k, 1, :], h2T[:, k, c * 512:(c + 1) * 512], k == 0, k == 7,
                                               [("wgu", sl, 1)] + h2keys(k, c * 4, 4), [("pu", ps_)])
                                        act(gs[ps_][:], pg[ps_][:], AF.Silu, [("pg", ps_)], [("gs", ps_)])
                                        tt("dve", actT[:, fc, c * 512:(c + 1) * 512], pu[ps_][:], gs[ps_][:], ALU.mult,
                                           [("pu", ps_), ("gs", ps_)], [("actT", fc, c)])
                                P.barrier()
                        with Scope() as L2:
                            wd = L2.sb("wd", [128, NFC, 512], BF16)
                            wds = [L2.sb("wds%d" % i, [128, 512], F32) for i in range(2)]
                            pd = [L2.ps("pd%d" % i, [128, 512], F32) for i in range(4)]
                            ot = [L2.sb("ot%d" % i, [128, D], F32) for i in range(2)]
                            junk = L2.sb("junkF", [128, D], BF16)
                            ssf = L2.sb("ssf", [128, 8], F32)
                            rsf = L2.sb("rsf", [128, 8], F32)
                            cnt = 0
                            gbc, gkeys = build_bc(L2, lambda k: modT[:, 40 + k, b:b + 1], [("modT", b)], "F")
                            for dh in range(2):
                                for fc in range(NFC):
                                    sl = (dh * NFC + fc) % 2
                                    dma("sp", wds[sl][:], wdn_d[fc * 128:(fc + 1) * 128, dh * 512:(dh + 1) * 512], (),
                                        [("wds", sl)], "s_wds%d" % sl)
                                    tt("pool" if fc % 3 == 2 else "dve", wd[:, fc, :], wds[sl][:], gbc[:, dh * 512:(dh + 1) * 512], ALU.mult,
                                       [("wds", sl)] + gkeys, [("wd", fc)])
                                for ti in range(8):
                                    t = half * 8 + ti
                                    ps_ = cnt % 4
                                    cnt += 1
                                    for fc in range(NFC):
                                        mm(pd[ps_][:], actT[:, fc, ti * 128:(ti + 1) * 128], wd[:, fc, :], fc == 0, fc == NFC - 1,
                                           [("actT", fc, ti // 4), ("wd", fc)], [("pd", ps_)])
                                    tt("dve", x1[:, t, dh * 512:(dh + 1) * 512], pd[ps_][:], x1[:, t, dh * 512:(dh + 1) * 512],
                                       ALU.add, [("pd", ps_), ("x1", t, dh)], [("x1", t, dh)])
                            memset("dve", ssf[:], 0.0, ["ssf"])
                            for ti in range(8):
                                t = half * 8 + ti
                                sl = ti % 2
                                act(junk[:], x1[:, t, :], AF.Square, [("x1", t, 0), ("x1", t, 1), "ssf"],
                                    ["junkF", ("ssfd", ti)], accum=ssf[:, ti:ti + 1])
                                ts("dve", rsf[:, ti:ti + 1], ssf[:, ti:ti + 1], 1.0 / D, EPS, ALU.mult, ALU.add,
                                   [("ssfd", ti)], [("rsf", ti)])
                                act(rsf[:, ti:ti + 1], rsf[:, ti:ti + 1], AF.Sqrt, [("rsf", ti)], [("rsf", ti)])
                                recip(rsf[:, ti:ti + 1], rsf[:, ti:ti + 1], [("rsf", ti)], [("rsf", ti)])
                                stt("dve", ot[sl][:], x1[:, t, :], rsf[:, ti:ti + 1], gfbc[:], ALU.mult, ALU.mult,
                                    [("x1", t, 0), ("x1", t, 1), ("rsf", ti), "gfbc"], [("ot", sl)])
                                dma("sp", out_d[b, t * 128:(t + 1) * 128, :], ot[sl][:], [("ot", sl)], (),
                                    "s_out%d" % sl, final=True)
                            P.barrier(last=(b == NB - 1 and half == 1))
    return nc


_NC_CACHE = {}


def kernel(x, c, w_ada, b_ada, g_mix, w_in, rpb, lambda_q1, lambda_k1, lambda_q2, lambda_k2,
           subln_w, w_out, g_ffn, w_gate_up, w_down, g_final):
    f32 = np.float32
    x = np.asarray(x, f32)
    c = np.asarray(c, f32)
    if "nc" not in _NC_CACHE:
        _NC_CACHE["nc"] = build_nc()
    nc = _NC_CACHE["nc"]
    qaug, kaug, bdiag, mask = _host_consts()
    rpb0 = np.asarray(rpb, f32)[0]
    nabg = np.zeros((8, 128, NPAT, 128), f32)
    for p, pat in enumerate(_NA_BLKPAT):
        dr, dc, valid = _NA_PATS[pat]
        g_ = rpb0[:, dr, dc]
        nabg[:, :, p, :] = np.where(valid[None], g_, f32(0))
    nabg = nabg.reshape(8, 128, NPAT * 128)

    def colT(v, n):
        return np.ascontiguousarray(np.asarray(v, f32).reshape(n, 128).T)

    shared = {
        "w_ada": np.ascontiguousarray(np.asarray(w_ada, f32)[0]),
        "b_adaT": colT(np.asarray(b_ada, f32)[0], 48),
        "g_mixT": colT(np.asarray(g_mix, f32)[0], 8),
        "g_ffnT": colT(np.asarray(g_ffn, f32)[0], 8),
        "w_in": np.ascontiguousarray(np.asarray(w_in, f32)[0]),
        "w_out": np.ascontiguousarray(np.asarray(w_out, f32)[0]),
        "w_gate_up": np.ascontiguousarray(np.asarray(w_gate_up, f32)[0]),
        "w_down": np.ascontiguousarray(np.asarray(w_down, f32)[0]),
        "g_final": np.ascontiguousarray(np.asarray(g_final, f32)),
        "subln_w": np.ascontiguousarray(np.asarray(subln_w, f32)[0]),
        "lams": np.ascontiguousarray(np.stack([np.asarray(v, f32)[0] for v in
                                               (lambda_q1, lambda_k1, lambda_q2, lambda_k2)])),
        "nab_g": nabg,
        "na_mask": np.ascontiguousarray(mask.reshape(128, NPAT * 128)),
        "qaug": qaug, "kaug": kaug,
        "bdiag": np.ascontiguousarray(bdiag.reshape(128, 512)),
    }
    in_maps = []
    for i in range(8):
        cc = c[2 * i:2 * i + 2]
        cT = np.ascontiguousarray(cc.reshape(2, 8, 128).transpose(2, 1, 0).reshape(128, 16))
        m = dict(shared)
        m["x"] = np.ascontiguousarray(x[2 * i:2 * i + 2])
        m["cT"] = cT
        in_maps.append(m)
    res = run_bass_kernel_spmd(nc, in_maps, core_ids=list(range(8)))
    return np.concatenate([np.asarray(r["out"], f32) for r in res.results], axis=0)
```

```python
import contextlib
import math
import numpy as np
import concourse.bass as bass
import concourse.mybir as mybir
from concourse.bass_utils import run_bass_kernel_spmd

F32 = mybir.dt.float32
BF16 = mybir.dt.bfloat16
AF = mybir.ActivationFunctionType
ALU = mybir.AluOpType

D = 1024
S = 2048
NT = 16
NB = 2
DFF = 2816
NFC = 22
EPS = 1e-6
LAMBDA_INIT = 0.2
NEG = -30000.0
SAME_ENGINE_SYNC = True


class Prog:
    ENGINES = ("pe", "act", "dve", "pool", "sp")

    def __init__(self, nc, semstack):
        self.nc = nc
        self.semstack = semstack
        self.ops = []
        self.last_writer = {}
        self.readers = {}
        self.bar_deps = set()
        self.last_on_eng = {}
        self.dma_since_bar = []
        self.flushed = 0
        self.sems = {}
        self.counts = {}
        self.known = {e: {} for e in self.ENGINES}
        self.final_ops = []
        self.excl = set()

    def op(self, eng, fn, reads=(), writes=(), dma=None, final=False):
        oid = len(self.ops)
        deps = set(self.bar_deps)
        for k in reads:
            w = self.last_writer.get(k)
            if w is not None:
                deps.add(w)
            if k in self.excl:
                for r in self.readers.get(k, ()):
                    if self.ops[r]["eng"] != eng:
                        deps.add(r)
        for k in writes:
            w = self.last_writer.get(k)
            if w is not None:
                deps.add(w)
            for r in self.readers.get(k, ()):
                deps.add(r)
        for k in reads:
            self.readers.setdefault(k, []).append(oid)
        for k in writes:
            self.last_writer[k] = oid
            self.readers[k] = []
        self.ops.append(dict(eng=eng, fn=fn, deps=deps, dma=dma, needed=final, ev=None))
        self.last_on_eng[eng] = oid
        if dma is not None:
            self.dma_since_bar.append(oid)
        if final:
            self.final_ops.append(oid)
        return oid

    def _skip(self, do, ename):
        return do["dma"] is None and do["eng"] == ename and (ename == "pe" or not SAME_ENGINE_SYNC)

    def _sem(self, sn):
        if sn not in self.sems:
            self.sems[sn] = self.semstack.enter_context(self.nc.semaphore("sm%d" % len(self.sems)))
            self.counts[sn] = 0
        return self.sems[sn]

    def barrier(self, last=False):
        self.bar_deps = set(self.last_on_eng.values()) | set(self.dma_since_bar)
        self.dma_since_bar = []
        self._flush(last)

    def _flush(self, last):
        nc = self.nc
        ops = self.ops
        lo, hi = self.flushed, len(ops)
        self.flushed = hi
        for i in range(lo, hi):
            o = ops[i]
            for d in o["deps"]:
                if d >= lo and not self._skip(ops[d], o["eng"]):
                    ops[d]["needed"] = True
        for d in self.bar_deps:
            if d >= lo:
                ops[d]["needed"] = True
        for e in self.ENGINES:
            self._sem(("eng", e))
        for i in range(lo, hi):
            o = ops[i]
            if not o["needed"]:
                continue
            sn = ("dma", o["dma"]) if o["dma"] is not None else ("eng", o["eng"])
            self._sem(sn)
            self.counts[sn] += 16 if o["dma"] is not None else 1
            o["ev"] = (sn, self.counts[sn])
        by_eng = {e: [] for e in self.ENGINES}
        for i in range(lo, hi):
            by_eng[ops[i]["eng"]].append(i)

        def run_engine(ename, eh):
            known = self.known[ename]
            for i in by_eng[ename]:
                o = ops[i]
                need = {}
                for d in o["deps"]:
                    do = ops[d]
                    if do["ev"] is None or self._skip(do, ename):
                        continue
                    sn, v = do["ev"]
                    if need.get(sn, 0) < v:
                        need[sn] = v
                for sn, v in need.items():
                    if known.get(sn, 0) >= v:
                        continue
                    eh.wait_ge(self.sems[sn], v)
                    known[sn] = v
                ins = o["fn"](eh)
                if o["ev"] is not None:
                    ins.then_inc(self.sems[o["ev"][0]], 16 if o["dma"] is not None else 1)
                o["fn"] = None
            if last and ename == "sp":
                for d in self.final_ops:
                    sn, v = ops[d]["ev"]
                    if known.get(sn, 0) < v:
                        eh.wait_ge(self.sems[sn], v)
                        known[sn] = v

        with nc.Block() as block:
            @block.tensor
            def _(e):
                run_engine("pe", e)

            @block.scalar
            def _(e):
                run_engine("act", e)

            @block.vector
            def _(e):
                run_engine("dve", e)

            @block.gpsimd
            def _(e):
                run_engine("pool", e)

            @block.sync
            def _(e):
                run_engine("sp", e)


def _na_tables():
    rows, W, wh, ww = 32, 64, 8, 16
    pats = []
    pat_key = {}
    J = []
    pid = {}
    kk = np.arange(128)
    kr, ck = kk // 64, kk % 64
    for t in range(16):
        r_q = 2 * t + kr
        cq = ck
        rs = np.clip(r_q - wh // 2, 0, rows - wh)
        cs = np.clip(cq - ww // 2, 0, W - ww)
        jlo = int(rs.min()) // 2
        jhi = int(rs.max() + wh - 1) // 2
        js = list(range(jlo, jhi + 1))
        J.append(js)
        for j in js:
            krow = 2 * j + kr
            valid_r = (krow[:, None] >= rs[None, :]) & (krow[:, None] < rs[None, :] + wh)
            dr = krow[:, None] - r_q[None, :] + 7
            valid_c = (ck[:, None] >= cs[None, :]) & (ck[:, None] < cs[None, :] + ww)
            dc = np.clip(ck[:, None] - cq[None, :] + ww - 1, 0, 2 * ww - 2)
            valid = valid_r & valid_c
            drc = np.where(valid, dr, 0)
            key = (drc.tobytes(), dc.tobytes(), valid.tobytes())
            if key not in pat_key:
                pat_key[key] = len(pats)
                pats.append((drc.astype(np.int64), dc.astype(np.int64), valid))
            pid[(t, j)] = pat_key[key]
    return J, pid, pats


_NA_J, _NA_PID, _NA_PATS = _na_tables()


def _na_blocks():
    blk_pat = []
    off = {}
    ref = [_NA_PID[(2, j)] for j in _NA_J[2]]
    for t in range(2, 14):
        assert [_NA_PID[(t, j)] for j in _NA_J[t]] == ref
    blk_pat += ref
    for t in range(2, 14):
        off[t] = 0
    for t in (0, 1, 14, 15):
        off[t] = len(blk_pat)
        blk_pat += [_NA_PID[(t, j)] for j in _NA_J[t]]
    return blk_pat, off


_NA_BLKPAT, _NA_OFF = _na_blocks()
NPAT = len(_NA_BLKPAT)


def _host_consts():
    slopes = [2.0 ** (-8.0 * (i + 1) / 4) for i in range(4)]
    tok = np.arange(S)
    qaug = np.zeros((4, 4, S), np.float32)
    kaug = np.zeros((4, 2, 4, S), np.float32)
    bdiag = np.zeros((128, 4, 128), np.float32)
    for h, sl in enumerate(slopes):
        qaug[h, 0] = sl * (tok % 256)
        qaug[h, 1] = sl * (tok - tok % 256)
        qaug[h, 2] = 1.0
        qaug[h, 3] = 1.0
        kaug[h, 0, 0] = -1.0
        kaug[h, 0, 1] = -1.0
        kaug[h, 0, 2] = sl * (tok % 128)
        kaug[h, 0, 3] = sl * (tok - tok % 128)
        kaug[h, 1] = -kaug[h, 0]
        i = np.arange(128)
        bdiag[:, h, :] = -sl * np.abs(i[:, None] - i[None, :])
    mask = np.zeros((128, NPAT, 128), np.float32)
    for p, pat in enumerate(_NA_BLKPAT):
        mask[:, p, :] = np.where(_NA_PATS[pat][2], 0.0, NEG)
    return qaug, kaug, bdiag, mask


def build_nc(stop=None):
    nc = bass.Bass("TRN2", target_bir_lowering=False)

    def din(name, shape):
        return nc.dram_tensor(name, list(shape), F32, kind="ExternalInput").ap()

    x_d = din("x", [NB, S, D])
    cT_d = din("cT", [128, 16])
    wada_d = din("w_ada", [D, 6 * D])
    badaT_d = din("b_adaT", [128, 48])
    gmixT_d = din("g_mixT", [128, 8])
    gffnT_d = din("g_ffnT", [128, 8])
    win_d = din("w_in", [D, 3 * D])
    wout_d = din("w_out", [D, D])
    wgu_d = din("w_gate_up", [D, 2 * DFF])
    wdn_d = din("w_down", [DFF, D])
    gfin_d = din("g_final", [D])
    subw_d = din("subln_w", [128])
    lam_d = din("lams", [4, 64])
    nabg_d = din("nab_g", [8, 128, NPAT * 128])
    mask_d = din("na_mask", [128, NPAT * 128])
    qaug_d = din("qaug", [4, 4, S])
    kaug_d = din("kaug", [4, 2, 4, S])
    bdiag_d = din("bdiag", [128, 4 * 128])
    out_d = nc.dram_tensor("out", [NB, S, D], F32, kind="ExternalOutput").ap()

    semstack = contextlib.ExitStack()
    P = Prog(nc, semstack)
    _uid = [0]
    for _k in ["prjA", "prjB", "ptrA", "ptrB", "pmod"]:
        P.excl.add(_k)
    for _i in range(4):
        for _j in range(4):
            P.excl.update([("snA", _i), ("accA", _i), ("sps", _i), ("accb", _i, _j), ("pc", _i, _j), ("pg", _i), ("pu", _i),
                           ("pd", _i), ("ptr", "A", _i), ("ptr", "D", _i)])

    class Scope:
        def __init__(self):
            self.st = contextlib.ExitStack()

        def __enter__(self):
            self.st.__enter__()
            return self

        def __exit__(self, *a):
            return self.st.__exit__(*a)

        def sb(self, name, shape, dt):
            _uid[0] += 1
            return self.st.enter_context(nc.sbuf_tensor("%s_%d" % (name, _uid[0]), list(shape), dt))

        def ps(self, name, shape, dt):
            _uid[0] += 1
            return self.st.enter_context(nc.psum_tensor("%s_%d" % (name, _uid[0]), list(shape), dt))

    def mm(out, lhsT, rhs, start, stop, reads, writes, skip=False):
        if skip:
            P.op("pe", lambda e: e.matmul(out, lhsT=lhsT, rhs=rhs, start=start, stop=stop,
                                          skip_group_check=True), reads, writes)
        else:
            P.op("pe", lambda e: e.matmul(out, lhsT=lhsT, rhs=rhs, start=start, stop=stop), reads, writes)

    def tr(out, in_, ident, reads, writes):
        P.op("pe", lambda e: e.transpose(out=out, in_=in_, identity=ident), reads, writes)

    def act(out, in_, func, reads, writes, scale=None, bias=None, accum=None):
        kw = {}
        if scale is not None:
            kw["scale"] = scale
        if bias is not None:
            kw["bias"] = bias
        if accum is not None:
            kw["accum_out"] = accum
        P.op("act", lambda e: e.activation(out=out, in_=in_, func=func, **kw), reads, writes)

    def ts(eng, out, in0, s1, s2, op0, op1, reads, writes):
        if op1 is None:
            P.op(eng, lambda e: e.tensor_scalar(out=out, in0=in0, scalar1=s1, scalar2=None, op0=op0), reads, writes)
        else:
            P.op(eng, lambda e: e.tensor_scalar(out=out, in0=in0, scalar1=s1, scalar2=s2, op0=op0, op1=op1), reads, writes)

    def tt(eng, out, in0, in1, op, reads, writes):
        P.op(eng, lambda e: e.tensor_tensor(out=out, in0=in0, in1=in1, op=op), reads, writes)

    def stt(eng, out, in0, scalar, in1, op0, op1, reads, writes):
        P.op(eng, lambda e: e.scalar_tensor_tensor(out=out, in0=in0, scalar=scalar, in1=in1, op0=op0, op1=op1), reads, writes)

    def cp(eng, out, in_, reads, writes):
        if eng == "act":
            P.op(eng, lambda e: e.activation(out=out, in_=in_, func=AF.Copy), reads, writes)
        else:
            P.op(eng, lambda e: e.tensor_copy(out=out, in_=in_), reads, writes)

    def recip(out, in_, reads, writes):
        P.op("dve", lambda e: e.reciprocal(out=out, in_=in_), reads, writes)

    def memset(eng, ap, val, writes):
        P.op(eng, lambda e: e.memset(ap, val), (), writes)

    def dma(eng, out, in_, reads, writes, key, final=False):
        return P.op(eng, lambda e: e.dma_start(out=out, in_=in_), reads, writes, dma=key, final=final)

    with semstack, Scope() as G:
        identf = G.sb("identf", [128, 128], F32)
        ident = G.sb("ident", [128, 128], BF16)
        onesf = G.sb("onesf", [128, 128], F32)
        modT = G.sb("modT", [128, 48, 2], F32)
        aM = G.sb("aM", [128, 8, 2], F32)
        aF = G.sb("aF", [128, 8, 2], F32)
        gfbc = G.sb("gfbc", [128, D], F32)
        subw = G.sb("subw", [128, 128], F32)
        neglam = G.sb("neglam", [128, 1], F32)
        bdg = G.sb("bdg", [128, 4, 128], BF16)
        oT = G.sb("oT", [128, 8, S], BF16)

        def build_bc(L, colfn, rkeys, tag):
            bc = L.sb("bc" + tag, [128, D], F32)
            dg = [L.sb("dg%s%d" % (tag, i), [128, 128], F32) for i in range(2)]
            pb = [L.ps("pb%s%d" % (tag, i), [128, 512], F32) for i in range(2)]
            for half in range(2):
                for kk in range(4):
                    k = half * 4 + kk
                    sl = k % 2
                    ts("dve", dg[sl][:], identf[:], colfn(k), None, ALU.mult, None, ["identf"] + rkeys, [("dg", tag, sl)])
                    mm(pb[half][:, kk * 128:(kk + 1) * 128], onesf[:], dg[sl][:], True, True,
                       ["onesf", ("dg", tag, sl)], [("pb", tag, half)])
                cp("act", bc[:, half * 512:(half + 1) * 512], pb[half][:], [("pb", tag, half)], [("bc", tag, half)])
            return bc, [("bc", tag, 0), ("bc", tag, 1)]

        with Scope() as L:
            cT = L.sb("cT", [128, 16], F32)
            cS = L.sb("cS", [128, 16], BF16)
            badaT = L.sb("badaT", [128, 48], F32)
            gmT = L.sb("gmT", [128, 8], F32)
            gfT = L.sb("gfT", [128, 8], F32)
            lamv = L.sb("lamv", [128, 4, 64], F32)
            lamp = L.sb("lamp", [128, 2, 64], F32)
            lams = L.sb("lams_s", [128, 2], F32)
            wad = [L.sb("wad%d" % i, [128, 8, 1024], BF16) for i in range(2)]
            pmod = L.ps("pmod", [128, 512], F32)

            memset("pool", identf[:], 0.0, ["identf"])
            P.op("pool", lambda e: e.affine_select(out=identf[:], in_=identf[:], pattern=[[-1, 128]],
                                                   compare_op=ALU.not_equal, fill=1.0, base=0,
                                                   channel_multiplier=1), ["identf"], ["identf"])
            cp("dve", ident[:], identf[:], ["identf"], ["ident"])
            memset("pool", onesf[:], 1.0, ["onesf"])
            dma("sp", cT[:], cT_d, (), ["cT"], "s_cT")
            dma("sp", badaT[:], badaT_d, (), ["badaT"], "s_bada")
            dma("sp", gmT[:], gmixT_d, (), ["gmT"], "s_gm")
            dma("sp", gfT[:], gffnT_d, (), ["gfT"], "s_gf")
            dma("sp", gfbc[:], gfin_d.partition_broadcast(128), (), ["gfbc"], "s_gfbc")
            dma("sp", subw[:], subw_d.partition_broadcast(128), (), ["subw"], "s_subw")
            for i in range(4):
                dma("sp", lamv[:, i, :], lam_d[i].partition_broadcast(128), (), [("lamv", i)], "s_lam%d" % i)
            dma("pool", bdg[:].rearrange("p h q -> p (h q)"), bdiag_d, (), ["bdg"], "s_bdg")
            ts("dve", subw[:], subw[:], 1.0 - LAMBDA_INIT, None, ALU.mult, None, ["subw"], ["subw"])
            tt("dve", lamp[:, 0, :], lamv[:, 0, :], lamv[:, 1, :], ALU.mult, [("lamv", 0), ("lamv", 1)], [("lamp", 0)])
            tt("dve", lamp[:, 1, :], lamv[:, 2, :], lamv[:, 3, :], ALU.mult, [("lamv", 2), ("lamv", 3)], [("lamp", 1)])
            P.op("dve", lambda e: e.reduce_sum(out=lams[:, 0:1], in_=lamp[:, 0, :], axis=mybir.AxisListType.X),
                 [("lamp", 0)], [("lams", 0)])
            P.op("dve", lambda e: e.reduce_sum(out=lams[:, 1:2], in_=lamp[:, 1, :], axis=mybir.AxisListType.X),
                 [("lamp", 1)], [("lams", 1)])
            act(lams[:], lams[:], AF.Exp, [("lams", 0), ("lams", 1)], ["lamse"])
            tt("dve", neglam[:], lams[:, 1:2], lams[:, 0:1], ALU.subtract, ["lamse"], ["neglam"])
            ts("dve", neglam[:], neglam[:], -LAMBDA_INIT, None, ALU.add, None, ["neglam"], ["neglam"])
            act(cS[:], cT[:], AF.Silu, ["cT"], ["cS"])
            wv = wada_d.rearrange("(k p) n -> p k n", p=128)
            for pc in range(6):
                sl = pc % 2
                dma("pool", wad[sl][:], wv[:, :, pc * 1024:(pc + 1) * 1024], (), [("wad", sl)], "s_wad%d" % sl)
                for jj in range(8):
                    j = pc * 8 + jj
                    for k in range(8):
                        mm(pmod[:, 2 * j:2 * j + 2], wad[sl][:, k, jj * 128:(jj + 1) * 128],
                           cS[:, 2 * k:2 * k + 2], k == 0, k == 7, [("wad", sl), "cS"], ["pmod"])
            pm3 = pmod[:, 0:96].rearrange("p (j b) -> p j b", b=2)
            for b in range(2):
                tt("dve", modT[:, :, b], pm3[:, :, b], badaT[:], ALU.add, ["pmod", "badaT"], [("modT", b)])
            for b in range(2):
                stt("dve", aM[:, :, b], modT[:, 8:16, b], 1.0, gmT[:], ALU.add, ALU.mult, [("modT", b), "gmT"], [("aM", b)])
                stt("dve", aF[:, :, b], modT[:, 32:40, b], 1.0, gfT[:], ALU.add, ALU.mult, [("modT", b), "gfT"], [("aF", b)])
            P.barrier()
            if stop == 0:
                return nc

        for b in range(NB):
            with Scope() as M:
                hT = M.sb("hT", [128, 8, S], BF16)
                nab = M.sb("nab", [128, 8, NPAT, 128], BF16)

                def norm_phase(L, src_tile_fn, dstT, a_sc, sh_sc, tiles, tok0, tagp):
                    ss = L.sb("ss" + tagp, [128, 16], F32)
                    rs = L.sb("rs" + tagp, [128, 16], F32)
                    junk = L.sb("junk" + tagp, [128, D], BF16)
                    xn = [L.sb("xn%s%d" % (tagp, i), [128, D], BF16) for i in range(2)]
                    xf = [L.sb("xf%s%d" % (tagp, i), [128, D], F32) for i in range(2)]
                    ptr = [L.ps("ptr%s%d" % (tagp, i), [128, 8, 128], BF16) for i in range(2)]
                    memset("dve", ss[:], 0.0, [("ss", tagp)])
                    import os
                    DBG = int(os.environ.get("DBGA", "9"))
                    for ti, t in enumerate(tiles):
                        src, rk = src_tile_fn(t)
                        sl = ti % 2
                        act(junk[:], src, AF.Square, rk + [("ss", tagp)], [("junk", tagp), ("ssd", tagp, ti)],
                            accum=ss[:, ti:ti + 1])
                        ts("dve", rs[:, ti:ti + 1], ss[:, ti:ti + 1], 1.0 / D, EPS, ALU.mult, ALU.add,
                           [("ssd", tagp, ti)], [("rs", tagp, ti)])
                        act(rs[:, ti:ti + 1], rs[:, ti:ti + 1], AF.Sqrt, [("rs", tagp, ti)], [("rs", tagp, ti)])
                        recip(rs[:, ti:ti + 1], rs[:, ti:ti + 1], [("rs", tagp, ti)], [("rs", tagp, ti)])
                        stt("dve", xf[sl][:], src, rs[:, ti:ti + 1], a_sc[0][:], ALU.mult, ALU.mult,
                            rk + [("rs", tagp, ti)] + a_sc[1], [("xf", tagp, sl)])
                        tt("pool" if ti % 2 == 0 else "dve", xn[sl][:], xf[sl][:], sh_sc[0][:], ALU.add, [("xf", tagp, sl)] + sh_sc[1], [("xn", tagp, sl)])
                        for k in range(8):
                            tr(ptr[sl][:, k, :], xn[sl][:, k * 128:(k + 1) * 128], ident[:],
                               [("xn", tagp, sl), "ident"], [("ptr", tagp, sl)])
                        dst = dstT[:, :, tok0 + ti * 128: tok0 + (ti + 1) * 128]
                        wk = [(tagp, "T", k, tok0 // 128 + ti) for k in range(8)]
                        cp("act" if ti % 2 == 0 else "dve", dst, ptr[sl][:], [("ptr", tagp, sl)], wk)

                with Scope() as L:
                    xt = [L.sb("xtA%d" % i, [128, D], F32) for i in range(2)]
                    msk = L.sb("msk", [128, NPAT * 128], F32)
                    stg = [L.sb("stg%d" % i, [128, NPAT * 128], F32) for i in range(2)]
                    dma("sp", msk[:], mask_d, (), ["msk"], "s_msk")
                    for h in range(8):
                        sl = h % 2
                        dma("sp", stg[sl][:], nabg_d[h], (), [("stg", sl)], "s_stg%d" % sl)
                        tt("pool", nab[:, h, :, :].rearrange("p a q -> p (a q)"), stg[sl][:], msk[:], ALU.add,
                           [("stg", sl), "msk"], [("nab", h)])
                        act(nab[:, h, :, :].rearrange("p a q -> p (a q)"), nab[:, h, :, :].rearrange("p a q -> p (a q)"),
                            AF.Exp, [("nab", h)], [("nab", h)])

                    def srcA(t):
                        sl = t % 2
                        dma("sp", xt[sl][:], x_d[b, t * 128:(t + 1) * 128, :], (), [("xtA", sl)], "s_xtA%d" % sl)
                        return xt[sl][:], [("xtA", sl)]

                    abc = build_bc(L, lambda k: aM[:, k, b:b + 1], [("aM", b)], "A")
                    sbc = build_bc(L, lambda k: modT[:, 0 + k, b:b + 1], [("modT", b)], "As")
                    norm_phase(L, srcA, hT, abc, sbc, list(range(NT)), 0, "A")
                    P.barrier()
                    if stop == 1:
                        return nc
                hkeys = lambda k, t0, n: [("A", "T", k, t) for t in range(t0, t0 + n)]

                with Scope() as L:
                    wg = [L.sb("wgA%d" % i, [128, 8, 3, 128], BF16) for i in range(2)]
                    Qz = [L.sb("Qz%d" % i, [128, S], BF16) for i in range(2)]
                    KaT = L.sb("KaT", [128, S], BF16)
                    Va = L.sb("Va", [128, NT, 2, 65], BF16)
                    Ea = [L.sb("Ea%d" % i, [128, 640], BF16) for i in range(3)]
                    rc = L.sb("rcA", [128, 2], F32)
                    oa = [L.sb("oa%d" % i, [128, 128], BF16) for i in range(2)]
                    snA = [L.ps("snA%d" % i, [128, 1024], F32) for i in range(2)]
                    accA = [L.ps("accA%d" % i, [128, 512], F32) for i in range(2)]
                    prj = L.ps("prjA", [128, 512], F32)
                    ptr = L.ps("ptrA", [128, 8, 128], BF16)
                    memset("pool", Va[:, :, :, 64:65], 1.0, ["Va1"])
                    memset("pool", Qz[0][64:128, :], 0.0, [("Qzpad", 0)])
                    memset("pool", Qz[1][0:64, :], 0.0, [("Qzpad", 1)])
                    wiv = win_d.rearrange("(k p) n -> p k n", p=128)
                    ecnt = 0
                    for g in range(4):
                        sl = g % 2
                        for tsel in range(3):
                            c0 = tsel * 512 + g * 128
                            dma("pool", wg[sl][:, :, tsel, :], wiv[:, :, c0:c0 + 128], (), [("wgA", sl, tsel)],
                                "s_wgA%d_%d" % (sl, tsel))
                        for tsel, dst, scl in ((0, None, 0.125), (1, KaT, 1.0)):
                            for c in range(4):
                                for k in range(8):
                                    mm(prj[:], wg[sl][:, k, tsel, :], hT[:, k, c * 512:(c + 1) * 512], k == 0, k == 7,
                                       [("wgA", sl, tsel)] + hkeys(k, c * 4, 4), ["prjA"])
                                if tsel == 0:
                                    act(Qz[0][0:64, c * 512:(c + 1) * 512], prj[0:64, :], AF.Copy, ["prjA"], [("Qz", 0, c)], scale=scl)
                                    act(Qz[1][64:128, c * 512:(c + 1) * 512], prj[64:128, :], AF.Copy, ["prjA"], [("Qz", 1, c)], scale=scl)
                                else:
                                    cp("dve", dst[:, c * 512:(c + 1) * 512], prj[:], ["prjA"], [("KaT", c)])
                        for c in range(4):
                            for tq in range(4):
                                t = c * 4 + tq
                                for k in range(8):
                                    mm(prj[:, tq * 128:(tq + 1) * 128], hT[:, k, t * 128:(t + 1) * 128], wg[sl][:, k, 2, :],
                                       k == 0, k == 7, [("wgA", sl, 2)] + hkeys(k, t, 1), ["prjA"])
                            cp("dve", Va[:, c * 4:(c + 1) * 4, :, 0:64],
                               prj[:].rearrange("p (t h e) -> p t h e", t=4, h=2), ["prjA"], [("Va", c)])
                        units = [(t, hh) for t in range(NT) for hh in range(2)]
                        u0 = ecnt

                        def qk_unit(u):
                            t, hh = units[u]
                            ssl = (u0 + u) % 2
                            for i, j in enumerate(_NA_J[t]):
                                mm(snA[ssl][:, i * 128:(i + 1) * 128], KaT[:, j * 128:(j + 1) * 128],
                                   Qz[hh][:, t * 128:(t + 1) * 128], True, True,
                                   [("KaT", j // 4), ("Qz", hh, t // 4), ("Qzpad", hh)], [("snA", ssl)])

                        qk_unit(0)
                        pend = []
                        for u, (t, hh) in enumerate(units):
                            js = _NA_J[t]
                            nb_ = len(js)
                            asl = t % 2
                            head = 2 * g + hh
                            ssl = (u0 + u) % 2
                            esl = (u0 + u) % 3
                            if u + 1 < len(units):
                                qk_unit(u + 1)
                            act(Ea[esl][:, 0:nb_ * 128], snA[ssl][:, 0:nb_ * 128], AF.Exp, [("snA", ssl)], [("Ea", esl)])
                            o0 = _NA_OFF[t]
                            tt("dve", Ea[esl][:, 0:nb_ * 128], Ea[esl][:, 0:nb_ * 128],
                               nab[:, head, o0:o0 + nb_, :].rearrange("p a q -> p (a q)"), ALU.mult,
                               [("Ea", esl), ("nab", head)], [("Ea", esl)])
                            for i, j in enumerate(js):
                                mm(accA[asl][:, hh * 65:(hh + 1) * 65], Ea[esl][:, i * 128:(i + 1) * 128],
                                   Va[:, j, hh, :], i == 0, i == nb_ - 1,
                                   [("Ea", esl), ("Va", j // 4), "Va1"], [("accA", asl)])
                            while pend:
                                pend.pop(0)()
                            if hh == 0:
                                continue
                            a3 = accA[asl][:, 0:130].rearrange("p (h e) -> p h e", h=2)
                            recip(rc[:], a3[:, :, 64], [("accA", asl)], ["rcA"])
                            for h2 in range(2):
                                ts("dve", oa[asl][:, h2 * 64:(h2 + 1) * 64], a3[:, h2, 0:64], rc[:, h2:h2 + 1], None,
                                   ALU.mult, None, [("accA", asl), "rcA"], [("oa", asl, h2)])

                            def epi(t=t, asl=asl):
                                tr(ptr[:, 0, :], oa[asl][:], ident[:], [("oa", asl, 0), ("oa", asl, 1), "ident"],
                                   ["ptrA"])
                                cp("act", oT[:, g, t * 128:(t + 1) * 128], ptr[:, 0, :], ["ptrA"], [("oT", g, t)])
                            pend.append(epi)
                        while pend:
                            pend.pop(0)()
                        ecnt += len(units)
                    P.barrier()
                    if stop == 2:
                        return nc

                with Scope() as L:
                    wg = [L.sb("wgB%d" % i, [128, 8, 3, 128], BF16) for i in range(2)]
                    QT = [L.sb("QT%d" % m, [128, S], BF16) for m in range(2)]
                    KT = [[L.sb("KT%d_%d" % (m, v), [128, S], BF16) for v in range(2)] for m in range(2)]
                    Vb = L.sb("Vb", [128, NT, 129], BF16)
                    Eb = [L.sb("Eb%d" % i, [128, 512], BF16) for i in range(3)]
                    om = [L.sb("om%d" % m, [128, 4, 128], F32) for m in range(2)]
                    df = L.sb("df", [128, 4, 128], F32)
                    junk = L.sb("junkB", [128, 128], BF16)
                    ssb = L.sb("ssb", [128, 4], F32)
                    rsb = L.sb("rsb", [128, 4], F32)
                    rcb = L.sb("rcb", [128, 4], F32)
                    ob = [L.sb("ob%d" % i, [128, 128], BF16) for i in range(4)]
                    sps = [L.ps("sps%d" % i, [128, 512], F32) for i in range(2)]
                    accb = [[L.ps("accb%d_%d" % (s_, i), [128, 512], F32) for i in range(2)] for s_ in range(2)]
                    prj = L.ps("prjB", [128, 512], F32)
                    ptr = L.ps("ptrB", [128, 8, 128], BF16)
                    memset("pool", Vb[:, :, 128:129], 1.0, ["Vb1"])
                    for m_ in range(2):
                        memset("pool", QT[m_][64:128, :], 0.0, [("QTa", m_)])
                        for v_ in range(2):
                            memset("pool", KT[m_][v_][64:128, :], 0.0, [("KTa", m_, v_)])
                    wiv = win_d.rearrange("(k p) n -> p k n", p=128)
                    ecnt = 0
                    acnt = 0
                    tcnt = 0
                    pendB = []
                    for h in range(4):
                        sl = h % 2
                        for tsel in range(3):
                            c0 = 1536 + tsel * 512 + h * 128
                            dma("pool", wg[sl][:, :, tsel, :], wiv[:, :, c0:c0 + 128], (), [("wgB", sl, tsel)],
                                "s_wgB%d_%d" % (sl, tsel))
                        import os
                        DBGQ = os.environ.get("DBGQ", "").split(",")
                        for m in range(2):
                            if "noaug" in DBGQ:
                                continue
                            dma("pool", QT[m][64:68, :], qaug_d[h], (), [("QTa", m)], "s_qa%d" % m)
                            for v in range(2):
                                dma("pool", KT[m][v][64:68, :], kaug_d[h, v], (), [("KTa", m, v)], "s_ka%d_%d" % (m, v))
                        for c in range(4):
                            if "noq" in DBGQ:
                                continue
                            for k in range(8):
                                mm(prj[:], wg[sl][:, k, 0, :], hT[:, k, c * 512:(c + 1) * 512], k == 0, k == 7,
                                   [("wgB", sl, 0)] + hkeys(k, c * 4, 4), ["prjB"])
                            act(QT[0][0:64, c * 512:(c + 1) * 512], prj[0:64, :], AF.Copy, ["prjB"], [("QT", 0, c)], scale=0.125)
                            act(QT[1][0:64, c * 512:(c + 1) * 512], prj[64:128, :], AF.Copy, ["prjB"], [("QT", 1, c)], scale=0.125)
                        for c in range(4):
                            if "nok" in DBGQ:
                                continue
                            for k in range(8):
                                mm(prj[:], wg[sl][:, k, 1, :], hT[:, k, c * 512:(c + 1) * 512], k == 0, k == 7,
                                   [("wgB", sl, 1)] + hkeys(k, c * 4, 4), ["prjB"])
                            cp("dve", KT[0][0][0:64, c * 512:(c + 1) * 512], prj[0:64, :], ["prjB"], [("KT", 0, 0, c)])
                            cp("dve", KT[1][0][0:64, c * 512:(c + 1) * 512], prj[64:128, :], ["prjB"], [("KT", 1, 0, c)])
                            for m_ in range(2):
                                cp("pool", KT[m_][1][0:64, c * 512:(c + 1) * 512], KT[m_][0][0:64, c * 512:(c + 1) * 512],
                                   [("KT", m_, 0, c)], [("KT", m_, 1, c)])
                        for c in range(4):
                            if "nov" in DBGQ:
                                continue
                            for tq in range(4):
                                t = c * 4 + tq
                                for k in range(8):
                                    mm(prj[:, tq * 128:(tq + 1) * 128], hT[:, k, t * 128:(t + 1) * 128], wg[sl][:, k, 2, :],
                                       k == 0, k == 7, [("wgB", sl, 2)] + hkeys(k, t, 1), ["prjB"])
                            cp("dve", Vb[:, c * 4:(c + 1) * 4, 0:128], prj[:].rearrange("p (t e) -> p t e", t=4),
                               ["prjB"], [("Vb", c)])
                        import os
                        DBGB = int(os.environ.get("DBGB", "9"))
                        for qc in range(4):
                            if DBGB < 2:
                                continue
                            for m in range(2):
                                aset = accb[acnt % 2]
                                akey = acnt % 2
                                acnt += 1

                                def qk(kt, ssl):
                                    o = sps[ssl]
                                    wk = [("sps", ssl)]

                                    def rng(v, c0, c1, aug=True):
                                        kk = 128 if aug else 64
                                        rd = [("KT", m, v, kt // 4), ("QT", m, qc)]
                                        if aug:
                                            rd += [("KTa", m, v), ("QTa", m)]
                                        return (o[:, c0:c1], KT[m][v][0:kk, kt * 128:(kt + 1) * 128],
                                                QT[m][0:kk, qc * 512 + c0: qc * 512 + c1], rd)
                                    if kt < 4 * qc:
                                        o_, l_, r_, rd = rng(0, 0, 512)
                                        mm(o_, l_, r_, True, True, rd, wk)
                                    elif kt >= 4 * qc + 4:
                                        o_, l_, r_, rd = rng(1, 0, 512)
                                        mm(o_, l_, r_, True, True, rd, wk)
                                    else:
                                        jd = kt - 4 * qc
                                        if jd > 0:
                                            o_, l_, r_, rd = rng(1, 0, jd * 128)
                                            mm(o_, l_, r_, True, True, rd, wk)
                                        o_, l_, r_, rd = rng(0, jd * 128, (jd + 1) * 128, aug=False)
                                        mm(o_, l_, r_, True, False, rd, wk)
                                        mm(o_, ident[:], bdg[:, h, :], False, True, ["ident", "bdg"], wk)
                                        if jd < 3:
                                            o_, l_, r_, rd = rng(0, (jd + 1) * 128, 512)
                                            mm(o_, l_, r_, True, True, rd, wk)

                                qk(0, ecnt % 2)
                                for kt in range(16):
                                    if kt == 2:
                                        while pendB:
                                            pendB.pop(0)()
                                    ssl = ecnt % 2
                                    esl = ecnt % 3
                                    ecnt += 1
                                    if kt + 1 < 16:
                                        qk(kt + 1, ecnt % 2)
                                    act(Eb[esl][:], sps[ssl][:], AF.Exp, [("sps", ssl)], [("Eb", esl)])
                                    for j in range(4):
                                        if DBGB < 3:
                                            continue
                                        mm(aset[j // 2][:, (j % 2) * 129:(j % 2) * 129 + 129], Eb[esl][:, j * 128:(j + 1) * 128],
                                           Vb[:, kt, :], kt == 0 and j % 2 == 0, kt == 15,
                                           [("Eb", esl), ("Vb", kt // 4), "Vb1"], [("accb", akey, j // 2)], skip=True)
                                for j in range(4):
                                    if DBGB < 4:
                                        continue
                                    a_ = aset[j // 2]
                                    c0 = (j % 2) * 129
                                    recip(rcb[:, j:j + 1], a_[:, c0 + 128:c0 + 129], [("accb", akey, j // 2)], [("rcb", j)])
                                    ts("dve", om[m][:, j, :], a_[:, c0:c0 + 128], rcb[:, j:j + 1], None, ALU.mult, None,
                                       [("accb", akey, j // 2), ("rcb", j)], [("om", m, j)])
                            if DBGB < 5:
                                continue
                            stt("dve", df[:], om[1][:], neglam[:, 0:1], om[0][:], ALU.mult, ALU.add,
                                [("om", 0, j) for j in range(4)] + [("om", 1, j) for j in range(4)] + ["neglam"], ["df"])
                            memset("dve", ssb[:], 0.0, ["ssb"])
                            for j in range(4):
                                act(junk[:], df[:, j, :], AF.Square, ["df", "ssb"], ["junkB", ("ssbd", j)], accum=ssb[:, j:j + 1])
                            ts("dve", rsb[:], ssb[:], 1.0 / 128, EPS, ALU.mult, ALU.add, [("ssbd", j) for j in range(4)], ["rsb"])
                            act(rsb[:], rsb[:], AF.Sqrt, ["rsb"], ["rsb"])
                            recip(rsb[:], rsb[:], ["rsb"], ["rsb"])
                            for j in range(4):
                                stt("dve", ob[j][:], df[:, j, :], rsb[:, j:j + 1], subw[:], ALU.mult, ALU.mult,
                                    ["df", "rsb", "subw"], [("ob", j)])

                            def epiB(h=h, qc=qc):
                                for j in range(4):
                                    tr(ptr[:, j, :], ob[j][:], ident[:], [("ob", j), "ident"], ["ptrB"])
                                cp("act", oT[:, 4 + h, qc * 512:(qc + 1) * 512], ptr[:, 0:4, :].rearrange("p a q -> p (a q)"),
                                   ["ptrB"], [("oT", 4 + h, qc * 4 + j) for j in range(4)])
                            pendB.append(epiB)
                    while pendB:
                        pendB.pop(0)()
                    P.barrier()
                    if stop == 3:
                        return nc

            with Scope() as R:
                x1 = R.sb("x1", [128, NT, D], F32)
                with Scope() as L:
                    wo = L.sb("wo", [128, 8, D], BF16)
                    wst = [L.sb("wst%d" % i, [128, D], F32) for i in range(2)]
                    xt = [L.sb("xtC%d" % i, [128, D], F32) for i in range(2)]
                    pc = [[L.ps("pc%d_%d" % (i, hf), [128, 512], F32) for hf in range(2)] for i in range(2)]
                    gbc, gkeys = build_bc(L, lambda k: modT[:, 16 + k, b:b + 1], [("modT", b)], "C")
                    for k in range(8):
                        sl = k % 2
                        dma("sp", wst[sl][:], wout_d[k * 128:(k + 1) * 128, :], (), [("wst", sl)], "s_wst%d" % sl)
                        tt("pool" if k % 3 == 2 else "dve", wo[:, k, :], wst[sl][:], gbc[:], ALU.mult,
                           [("wst", sl)] + gkeys, [("wo", k)])
                    for t in range(NT):
                        sl = t % 2
                        dma("sp", xt[sl][:], x_d[b, t * 128:(t + 1) * 128, :], (), [("xtC", sl)], "s_xtC%d" % sl)
                        for hf in range(2):
                            for k in range(8):
                                mm(pc[sl][hf][:], oT[:, k, t * 128:(t + 1) * 128], wo[:, k, hf * 512:(hf + 1) * 512],
                                   k == 0, k == 7, [("oT", k, t), ("wo", k)], [("pc", sl, hf)])
                            tt("dve", x1[:, t, hf * 512:(hf + 1) * 512], pc[sl][hf][:], xt[sl][:, hf * 512:(hf + 1) * 512],
                               ALU.add, [("pc", sl, hf), ("xtC", sl)], [("x1", t, hf)])
                    P.barrier()
                    if stop == 4:
                        return nc

                wguv = wgu_d.rearrange("(k p) n -> p k n", p=128)
                for half in range(2):
                    with Scope() as L:
                        actT = L.sb("actT", [128, NFC, 1024], BF16)
                        with Scope() as L2:
                            h2T = L2.sb("h2T", [128, 8, 1024], BF16)

                            def srcD(t):
                                return x1[:, t, :], [("x1", t, 0), ("x1", t, 1)]

                            with Scope() as L3:
                                abc = build_bc(L3, lambda k: aF[:, k, b:b + 1], [("aF", b)], "D")
                                sbc = build_bc(L3, lambda k: modT[:, 24 + k, b:b + 1], [("modT", b)], "Ds")
                                norm_phase(L3, srcD, h2T, abc, sbc, list(range(half * 8, half * 8 + 8)), 0, "D")
                                P.barrier()
                            h2keys = lambda k, t0, n: [("D", "T", k, t) for t in range(t0, t0 + n)]
                            with Scope() as L3:
                                wgu = [L3.sb("wgu%d" % i, [128, 8, 2, 128], BF16) for i in range(3)]
                                gs = [L3.sb("gs%d" % i, [128, 512], F32) for i in range(2)]
                                pg = [L3.ps("pg%d" % i, [128, 512], F32) for i in range(2)]
                                pu = [L3.ps("pu%d" % i, [128, 512], F32) for i in range(2)]
                                cnt = 0
                                for fc in range(NFC):
                                    sl = fc % 3
                                    dma("pool", wgu[sl][:, :, 0, :], wguv[:, :, fc * 128:(fc + 1) * 128], (), [("wgu", sl, 0)],
                                        "s_wgu%d_0" % sl)
                                    dma("pool", wgu[sl][:, :, 1, :], wguv[:, :, DFF + fc * 128:DFF + (fc + 1) * 128], (),
                                        [("wgu", sl, 1)], "s_wgu%d_1" % sl)
                                    for c in range(2):
                                        ps_ = cnt % 2
                                        cnt += 1
                                        for k in range(8):
                                            mm(pg[ps_][:], wgu[sl][:, k, 0, :], h2T[:, k, c * 512:(c + 1) * 512], k == 0, k == 7,
                                               [("wgu", sl, 0)] + h2keys(k, c * 4, 4), [("pg", ps_)])
                                        for k in range(8):
                                            mm(pu[ps_][:], wgu[sl][:, k, 1, :], h2T[:, k, c * 512:(c + 1) * 512], k == 0, k == 7,
                                               [("wgu", sl, 1)] + h2keys(k, c * 4, 4), [("pu", ps_)])
                                        act(gs[ps_][:], pg[ps_][:], AF.Silu, [("pg", ps_)], [("gs", ps_)])
                                        tt("dve", actT[:, fc, c * 512:(c + 1) * 512], pu[ps_][:], gs[ps_][:], ALU.mult,
                                           [("pu", ps_), ("gs", ps_)], [("actT", fc, c)])
                                P.barrier()
                        with Scope() as L2:
                            wd = L2.sb("wd", [128, NFC, 512], BF16)
                            wds = [L2.sb("wds%d" % i, [128, 512], F32) for i in range(2)]
                            pd = [L2.ps("pd%d" % i, [128, 512], F32) for i in range(4)]
                            ot = [L2.sb("ot%d" % i, [128, D], F32) for i in range(2)]
                            junk = L2.sb("junkF", [128, D], BF16)
                            ssf = L2.sb("ssf", [128, 8], F32)
                            rsf = L2.sb("rsf", [128, 8], F32)
                            cnt = 0
                            gbc, gkeys = build_bc(L2, lambda k: modT[:, 40 + k, b:b + 1], [("modT", b)], "F")
                            for dh in range(2):
                                for fc in range(NFC):
                                    sl = (dh * NFC + fc) % 2
                                    dma("sp", wds[sl][:], wdn_d[fc * 128:(fc + 1) * 128, dh * 512:(dh + 1) * 512], (),
                                        [("wds", sl)], "s_wds%d" % sl)
                                    tt("pool" if fc % 3 == 2 else "dve", wd[:, fc, :], wds[sl][:], gbc[:, dh * 512:(dh + 1) * 512], ALU.mult,
                                       [("wds", sl)] + gkeys, [("wd", fc)])
                                for ti in range(8):
                                    t = half * 8 + ti
                                    ps_ = cnt % 4
                                    cnt += 1
                                    for fc in range(NFC):
                                        mm(pd[ps_][:], actT[:, fc, ti * 128:(ti + 1) * 128], wd[:, fc, :], fc == 0, fc == NFC - 1,
                                           [("actT", fc, ti // 4), ("wd", fc)], [("pd", ps_)])
                                    tt("dve", x1[:, t, dh * 512:(dh + 1) * 512], pd[ps_][:], x1[:, t, dh * 512:(dh + 1) * 512],
                                       ALU.add, [("pd", ps_), ("x1", t, dh)], [("x1", t, dh)])
                            memset("dve", ssf[:], 0.0, ["ssf"])
                            for ti in range(8):
                                t = half * 8 + ti
                                sl = ti % 2
                                act(junk[:], x1[:, t, :], AF.Square, [("x1", t, 0), ("x1", t, 1), "ssf"],
                                    ["junkF", ("ssfd", ti)], accum=ssf[:, ti:ti + 1])
                                ts("dve", rsf[:, ti:ti + 1], ssf[:, ti:ti + 1], 1.0 / D, EPS, ALU.mult, ALU.add,
                                   [("ssfd", ti)], [("rsf", ti)])
                                act(rsf[:, ti:ti + 1], rsf[:, ti:ti + 1], AF.Sqrt, [("rsf", ti)], [("rsf", ti)])
                                recip(rsf[:, ti:ti + 1], rsf[:, ti:ti + 1], [("rsf", ti)], [("rsf", ti)])
                                stt("dve", ot[sl][:], x1[:, t, :], rsf[:, ti:ti + 1], gfbc[:], ALU.mult, ALU.mult,
                                    [("x1", t, 0), ("x1", t, 1), ("rsf", ti), "gfbc"], [("ot", sl)])
                                dma("sp", out_d[b, t * 128:(t + 1) * 128, :], ot[sl][:], [("ot", sl)], (),
                                    "s_out%d" % sl, final=True)
                            P.barrier(last=(b == NB - 1 and half == 1))
    return nc


_NC_CACHE = {}


def kernel(x, c, w_ada, b_ada, g_mix, w_in, rpb, lambda_q1, lambda_k1, lambda_q2, lambda_k2,
           subln_w, w_out, g_ffn, w_gate_up, w_down, g_final):
    f32 = np.float32
    x = np.asarray(x, f32)
    c = np.asarray(c, f32)
    if "nc" not in _NC_CACHE:
        _NC_CACHE["nc"] = build_nc()
    nc = _NC_CACHE["nc"]
    qaug, kaug, bdiag, mask = _host_consts()
    rpb0 = np.asarray(rpb, f32)[0]
    nabg = np.zeros((8, 128, NPAT, 128), f32)
    for p, pat in enumerate(_NA_BLKPAT):
        dr, dc, valid = _NA_PATS[pat]
        g_ = rpb0[:, dr, dc]
        nabg[:, :, p, :] = np.where(valid[None], g_, f32(0))
    nabg = nabg.reshape(8, 128, NPAT * 128)

    def colT(v, n):
        return np.ascontiguousarray(np.asarray(v, f32).reshape(n, 128).T)

    shared = {
        "w_ada": np.ascontiguousarray(np.asarray(w_ada, f32)[0]),
        "b_adaT": colT(np.asarray(b_ada, f32)[0], 48),
        "g_mixT": colT(np.asarray(g_mix, f32)[0], 8),
        "g_ffnT": colT(np.asarray(g_ffn, f32)[0], 8),
        "w_in": np.ascontiguousarray(np.asarray(w_in, f32)[0]),
        "w_out": np.ascontiguousarray(np.asarray(w_out, f32)[0]),
        "w_gate_up": np.ascontiguousarray(np.asarray(w_gate_up, f32)[0]),
        "w_down": np.ascontiguousarray(np.asarray(w_down, f32)[0]),
        "g_final": np.ascontiguousarray(np.asarray(g_final, f32)),
        "subln_w": np.ascontiguousarray(np.asarray(subln_w, f32)[0]),
        "lams": np.ascontiguousarray(np.stack([np.asarray(v, f32)[0] for v in
                                               (lambda_q1, lambda_k1, lambda_q2, lambda_k2)])),
        "nab_g": nabg,
        "na_mask": np.ascontiguousarray(mask.reshape(128, NPAT * 128)),
        "qaug": qaug, "kaug": kaug,
        "bdiag": np.ascontiguousarray(bdiag.reshape(128, 512)),
    }
    in_maps = []
    for i in range(8):
        cc = c[2 * i:2 * i + 2]
        cT = np.ascontiguousarray(cc.reshape(2, 8, 128).transpose(2, 1, 0).reshape(128, 16))
        m = dict(shared)
        m["x"] = np.ascontiguousarray(x[2 * i:2 * i + 2])
        m["cT"] = cT
        in_maps.append(m)
    res = run_bass_kernel_spmd(nc, in_maps, core_ids=list(range(8)))
    return np.concatenate([np.asarray(r["out"], f32) for r in res.results], axis=0)
```

```python
import contextlib
import math
import numpy as np
import concourse.bass as bass
import concourse.mybir as mybir
from concourse.bass_utils import run_bass_kernel_spmd

F32 = mybir.dt.float32
BF16 = mybir.dt.bfloat16
AF = mybir.ActivationFunctionType
ALU = mybir.AluOpType

D = 1024
S = 2048
NT = 16
NB = 2
DFF = 2816
NFC = 22
EPS = 1e-6
LAMBDA_INIT = 0.2
NEG = -30000.0
SAME_ENGINE_SYNC = True


class Prog:
    ENGINES = ("pe", "act", "dve", "pool", "sp")

    def __init__(self, nc, semstack):
        self.nc = nc
        self.semstack = semstack
        self.ops = []
        self.last_writer = {}
        self.readers = {}
        self.bar_deps = set()
        self.last_on_eng = {}
        self.dma_since_bar = []
        self.flushed = 0
        self.sems = {}
        self.counts = {}
        self.known = {e: {} for e in self.ENGINES}
        self.final_ops = []
        self.excl = set()

    def op(self, eng, fn, reads=(), writes=(), dma=None, final=False):
        oid = len(self.ops)
        deps = set(self.bar_deps)
        raw = set()
        for k in reads:
            w = self.last_writer.get(k)
            if w is not None:
                deps.add(w)
                raw.add(w)
            if k in self.excl:
                for r in self.readers.get(k, ()):
                    if self.ops[r]["eng"] != eng:
                        deps.add(r)
        for k in writes:
            w = self.last_writer.get(k)
            if w is not None:
                deps.add(w)
            for r in self.readers.get(k, ()):
                deps.add(r)
        for k in reads:
            self.readers.setdefault(k, []).append(oid)
        for k in writes:
            self.last_writer[k] = oid
            self.readers[k] = []
        self.ops.append(dict(eng=eng, fn=fn, deps=deps, raw=raw, dma=dma, needed=final, ev=None))
        self.last_on_eng[eng] = oid
        if dma is not None:
            self.dma_since_bar.append(oid)
        if final:
            self.final_ops.append(oid)
        return oid

    def _skip(self, do, ename, raw=True):
        return do["dma"] is None and do["eng"] == ename and (ename == "pe" or not SAME_ENGINE_SYNC or not raw)

    def _sem(self, sn):
        if sn not in self.sems:
            self.sems[sn] = self.semstack.enter_context(self.nc.semaphore("sm%d" % len(self.sems)))
            self.counts[sn] = 0
        return self.sems[sn]

    def barrier(self, last=False):
        self.bar_deps = set(self.last_on_eng.values()) | set(self.dma_since_bar)
        self.dma_since_bar = []
        self._flush(last)

    def _flush(self, last):
        nc = self.nc
        ops = self.ops
        lo, hi = self.flushed, len(ops)
        self.flushed = hi
        for i in range(lo, hi):
            o = ops[i]
            for d in o["deps"]:
                if d >= lo and not self._skip(ops[d], o["eng"], d in o["raw"]):
                    ops[d]["needed"] = True
        for d in self.bar_deps:
            if d >= lo:
                ops[d]["needed"] = True
        for e in self.ENGINES:
            self._sem(("eng", e))
        for i in range(lo, hi):
            o = ops[i]
            if not o["needed"]:
                continue
            sn = ("dma", o["dma"]) if o["dma"] is not None else ("eng", o["eng"])
            self._sem(sn)
            self.counts[sn] += 16 if o["dma"] is not None else 1
            o["ev"] = (sn, self.counts[sn])
        by_eng = {e: [] for e in self.ENGINES}
        for i in range(lo, hi):
            by_eng[ops[i]["eng"]].append(i)

        def run_engine(ename, eh):
            known = self.known[ename]
            for i in by_eng[ename]:
                o = ops[i]
                need = {}
                for d in o["deps"]:
                    do = ops[d]
                    if do["ev"] is None or self._skip(do, ename, d in o["raw"]):
                        continue
                    sn, v = do["ev"]
                    if need.get(sn, 0) < v:
                        need[sn] = v
                for sn, v in need.items():
                    if known.get(sn, 0) >= v:
                        continue
                    eh.wait_ge(self.sems[sn], v)
                    known[sn] = v
                ins = o["fn"](eh)
                if o["ev"] is not None:
                    ins.then_inc(self.sems[o["ev"][0]], 16 if o["dma"] is not None else 1)
                o["fn"] = None
            if last and ename == "sp":
                for d in self.final_ops:
                    sn, v = ops[d]["ev"]
                    if known.get(sn, 0) < v:
                        eh.wait_ge(self.sems[sn], v)
                        known[sn] = v

        with nc.Block() as block:
            @block.tensor
            def _(e):
                run_engine("pe", e)

            @block.scalar
            def _(e):
                run_engine("act", e)

            @block.vector
            def _(e):
                run_engine("dve", e)

            @block.gpsimd
            def _(e):
                run_engine("pool", e)

            @block.sync
            def _(e):
                run_engine("sp", e)


def _na_tables():
    rows, W, wh, ww = 32, 64, 8, 16
    pats = []
    pat_key = {}
    J = []
    pid = {}
    kk = np.arange(128)
    kr, ck = kk // 64, kk % 64
    for t in range(16):
        r_q = 2 * t + kr
        cq = ck
        rs = np.clip(r_q - wh // 2, 0, rows - wh)
        cs = np.clip(cq - ww // 2, 0, W - ww)
        jlo = int(rs.min()) // 2
        jhi = int(rs.max() + wh - 1) // 2
        js = list(range(jlo, jhi + 1))
        J.append(js)
        for j in js:
            krow = 2 * j + kr
            valid_r = (krow[:, None] >= rs[None, :]) & (krow[:, None] < rs[None, :] + wh)
            dr = krow[:, None] - r_q[None, :] + 7
            valid_c = (ck[:, None] >= cs[None, :]) & (ck[:, None] < cs[None, :] + ww)
            dc = np.clip(ck[:, None] - cq[None, :] + ww - 1, 0, 2 * ww - 2)
            valid = valid_r & valid_c
            drc = np.where(valid, dr, 0)
            key = (drc.tobytes(), dc.tobytes(), valid.tobytes())
            if key not in pat_key:
                pat_key[key] = len(pats)
                pats.append((drc.astype(np.int64), dc.astype(np.int64), valid))
            pid[(t, j)] = pat_key[key]
    return J, pid, pats


_NA_J, _NA_PID, _NA_PATS = _na_tables()


def _na_blocks():
    blk_pat = []
    off = {}
    ref = [_NA_PID[(2, j)] for j in _NA_J[2]]
    for t in range(2, 14):
        assert [_NA_PID[(t, j)] for j in _NA_J[t]] == ref
    blk_pat += ref
    for t in range(2, 14):
        off[t] = 0
    for t in (0, 1, 14, 15):
        off[t] = len(blk_pat)
        blk_pat += [_NA_PID[(t, j)] for j in _NA_J[t]]
    return blk_pat, off


_NA_BLKPAT, _NA_OFF = _na_blocks()
NPAT = len(_NA_BLKPAT)


def _host_consts():
    slopes = [2.0 ** (-8.0 * (i + 1) / 4) for i in range(4)]
    tok = np.arange(S)
    qaug = np.zeros((4, 4, S), np.float32)
    kaug = np.zeros((4, 2, 4, S), np.float32)
    bdiag = np.zeros((128, 4, 128), np.float32)
    for h, sl in enumerate(slopes):
        qaug[h, 0] = sl * (tok % 256)
        qaug[h, 1] = sl * (tok - tok % 256)
        qaug[h, 2] = 1.0
        qaug[h, 3] = 1.0
        kaug[h, 0, 0] = -1.0
        kaug[h, 0, 1] = -1.0
        kaug[h, 0, 2] = sl * (tok % 128)
        kaug[h, 0, 3] = sl * (tok - tok % 128)
        kaug[h, 1] = -kaug[h, 0]
        i = np.arange(128)
        bdiag[:, h, :] = -sl * np.abs(i[:, None] - i[None, :])
    mask = np.zeros((128, NPAT, 128), np.float32)
    for p, pat in enumerate(_NA_BLKPAT):
        mask[:, p, :] = np.where(_NA_PATS[pat][2], 0.0, NEG)
    return qaug, kaug, bdiag, mask


def build_nc(stop=None):
    nc = bass.Bass("TRN2", target_bir_lowering=False)

    def din(name, shape):
        return nc.dram_tensor(name, list(shape), F32, kind="ExternalInput").ap()

    x_d = din("x", [NB, S, D])
    cT_d = din("cT", [128, 16])
    wada_d = din("w_ada", [D, 6 * D])
    badaT_d = din("b_adaT", [128, 48])
    gmixT_d = din("g_mixT", [128, 8])
    gffnT_d = din("g_ffnT", [128, 8])
    win_d = din("w_in", [D, 3 * D])
    wout_d = din("w_out", [D, D])
    wgu_d = din("w_gate_up", [D, 2 * DFF])
    wdn_d = din("w_down", [DFF, D])
    gfin_d = din("g_final", [D])
    subw_d = din("subln_w", [128])
    lam_d = din("lams", [4, 64])
    nabg_d = din("nab_g", [8, 128, NPAT * 128])
    mask_d = din("na_mask", [128, NPAT * 128])
    qaug_d = din("qaug", [4, 4, S])
    kaug_d = din("kaug", [4, 2, 4, S])
    bdiag_d = din("bdiag", [128, 4 * 128])
    out_d = nc.dram_tensor("out", [NB, S, D], F32, kind="ExternalOutput").ap()

    semstack = contextlib.ExitStack()
    P = Prog(nc, semstack)
    _uid = [0]
    for _k in ["prjA", "prjB", "ptrA", "ptrB", "pmod"]:
        P.excl.add(_k)
    for _i in range(4):
        for _j in range(4):
            P.excl.update([("snA", _i), ("accA", _i), ("sps", _i), ("accb", _i, _j), ("pc", _i, _j), ("pg", _i), ("pu", _i),
                           ("pd", _i), ("ptr", "A", _i), ("ptr", "D", _i)])

    class Scope:
        def __init__(self):
            self.st = contextlib.ExitStack()

        def __enter__(self):
            self.st.__enter__()
            return self

        def __exit__(self, *a):
            return self.st.__exit__(*a)

        def sb(self, name, shape, dt):
            _uid[0] += 1
            return self.st.enter_context(nc.sbuf_tensor("%s_%d" % (name, _uid[0]), list(shape), dt))

        def ps(self, name, shape, dt):
            _uid[0] += 1
            return self.st.enter_context(nc.psum_tensor("%s_%d" % (name, _uid[0]), list(shape), dt))

    def mm(out, lhsT, rhs, start, stop, reads, writes, skip=False):
        if skip:
            P.op("pe", lambda e: e.matmul(out, lhsT=lhsT, rhs=rhs, start=start, stop=stop,
                                          skip_group_check=True), reads, writes)
        else:
            P.op("pe", lambda e: e.matmul(out, lhsT=lhsT, rhs=rhs, start=start, stop=stop), reads, writes)

    def tr(out, in_, ident, reads, writes):
        P.op("pe", lambda e: e.transpose(out=out, in_=in_, identity=ident), reads, writes)

    def act(out, in_, func, reads, writes, scale=None, bias=None, accum=None):
        kw = {}
        if scale is not None:
            kw["scale"] = scale
        if bias is not None:
            kw["bias"] = bias
        if accum is not None:
            kw["accum_out"] = accum
        P.op("act", lambda e: e.activation(out=out, in_=in_, func=func, **kw), reads, writes)

    def ts(eng, out, in0, s1, s2, op0, op1, reads, writes):
        if op1 is None:
            P.op(eng, lambda e: e.tensor_scalar(out=out, in0=in0, scalar1=s1, scalar2=None, op0=op0), reads, writes)
        else:
            P.op(eng, lambda e: e.tensor_scalar(out=out, in0=in0, scalar1=s1, scalar2=s2, op0=op0, op1=op1), reads, writes)

    def tt(eng, out, in0, in1, op, reads, writes):
        P.op(eng, lambda e: e.tensor_tensor(out=out, in0=in0, in1=in1, op=op), reads, writes)

    def stt(eng, out, in0, scalar, in1, op0, op1, reads, writes):
        P.op(eng, lambda e: e.scalar_tensor_tensor(out=out, in0=in0, scalar=scalar, in1=in1, op0=op0, op1=op1), reads, writes)

    def cp(eng, out, in_, reads, writes):
        if eng == "act":
            P.op(eng, lambda e: e.activation(out=out, in_=in_, func=AF.Copy), reads, writes)
        else:
            P.op(eng, lambda e: e.tensor_copy(out=out, in_=in_), reads, writes)

    def recip(out, in_, reads, writes):
        P.op("dve", lambda e: e.reciprocal(out=out, in_=in_), reads, writes)

    def memset(eng, ap, val, writes):
        P.op(eng, lambda e: e.memset(ap, val), (), writes)

    def dma(eng, out, in_, reads, writes, key, final=False):
        return P.op(eng, lambda e: e.dma_start(out=out, in_=in_), reads, writes, dma=key, final=final)

    with semstack, Scope() as G:
        identf = G.sb("identf", [128, 128], F32)
        ident = G.sb("ident", [128, 128], BF16)
        onesf = G.sb("onesf", [128, 128], F32)
        modT = G.sb("modT", [128, 48, 2], F32)
        aM = G.sb("aM", [128, 8, 2], F32)
        aF = G.sb("aF", [128, 8, 2], F32)
        gfbc = G.sb("gfbc", [128, D], F32)
        subw = G.sb("subw", [128, 128], F32)
        neglam = G.sb("neglam", [128, 1], F32)
        bdg = G.sb("bdg", [128, 4, 128], BF16)
        oT = G.sb("oT", [128, 8, S], BF16)

        def build_bc(L, colfn, rkeys, tag):
            bc = L.sb("bc" + tag, [128, D], F32)
            dg = [L.sb("dg%s%d" % (tag, i), [128, 128], F32) for i in range(2)]
            pb = [L.ps("pb%s%d" % (tag, i), [128, 512], F32) for i in range(2)]
            for half in range(2):
                for kk in range(4):
                    k = half * 4 + kk
                    sl = k % 2
                    ts("dve", dg[sl][:], identf[:], colfn(k), None, ALU.mult, None, ["identf"] + rkeys, [("dg", tag, sl)])
                    mm(pb[half][:, kk * 128:(kk + 1) * 128], onesf[:], dg[sl][:], True, True,
                       ["onesf", ("dg", tag, sl)], [("pb", tag, half)])
                cp("act", bc[:, half * 512:(half + 1) * 512], pb[half][:], [("pb", tag, half)], [("bc", tag, half)])
            return bc, [("bc", tag, 0), ("bc", tag, 1)]

        with Scope() as L:
            cT = L.sb("cT", [128, 16], F32)
            cS = L.sb("cS", [128, 16], BF16)
            badaT = L.sb("badaT", [128, 48], F32)
            gmT = L.sb("gmT", [128, 8], F32)
            gfT = L.sb("gfT", [128, 8], F32)
            lamv = L.sb("lamv", [128, 4, 64], F32)
            lamp = L.sb("lamp", [128, 2, 64], F32)
            lams = L.sb("lams_s", [128, 2], F32)
            wad = [L.sb("wad%d" % i, [128, 8, 1024], BF16) for i in range(2)]
            pmod = L.ps("pmod", [128, 512], F32)

            memset("pool", identf[:], 0.0, ["identf"])
            P.op("pool", lambda e: e.affine_select(out=identf[:], in_=identf[:], pattern=[[-1, 128]],
                                                   compare_op=ALU.not_equal, fill=1.0, base=0,
                                                   channel_multiplier=1), ["identf"], ["identf"])
            cp("dve", ident[:], identf[:], ["identf"], ["ident"])
            memset("pool", onesf[:], 1.0, ["onesf"])
            dma("sp", cT[:], cT_d, (), ["cT"], "s_cT")
            dma("sp", badaT[:], badaT_d, (), ["badaT"], "s_bada")
            dma("sp", gmT[:], gmixT_d, (), ["gmT"], "s_gm")
            dma("sp", gfT[:], gffnT_d, (), ["gfT"], "s_gf")
            dma("sp", gfbc[:], gfin_d.partition_broadcast(128), (), ["gfbc"], "s_gfbc")
            dma("sp", subw[:], subw_d.partition_broadcast(128), (), ["subw"], "s_subw")
            for i in range(4):
                dma("sp", lamv[:, i, :], lam_d[i].partition_broadcast(128), (), [("lamv", i)], "s_lam%d" % i)
            dma("pool", bdg[:].rearrange("p h q -> p (h q)"), bdiag_d, (), ["bdg"], "s_bdg")
            ts("dve", subw[:], subw[:], 1.0 - LAMBDA_INIT, None, ALU.mult, None, ["subw"], ["subw"])
            tt("dve", lamp[:, 0, :], lamv[:, 0, :], lamv[:, 1, :], ALU.mult, [("lamv", 0), ("lamv", 1)], [("lamp", 0)])
            tt("dve", lamp[:, 1, :], lamv[:, 2, :], lamv[:, 3, :], ALU.mult, [("lamv", 2), ("lamv", 3)], [("lamp", 1)])
            P.op("dve", lambda e: e.reduce_sum(out=lams[:, 0:1], in_=lamp[:, 0, :], axis=mybir.AxisListType.X),
                 [("lamp", 0)], [("lams", 0)])
            P.op("dve", lambda e: e.reduce_sum(out=lams[:, 1:2], in_=lamp[:, 1, :], axis=mybir.AxisListType.X),
                 [("lamp", 1)], [("lams", 1)])
            act(lams[:], lams[:], AF.Exp, [("lams", 0), ("lams", 1)], ["lamse"])
            tt("dve", neglam[:], lams[:, 1:2], lams[:, 0:1], ALU.subtract, ["lamse"], ["neglam"])
            ts("dve", neglam[:], neglam[:], -LAMBDA_INIT, None, ALU.add, None, ["neglam"], ["neglam"])
            act(cS[:], cT[:], AF.Silu, ["cT"], ["cS"])
            wv = wada_d.rearrange("(k p) n -> p k n", p=128)
            for pc in range(6):
                sl = pc % 2
                dma("pool", wad[sl][:], wv[:, :, pc * 1024:(pc + 1) * 1024], (), [("wad", sl)], "s_wad%d" % sl)
                for jj in range(8):
                    j = pc * 8 + jj
                    for k in range(8):
                        mm(pmod[:, 2 * j:2 * j + 2], wad[sl][:, k, jj * 128:(jj + 1) * 128],
                           cS[:, 2 * k:2 * k + 2], k == 0, k == 7, [("wad", sl), "cS"], ["pmod"])
            pm3 = pmod[:, 0:96].rearrange("p (j b) -> p j b", b=2)
            for b in range(2):
                tt("dve", modT[:, :, b], pm3[:, :, b], badaT[:], ALU.add, ["pmod", "badaT"], [("modT", b)])
            for b in range(2):
                stt("dve", aM[:, :, b], modT[:, 8:16, b], 1.0, gmT[:], ALU.add, ALU.mult, [("modT", b), "gmT"], [("aM", b)])
                stt("dve", aF[:, :, b], modT[:, 32:40, b], 1.0, gfT[:], ALU.add, ALU.mult, [("modT", b), "gfT"], [("aF", b)])
            P.barrier()
            if stop == 0:
                return nc

        for b in range(NB):
            with Scope() as M:
                hT = M.sb("hT", [128, 8, S], BF16)
                nab = M.sb("nab", [128, 8, NPAT, 128], BF16)

                def norm_phase(L, src_tile_fn, dstT, a_sc, sh_sc, tiles, tok0, tagp):
                    ss = L.sb("ss" + tagp, [128, 16], F32)
                    rs = L.sb("rs" + tagp, [128, 16], F32)
                    junk = L.sb("junk" + tagp, [128, D], BF16)
                    xn = [L.sb("xn%s%d" % (tagp, i), [128, D], BF16) for i in range(2)]
                    xf = [L.sb("xf%s%d" % (tagp, i), [128, D], F32) for i in range(2)]
                    ptr = [L.ps("ptr%s%d" % (tagp, i), [128, 8, 128], BF16) for i in range(2)]
                    memset("dve", ss[:], 0.0, [("ss", tagp)])
                    import os
                    DBG = int(os.environ.get("DBGA", "9"))
                    for ti, t in enumerate(tiles):
                        src, rk = src_tile_fn(t)
                        sl = ti % 2
                        act(junk[:], src, AF.Square, rk + [("ss", tagp)], [("junk", tagp), ("ssd", tagp, ti)],
                            accum=ss[:, ti:ti + 1])
                        ts("dve", rs[:, ti:ti + 1], ss[:, ti:ti + 1], 1.0 / D, EPS, ALU.mult, ALU.add,
                           [("ssd", tagp, ti)], [("rs", tagp, ti)])
                        act(rs[:, ti:ti + 1], rs[:, ti:ti + 1], AF.Sqrt, [("rs", tagp, ti)], [("rs", tagp, ti)])
                        recip(rs[:, ti:ti + 1], rs[:, ti:ti + 1], [("rs", tagp, ti)], [("rs", tagp, ti)])
                        stt("dve", xf[sl][:], src, rs[:, ti:ti + 1], a_sc[0][:], ALU.mult, ALU.mult,
                            rk + [("rs", tagp, ti)] + a_sc[1], [("xf", tagp, sl)])
                        tt("pool" if ti % 2 == 0 else "dve", xn[sl][:], xf[sl][:], sh_sc[0][:], ALU.add, [("xf", tagp, sl)] + sh_sc[1], [("xn", tagp, sl)])
                        for k in range(8):
                            tr(ptr[sl][:, k, :], xn[sl][:, k * 128:(k + 1) * 128], ident[:],
                               [("xn", tagp, sl), "ident"], [("ptr", tagp, sl)])
                        dst = dstT[:, :, tok0 + ti * 128: tok0 + (ti + 1) * 128]
                        wk = [(tagp, "T", k, tok0 // 128 + ti) for k in range(8)]
                        cp("act" if ti % 2 == 0 else "dve", dst, ptr[sl][:], [("ptr", tagp, sl)], wk)

                with Scope() as L:
                    xt = [L.sb("xtA%d" % i, [128, D], F32) for i in range(2)]
                    msk = L.sb("msk", [128, NPAT * 128], F32)
                    stg = [L.sb("stg%d" % i, [128, NPAT * 128], F32) for i in range(2)]
                    dma("sp", msk[:], mask_d, (), ["msk"], "s_msk")
                    for h in range(8):
                        sl = h % 2
                        dma("sp", stg[sl][:], nabg_d[h], (), [("stg", sl)], "s_stg%d" % sl)
                        tt("pool", nab[:, h, :, :].rearrange("p a q -> p (a q)"), stg[sl][:], msk[:], ALU.add,
                           [("stg", sl), "msk"], [("nab", h)])
                        act(nab[:, h, :, :].rearrange("p a q -> p (a q)"), nab[:, h, :, :].rearrange("p a q -> p (a q)"),
                            AF.Exp, [("nab", h)], [("nab", h)])

                    def srcA(t):
                        sl = t % 2
                        dma("sp", xt[sl][:], x_d[b, t * 128:(t + 1) * 128, :], (), [("xtA", sl)], "s_xtA%d" % sl)
                        return xt[sl][:], [("xtA", sl)]

                    abc = build_bc(L, lambda k: aM[:, k, b:b + 1], [("aM", b)], "A")
                    sbc = build_bc(L, lambda k: modT[:, 0 + k, b:b + 1], [("modT", b)], "As")
                    norm_phase(L, srcA, hT, abc, sbc, list(range(NT)), 0, "A")
                    P.barrier()
                    if stop == 1:
                        return nc
                hkeys = lambda k, t0, n: [("A", "T", k, t) for t in range(t0, t0 + n)]

                with Scope() as L:
                    wg = [L.sb("wgA%d" % i, [128, 8, 3, 128], BF16) for i in range(2)]
                    Qz = [L.sb("Qz%d" % i, [128, S], BF16) for i in range(2)]
                    KaT = L.sb("KaT", [128, S], BF16)
                    Va = L.sb("Va", [128, NT, 2, 65], BF16)
                    Ea = [L.sb("Ea%d" % i, [128, 640], BF16) for i in range(3)]
                    rc = L.sb("rcA", [128, 2], F32)
                    oa = [L.sb("oa%d" % i, [128, 128], BF16) for i in range(2)]
                    snA = [L.ps("snA%d" % i, [128, 1024], F32) for i in range(2)]
                    accA = [L.ps("accA%d" % i, [128, 512], F32) for i in range(2)]
                    prj = L.ps("prjA", [128, 512], F32)
                    ptr = L.ps("ptrA", [128, 8, 128], BF16)
                    memset("pool", Va[:, :, :, 64:65], 1.0, ["Va1"])
                    memset("pool", Qz[0][64:128, :], 0.0, [("Qzpad", 0)])
                    memset("pool", Qz[1][0:64, :], 0.0, [("Qzpad", 1)])
                    wiv = win_d.rearrange("(k p) n -> p k n", p=128)
                    ecnt = 0
                    for g in range(4):
                        sl = g % 2
                        for tsel in range(3):
                            c0 = tsel * 512 + g * 128
                            dma("pool", wg[sl][:, :, tsel, :], wiv[:, :, c0:c0 + 128], (), [("wgA", sl, tsel)],
                                "s_wgA%d_%d" % (sl, tsel))
                        for tsel, dst, scl in ((0, None, 0.125), (1, KaT, 1.0)):
                            for c in range(4):
                                for k in range(8):
                                    mm(prj[:], wg[sl][:, k, tsel, :], hT[:, k, c * 512:(c + 1) * 512], k == 0, k == 7,
                                       [("wgA", sl, tsel)] + hkeys(k, c * 4, 4), ["prjA"])
                                if tsel == 0:
                                    act(Qz[0][0:64, c * 512:(c + 1) * 512], prj[0:64, :], AF.Copy, ["prjA"], [("Qz", 0, c)], scale=scl)
                                    act(Qz[1][64:128, c * 512:(c + 1) * 512], prj[64:128, :], AF.Copy, ["prjA"], [("Qz", 1, c)], scale=scl)
                                else:
                                    cp("dve", dst[:, c * 512:(c + 1) * 512], prj[:], ["prjA"], [("KaT", c)])
                        for c in range(4):
                            for tq in range(4):
                                t = c * 4 + tq
                                for k in range(8):
                                    mm(prj[:, tq * 128:(tq + 1) * 128], hT[:, k, t * 128:(t + 1) * 128], wg[sl][:, k, 2, :],
                                       k == 0, k == 7, [("wgA", sl, 2)] + hkeys(k, t, 1), ["prjA"])
                            cp("dve", Va[:, c * 4:(c + 1) * 4, :, 0:64],
                               prj[:].rearrange("p (t h e) -> p t h e", t=4, h=2), ["prjA"], [("Va", c)])
                        units = [(t, hh) for t in range(NT) for hh in range(2)]
                        u0 = ecnt

                        def qk_unit(u):
                            t, hh = units[u]
                            ssl = (u0 + u) % 2
                            for i, j in enumerate(_NA_J[t]):
                                mm(snA[ssl][:, i * 128:(i + 1) * 128], KaT[:, j * 128:(j + 1) * 128],
                                   Qz[hh][:, t * 128:(t + 1) * 128], True, True,
                                   [("KaT", j // 4), ("Qz", hh, t // 4), ("Qzpad", hh)], [("snA", ssl)])

                        qk_unit(0)
                        pend = []
                        for u, (t, hh) in enumerate(units):
                            js = _NA_J[t]
                            nb_ = len(js)
                            asl = t % 2
                            head = 2 * g + hh
                            ssl = (u0 + u) % 2
                            esl = (u0 + u) % 3
                            if u + 1 < len(units):
                                qk_unit(u + 1)
                            act(Ea[esl][:, 0:nb_ * 128], snA[ssl][:, 0:nb_ * 128], AF.Exp, [("snA", ssl)], [("Ea", esl)])
                            o0 = _NA_OFF[t]
                            tt("dve", Ea[esl][:, 0:nb_ * 128], Ea[esl][:, 0:nb_ * 128],
                               nab[:, head, o0:o0 + nb_, :].rearrange("p a q -> p (a q)"), ALU.mult,
                               [("Ea", esl), ("nab", head)], [("Ea", esl)])
                            for i, j in enumerate(js):
                                mm(accA[asl][:, hh * 65:(hh + 1) * 65], Ea[esl][:, i * 128:(i + 1) * 128],
                                   Va[:, j, hh, :], i == 0, i == nb_ - 1,
                                   [("Ea", esl), ("Va", j // 4), "Va1"], [("accA", asl)])
                            while pend:
                                pend.pop(0)()
                            if hh == 0:
                                continue
                            a3 = accA[asl][:, 0:130].rearrange("p (h e) -> p h e", h=2)
                            recip(rc[:], a3[:, :, 64], [("accA", asl)], ["rcA"])
                            for h2 in range(2):
                                ts("dve", oa[asl][:, h2 * 64:(h2 + 1) * 64], a3[:, h2, 0:64], rc[:, h2:h2 + 1], None,
                                   ALU.mult, None, [("accA", asl), "rcA"], [("oa", asl, h2)])

                            def epi(t=t, asl=asl):
                                tr(ptr[:, 0, :], oa[asl][:], ident[:], [("oa", asl, 0), ("oa", asl, 1), "ident"],
                                   ["ptrA"])
                                cp("act", oT[:, g, t * 128:(t + 1) * 128], ptr[:, 0, :], ["ptrA"], [("oT", g, t)])
                            pend.append(epi)
                        while pend:
                            pend.pop(0)()
                        ecnt += len(units)
                    P.barrier()
                    if stop == 2:
                        return nc

                with Scope() as L:
                    wg = [L.sb("wgB%d" % i, [128, 8, 3, 128], BF16) for i in range(2)]
                    QT = [L.sb("QT%d" % m, [128, S], BF16) for m in range(2)]
                    KT = [[L.sb("KT%d_%d" % (m, v), [128, S], BF16) for v in range(2)] for m in range(2)]
                    Vb = L.sb("Vb", [128, NT, 129], BF16)
                    Eb = [L.sb("Eb%d" % i, [128, 512], BF16) for i in range(3)]
                    om = [L.sb("om%d" % m, [128, 4, 128], F32) for m in range(2)]
                    df = L.sb("df", [128, 4, 128], F32)
                    junk = L.sb("junkB", [128, 128], BF16)
                    ssb = L.sb("ssb", [128, 4], F32)
                    rsb = L.sb("rsb", [128, 4], F32)
                    rcb = L.sb("rcb", [128, 4], F32)
                    ob = [L.sb("ob%d" % i, [128, 128], BF16) for i in range(4)]
                    sps = [L.ps("sps%d" % i, [128, 512], F32) for i in range(2)]
                    accb = [[L.ps("accb%d_%d" % (s_, i), [128, 512], F32) for i in range(2)] for s_ in range(2)]
                    prj = L.ps("prjB", [128, 512], F32)
                    ptr = L.ps("ptrB", [128, 8, 128], BF16)
                    memset("pool", Vb[:, :, 128:129], 1.0, ["Vb1"])
                    for m_ in range(2):
                        memset("pool", QT[m_][64:128, :], 0.0, [("QTa", m_)])
                        for v_ in range(2):
                            memset("pool", KT[m_][v_][64:128, :], 0.0, [("KTa", m_, v_)])
                    wiv = win_d.rearrange("(k p) n -> p k n", p=128)
                    ecnt = 0
                    acnt = 0
                    tcnt = 0
                    pendB = []
                    for h in range(4):
                        sl = h % 2
                        for tsel in range(3):
                            c0 = 1536 + tsel * 512 + h * 128
                            dma("pool", wg[sl][:, :, tsel, :], wiv[:, :, c0:c0 + 128], (), [("wgB", sl, tsel)],
                                "s_wgB%d_%d" % (sl, tsel))
                        import os
                        DBGQ = os.environ.get("DBGQ", "").split(",")
                        for m in range(2):
                            if "noaug" in DBGQ:
                                continue
                            dma("pool", QT[m][64:68, :], qaug_d[h], (), [("QTa", m)], "s_qa%d" % m)
                            for v in range(2):
                                dma("pool", KT[m][v][64:68, :], kaug_d[h, v], (), [("KTa", m, v)], "s_ka%d_%d" % (m, v))
                        for c in range(4):
                            if "noq" in DBGQ:
                                continue
                            for k in range(8):
                                mm(prj[:], wg[sl][:, k, 0, :], hT[:, k, c * 512:(c + 1) * 512], k == 0, k == 7,
                                   [("wgB", sl, 0)] + hkeys(k, c * 4, 4), ["prjB"])
                            act(QT[0][0:64, c * 512:(c + 1) * 512], prj[0:64, :], AF.Copy, ["prjB"], [("QT", 0, c)], scale=0.125)
                            act(QT[1][0:64, c * 512:(c + 1) * 512], prj[64:128, :], AF.Copy, ["prjB"], [("QT", 1, c)], scale=0.125)
                        for c in range(4):
                            if "nok" in DBGQ:
                                continue
                            for k in range(8):
                                mm(prj[:], wg[sl][:, k, 1, :], hT[:, k, c * 512:(c + 1) * 512], k == 0, k == 7,
                                   [("wgB", sl, 1)] + hkeys(k, c * 4, 4), ["prjB"])
                            cp("dve", KT[0][0][0:64, c * 512:(c + 1) * 512], prj[0:64, :], ["prjB"], [("KT", 0, 0, c)])
                            cp("dve", KT[1][0][0:64, c * 512:(c + 1) * 512], prj[64:128, :], ["prjB"], [("KT", 1, 0, c)])
                            for m_ in range(2):
                                cp("pool", KT[m_][1][0:64, c * 512:(c + 1) * 512], KT[m_][0][0:64, c * 512:(c + 1) * 512],
                                   [("KT", m_, 0, c)], [("KT", m_, 1, c)])
                        for c in range(4):
                            if "nov" in DBGQ:
                                continue
                            for tq in range(4):
                                t = c * 4 + tq
                                for k in range(8):
                                    mm(prj[:, tq * 128:(tq + 1) * 128], hT[:, k, t * 128:(t + 1) * 128], wg[sl][:, k, 2, :],
                                       k == 0, k == 7, [("wgB", sl, 2)] + hkeys(k, t, 1), ["prjB"])
                            cp("dve", Vb[:, c * 4:(c + 1) * 4, 0:128], prj[:].rearrange("p (t e) -> p t e", t=4),
                               ["prjB"], [("Vb", c)])
                        import os
                        DBGB = int(os.environ.get("DBGB", "9"))
                        for qc in range(4):
                            if DBGB < 2:
                                continue
                            for m in range(2):
                                aset = accb[acnt % 2]
                                akey = acnt % 2
                                acnt += 1

                                def qk(kt, ssl):
                                    o = sps[ssl]
                                    wk = [("sps", ssl)]

                                    def rng(v, c0, c1, aug=True):
                                        kk = 128 if aug else 64
                                        rd = [("KT", m, v, kt // 4), ("QT", m, qc)]
                                        if aug:
                                            rd += [("KTa", m, v), ("QTa", m)]
                                        return (o[:, c0:c1], KT[m][v][0:kk, kt * 128:(kt + 1) * 128],
                                                QT[m][0:kk, qc * 512 + c0: qc * 512 + c1], rd)
                                    if kt < 4 * qc:
                                        o_, l_, r_, rd = rng(0, 0, 512)
                                        mm(o_, l_, r_, True, True, rd, wk)
                                    elif kt >= 4 * qc + 4:
                                        o_, l_, r_, rd = rng(1, 0, 512)
                                        mm(o_, l_, r_, True, True, rd, wk)
                                    else:
                                        jd = kt - 4 * qc
                                        if jd > 0:
                                            o_, l_, r_, rd = rng(1, 0, jd * 128)
                                            mm(o_, l_, r_, True, True, rd, wk)
                                        o_, l_, r_, rd = rng(0, jd * 128, (jd + 1) * 128, aug=False)
                                        mm(o_, l_, r_, True, False, rd, wk)
                                        mm(o_, ident[:], bdg[:, h, :], False, True, ["ident", "bdg"], wk)
                                        if jd < 3:
                                            o_, l_, r_, rd = rng(0, (jd + 1) * 128, 512)
                                            mm(o_, l_, r_, True, True, rd, wk)

                                qk(0, ecnt % 2)
                                for kt in range(16):
                                    if kt == 2:
                                        while pendB:
                                            pendB.pop(0)()
                                    ssl = ecnt % 2
                                    esl = ecnt % 3
                                    ecnt += 1
                                    if kt + 1 < 16:
                                        qk(kt + 1, ecnt % 2)
                                    act(Eb[esl][:], sps[ssl][:], AF.Exp, [("sps", ssl)], [("Eb", esl)])
                                    for j in range(4):
                                        if DBGB < 3:
                                            continue
                                        mm(aset[j // 2][:, (j % 2) * 129:(j % 2) * 129 + 129], Eb[esl][:, j * 128:(j + 1) * 128],
                                           Vb[:, kt, :], kt == 0 and j % 2 == 0, kt == 15,
                                           [("Eb", esl), ("Vb", kt // 4), "Vb1"], [("accb", akey, j // 2)], skip=True)
                                for j in range(4):
                                    if DBGB < 4:
                                        continue
                                    a_ = aset[j // 2]
                                    c0 = (j % 2) * 129
                                    recip(rcb[:, j:j + 1], a_[:, c0 + 128:c0 + 129], [("accb", akey, j // 2)], [("rcb", j)])
                                    ts("dve", om[m][:, j, :], a_[:, c0:c0 + 128], rcb[:, j:j + 1], None, ALU.mult, None,
                                       [("accb", akey, j // 2), ("rcb", j)], [("om", m, j)])
                            if DBGB < 5:
                                continue
                            stt("dve", df[:], om[1][:], neglam[:, 0:1], om[0][:], ALU.mult, ALU.add,
                                [("om", 0, j) for j in range(4)] + [("om", 1, j) for j in range(4)] + ["neglam"], ["df"])
                            memset("dve", ssb[:], 0.0, ["ssb"])
                            for j in range(4):
                                act(junk[:], df[:, j, :], AF.Square, ["df", "ssb"], ["junkB", ("ssbd", j)], accum=ssb[:, j:j + 1])
                            ts("dve", rsb[:], ssb[:], 1.0 / 128, EPS, ALU.mult, ALU.add, [("ssbd", j) for j in range(4)], ["rsb"])
                            act(rsb[:], rsb[:], AF.Sqrt, ["rsb"], ["rsb"])
                            recip(rsb[:], rsb[:], ["rsb"], ["rsb"])
                            for j in range(4):
                                stt("dve", ob[j][:], df[:, j, :], rsb[:, j:j + 1], subw[:], ALU.mult, ALU.mult,
                                    ["df", "rsb", "subw"], [("ob", j)])

                            def epiB(h=h, qc=qc):
                                for j in range(4):
                                    tr(ptr[:, j, :], ob[j][:], ident[:], [("ob", j), "ident"], ["ptrB"])
                                cp("act", oT[:, 4 + h, qc * 512:(qc + 1) * 512], ptr[:, 0:4, :].rearrange("p a q -> p (a q)"),
                                   ["ptrB"], [("oT", 4 + h, qc * 4 + j) for j in range(4)])
                            pendB.append(epiB)
                    while pendB:
                        pendB.pop(0)()
                    P.barrier()
                    if stop == 3:
                        return nc

            with Scope() as R:
                x1 = R.sb("x1", [128, NT, D], F32)
                with Scope() as L:
                    wo = L.sb("wo", [128, 8, D], BF16)
                    wst = [L.sb("wst%d" % i, [128, D], F32) for i in range(2)]
                    xt = [L.sb("xtC%d" % i, [128, D], F32) for i in range(2)]
                    pc = [[L.ps("pc%d_%d" % (i, hf), [128, 512], F32) for hf in range(2)] for i in range(2)]
                    gbc, gkeys = build_bc(L, lambda k: modT[:, 16 + k, b:b + 1], [("modT", b)], "C")
                    for k in range(8):
                        sl = k % 2
                        dma("sp", wst[sl][:], wout_d[k * 128:(k + 1) * 128, :], (), [("wst", sl)], "s_wst%d" % sl)
                        tt("pool" if k % 3 == 2 else "dve", wo[:, k, :], wst[sl][:], gbc[:], ALU.mult,
                           [("wst", sl)] + gkeys, [("wo", k)])
                    for t in range(NT):
                        sl = t % 2
                        dma("sp", xt[sl][:], x_d[b, t * 128:(t + 1) * 128, :], (), [("xtC", sl)], "s_xtC%d" % sl)
                        for hf in range(2):
                            for k in range(8):
                                mm(pc[sl][hf][:], oT[:, k, t * 128:(t + 1) * 128], wo[:, k, hf * 512:(hf + 1) * 512],
                                   k == 0, k == 7, [("oT", k, t), ("wo", k)], [("pc", sl, hf)])
                            tt("dve", x1[:, t, hf * 512:(hf + 1) * 512], pc[sl][hf][:], xt[sl][:, hf * 512:(hf + 1) * 512],
                               ALU.add, [("pc", sl, hf), ("xtC", sl)], [("x1", t, hf)])
                    P.barrier()
                    if stop == 4:
                        return nc

                wguv = wgu_d.rearrange("(k p) n -> p k n", p=128)
                for half in range(2):
                    with Scope() as L:
                        actT = L.sb("actT", [128, NFC, 1024], BF16)
                        with Scope() as L2:
                            h2T = L2.sb("h2T", [128, 8, 1024], BF16)

                            def srcD(t):
                                return x1[:, t, :], [("x1", t, 0), ("x1", t, 1)]

                            with Scope() as L3:
                                abc = build_bc(L3, lambda k: aF[:, k, b:b + 1], [("aF", b)], "D")
                                sbc = build_bc(L3, lambda k: modT[:, 24 + k, b:b + 1], [("modT", b)], "Ds")
                                norm_phase(L3, srcD, h2T, abc, sbc, list(range(half * 8, half * 8 + 8)), 0, "D")
                                P.barrier()
                            h2keys = lambda k, t0, n: [("D", "T", k, t) for t in range(t0, t0 + n)]
                            with Scope() as L3:
                                wgu = [L3.sb("wgu%d" % i, [128, 8, 2, 128], BF16) for i in range(3)]
                                gs = [L3.sb("gs%d" % i, [128, 512], F32) for i in range(2)]
                                pg = [L3.ps("pg%d" % i, [128, 512], F32) for i in range(2)]
                                pu = [L3.ps("pu%d" % i, [128, 512], F32) for i in range(2)]
                                cnt = 0
                                for fc in range(NFC):
                                    sl = fc % 3
                                    dma("pool", wgu[sl][:, :, 0, :], wguv[:, :, fc * 128:(fc + 1) * 128], (), [("wgu", sl, 0)],
                                        "s_wgu%d_0" % sl)
                                    dma("pool", wgu[sl][:, :, 1, :], wguv[:, :, DFF + fc * 128:DFF + (fc + 1) * 128], (),
                                        [("wgu", sl, 1)], "s_wgu%d_1" % sl)
                                    for c in range(2):
                                        ps_ = cnt % 2
                                        cnt += 1
                                        for k in range(8):
                                            mm(pg[ps_][:], wgu[sl][:, k, 0, :], h2T[:, k, c * 512:(c + 1) * 512], k == 0, k == 7,
                                               [("wgu", sl, 0)] + h2keys(k, c * 4, 4), [("pg", ps_)])
                                        for k in range(8):
                                            mm(pu[ps_][:], wgu[sl][:, k, 1, :], h2T[:, k, c * 512:(c + 1) * 512], k == 0, k == 7,
                                               [("wgu", sl, 1)] + h2keys(k, c * 4, 4), [("pu", ps_)])
                                        act(gs[ps_][:], pg[ps_][:], AF.Silu, [("pg", ps_)], [("gs", ps_)])
                                        tt("dve", actT[:, fc, c * 512:(c + 1) * 512], pu[ps_][:], gs[ps_][:], ALU.mult,
                                           [("pu", ps_), ("gs", ps_)], [("actT", fc, c)])
                                P.barrier()
                        with Scope() as L2:
                            wd = L2.sb("wd", [128, NFC, 512], BF16)
                            wds = [L2.sb("wds%d" % i, [128, 512], F32) for i in range(2)]
                            pd = [L2.ps("pd%d" % i, [128, 512], F32) for i in range(4)]
                            ot = [L2.sb("ot%d" % i, [128, D], F32) for i in range(2)]
                            junk = L2.sb("junkF", [128, D], BF16)
                            ssf = L2.sb("ssf", [128, 8], F32)
                            rsf = L2.sb("rsf", [128, 8], F32)
                            cnt = 0
                            gbc, gkeys = build_bc(L2, lambda k: modT[:, 40 + k, b:b + 1], [("modT", b)], "F")
                            for dh in range(2):
                                for fc in range(NFC):
                                    sl = (dh * NFC + fc) % 2
                                    dma("sp", wds[sl][:], wdn_d[fc * 128:(fc + 1) * 128, dh * 512:(dh + 1) * 512], (),
                                        [("wds", sl)], "s_wds%d" % sl)
                                    tt("pool" if fc % 3 == 2 else "dve", wd[:, fc, :], wds[sl][:], gbc[:, dh * 512:(dh + 1) * 512], ALU.mult,
                                       [("wds", sl)] + gkeys, [("wd", fc)])
                                for ti in range(8):
                                    t = half * 8 + ti
                                    ps_ = cnt % 4
                                    cnt += 1
                                    for fc in range(NFC):
                                        mm(pd[ps_][:], actT[:, fc, ti * 128:(ti + 1) * 128], wd[:, fc, :], fc == 0, fc == NFC - 1,
                                           [("actT", fc, ti // 4), ("wd", fc)], [("pd", ps_)])
                                    tt("dve", x1[:, t, dh * 512:(dh + 1) * 512], pd[ps_][:], x1[:, t, dh * 512:(dh + 1) * 512],
                                       ALU.add, [("pd", ps_), ("x1", t, dh)], [("x1", t, dh)])
                            memset("dve", ssf[:], 0.0, ["ssf"])
                            for ti in range(8):
                                t = half * 8 + ti
                                sl = ti % 2
                                act(junk[:], x1[:, t, :], AF.Square, [("x1", t, 0), ("x1", t, 1), "ssf"],
                                    ["junkF", ("ssfd", ti)], accum=ssf[:, ti:ti + 1])
                                ts("dve", rsf[:, ti:ti + 1], ssf[:, ti:ti + 1], 1.0 / D, EPS, ALU.mult, ALU.add,
                                   [("ssfd", ti)], [("rsf", ti)])
                                act(rsf[:, ti:ti + 1], rsf[:, ti:ti + 1], AF.Sqrt, [("rsf", ti)], [("rsf", ti)])
                                recip(rsf[:, ti:ti + 1], rsf[:, ti:ti + 1], [("rsf", ti)], [("rsf", ti)])
                                stt("dve", ot[sl][:], x1[:, t, :], rsf[:, ti:ti + 1], gfbc[:], ALU.mult, ALU.mult,
                                    [("x1", t, 0), ("x1", t, 1), ("rsf", ti), "gfbc"], [("ot", sl)])
                                dma("sp", out_d[b, t * 128:(t + 1) * 128, :], ot[sl][:], [("ot", sl)], (),
                                    "s_out%d" % sl, final=True)
                            P.barrier(last=(b == NB - 1 and half == 1))
    return nc


_NC_CACHE = {}


def kernel(x, c, w_ada, b_ada, g_mix, w_in, rpb, lambda_q1, lambda_k1, lambda_q2, lambda_k2,
           subln_w, w_out, g_ffn, w_gate_up, w_down, g_final):
    f32 = np.float32
    x = np.asarray(x, f32)
    c = np.asarray(c, f32)
    if "nc" not in _NC_CACHE:
        _NC_CACHE["nc"] = build_nc()
    nc = _NC_CACHE["nc"]
    qaug, kaug, bdiag, mask = _host_consts()
    rpb0 = np.asarray(rpb, f32)[0]
    nabg = np.zeros((8, 128, NPAT, 128), f32)
    for p, pat in enumerate(_NA_BLKPAT):
        dr, dc, valid = _NA_PATS[pat]
        g_ = rpb0[:, dr, dc]
        nabg[:, :, p, :] = np.where(valid[None], g_, f32(0))
    nabg = nabg.reshape(8, 128, NPAT * 128)

    def colT(v, n):
        return np.ascontiguousarray(np.asarray(v, f32).reshape(n, 128).T)

    shared = {
        "w_ada": np.ascontiguousarray(np.asarray(w_ada, f32)[0]),
        "b_adaT": colT(np.asarray(b_ada, f32)[0], 48),
        "g_mixT": colT(np.asarray(g_mix, f32)[0], 8),
        "g_ffnT": colT(np.asarray(g_ffn, f32)[0], 8),
        "w_in": np.ascontiguousarray(np.asarray(w_in, f32)[0]),
        "w_out": np.ascontiguousarray(np.asarray(w_out, f32)[0]),
        "w_gate_up": np.ascontiguousarray(np.asarray(w_gate_up, f32)[0]),
        "w_down": np.ascontiguousarray(np.asarray(w_down, f32)[0]),
        "g_final": np.ascontiguousarray(np.asarray(g_final, f32)),
        "subln_w": np.ascontiguousarray(np.asarray(subln_w, f32)[0]),
        "lams": np.ascontiguousarray(np.stack([np.asarray(v, f32)[0] for v in
                                               (lambda_q1, lambda_k1, lambda_q2, lambda_k2)])),
        "nab_g": nabg,
        "na_mask": np.ascontiguousarray(mask.reshape(128, NPAT * 128)),
        "qaug": qaug, "kaug": kaug,
        "bdiag": np.ascontiguousarray(bdiag.reshape(128, 512)),
    }
    in_maps = []
    for i in range(8):
        cc = c[2 * i:2 * i + 2]
        cT = np.ascontiguousarray(cc.reshape(2, 8, 128).transpose(2, 1, 0).reshape(128, 16))
        m = dict(shared)
        m["x"] = np.ascontiguousarray(x[2 * i:2 * i + 2])
        m["cT"] = cT
        in_maps.append(m)
    res = run_bass_kernel_spmd(nc, in_maps, core_ids=list(range(8)))
    return np.concatenate([np.asarray(r["out"], f32) for r in res.results], axis=0)
```
